# Optimizing a Trainium2 kernel written in Bass

```python
import jax
import jax.numpy as jnp
from jax import lax
import numpy as np

D_MODEL = 1024
BATCH = 4
SEQ = 4096
DEPTH = 4

CTX_LEN = 256
GRID_W = 64
N_EVEN = (DEPTH + 1) // 2
N_ODD = DEPTH // 2
EPS = 1e-6

RET_HEADS = 4
RET_HEAD_DIM = D_MODEL // 8
RET_W = RET_HEADS * RET_HEAD_DIM
RET_CHUNK = 128
ROPE_BASE = 10000.0
ROPE_PAIRS = (RET_HEAD_DIM // 8, 3 * RET_HEAD_DIM // 16, 3 * RET_HEAD_DIM // 16)
CONV_CH = D_MODEL // 2
CONV_K = 31
EVEN_IN = 4 * RET_W + 2 * CONV_CH
EVEN_MIX = RET_W + CONV_CH
POOL_CH = D_MODEL // 2
POOL_WINDOWS = (2, 4, 8, 16)
POOL_GROUPS = len(POOL_WINDOWS)
POOL_GC = POOL_CH // POOL_GROUPS
SG_CH = D_MODEL // 2
SG_GROUPS = 4
SG_GC = SG_CH // SG_GROUPS
SG_CHUNK = 128
ODD_IN = POOL_CH + 2 * SG_CH
ODD_MIX = POOL_CH + SG_CH
D_FF = ((8 * D_MODEL // 3 + 255) // 256) * 256

kernel_name = 'hybrid_retention_conformer_pool_gmlp_prefix_dit'


def rms_norm(x, w):
    x32 = x.astype(jnp.float32)
    y = x32 * lax.rsqrt(jnp.mean(x32 * x32, axis=-1, keepdims=True) + EPS)
    return (y * w.astype(jnp.float32)).astype(x.dtype)


def layer_norm(x, w, b):
    x32 = x.astype(jnp.float32)
    mu = jnp.mean(x32, axis=-1, keepdims=True)
    xc = x32 - mu
    y = xc * lax.rsqrt(jnp.mean(xc * xc, axis=-1, keepdims=True) + EPS)
    return (y * w.astype(jnp.float32) + b.astype(jnp.float32)).astype(x.dtype)


def modulate(h, shift, scale):
    return h * (1.0 + scale) + shift


def swiglu(h, w_gate, w_up, w_down):
    return (jax.nn.silu(h @ w_gate) * (h @ w_up)) @ w_down


def rope_angles(p_seq, p_row, p_col):
    parts = []
    for p, n in zip((p_seq, p_row, p_col), ROPE_PAIRS):
        freq = ROPE_BASE ** (-jnp.arange(n, dtype=jnp.float32) / n)
        parts.append(p[:, None] * freq[None, :])
    return jnp.concatenate(parts, axis=-1)


def apply_rope(t, ang):
    t32 = t.astype(jnp.float32)
    half = t32.shape[-1] // 2
    t1, t2 = t32[..., :half], t32[..., half:]
    cos, sin = jnp.cos(ang), jnp.sin(ang)
    return jnp.concatenate([t1 * cos - t2 * sin, t1 * sin + t2 * cos], axis=-1)


def split_heads(t):
    b, l, _ = t.shape
    return t.reshape(b, l, RET_HEADS, RET_HEAD_DIM).transpose(0, 2, 1, 3)


def retention_chunkwise(q, k, v, log_g, s0, inclusive):
    b, h, l, dk = q.shape
    dv = v.shape[-1]
    n = l // RET_CHUNK
    qc = q.reshape(b, h, n, RET_CHUNK, dk)
    kc = k.reshape(b, h, n, RET_CHUNK, dk)
    vc = v.reshape(b, h, n, RET_CHUNK, dv)
    pos = jnp.arange(RET_CHUNK, dtype=jnp.float32)
    diff = pos[:, None] - pos[None, :]
    if inclusive:
        mask = diff >= 0
        expo = diff
        xi_exp = pos + 1.0
    else:
        mask = diff > 0
        expo = diff - 1.0
        xi_exp = pos
    lg = log_g[:, None, None]
    dmat = jnp.where(mask[None], jnp.exp(lg * jnp.where(mask, expo, 0.0)[None]), 0.0)
    xi = jnp.exp(log_g[:, None] * xi_exp[None, :])
    zeta = jnp.exp(log_g[:, None] * (RET_CHUNK - 1.0 - pos)[None, :])
    chunk_decay = jnp.exp(log_g * RET_CHUNK)[None, :, None, None]
    scores = jnp.einsum('bhncd,bhnmd->bhncm', qc, kc) * dmat[None, :, None]
    intra = jnp.einsum('bhncm,bhnme->bhnce', scores, vc)
    kv = jnp.einsum('bhnmd,hm,bhnme->bhnde', kc, zeta, vc)

    def step(s, kv_n):
        return s * chunk_decay + kv_n, s

    _, s_prev = lax.scan(step, s0, jnp.moveaxis(kv, 2, 0))
    inter = jnp.einsum('bhncd,hc,nbhde->bhnce', qc, xi, s_prev)
    return (intra + inter).reshape(b, h, l, dv)


def retention_final_states(k, v, log_g2):
    l = k.shape[2]
    pos = jnp.arange(l, dtype=jnp.float32)
    w_f = jnp.exp(log_g2[0][:, None] * (l - 1.0 - pos)[None, :])
    w_b = jnp.exp(log_g2[1][:, None] * pos[None, :])
    s_f = jnp.einsum('bhld,hl,bhle->bhde', k, w_f, v)
    s_b = jnp.einsum('bhld,hl,bhle->bhde', k, w_b, v)
    return s_f, s_b


def bidir_retention(q, k, v, log_g2, s_f, s_b):
    flip = lambda t: jnp.flip(t, axis=2)
    fwd = retention_chunkwise(q, k, v, log_g2[0], s_f, True)
    bwd = flip(retention_chunkwise(flip(q), flip(k), flip(v), log_g2[1], s_b, False))
    return fwd + bwd


def retention_kv(h, w_in, ang):
    k = apply_rope(split_heads(h @ w_in[:, RET_W:2 * RET_W]), ang)
    v = split_heads(h @ w_in[:, 2 * RET_W:3 * RET_W]).astype(jnp.float32)
    return k, v


def even_project(h, w_in, ang):
    p = h @ w_in
    q, k, v, g, a, gb = jnp.split(p, [RET_W, 2 * RET_W, 3 * RET_W, 4 * RET_W, 4 * RET_W + CONV_CH], axis=-1)
    q = apply_rope(split_heads(q), ang) * (RET_HEAD_DIM ** -0.5)
    k = apply_rope(split_heads(k), ang)
    v = split_heads(v).astype(jnp.float32)
    return q, k, v, g, a * jax.nn.sigmoid(gb)


def depthwise_conv(u, w):
    return lax.conv_general_dilated(
        u, w.astype(u.dtype)[:, None, :], window_strides=(1,),
        padding=[(CONV_K // 2, CONV_K // 2)],
        dimension_numbers=('NWC', 'WIO', 'NWC'), feature_group_count=u.shape[-1])


def even_mix(q, k, v, g, u, s_f, s_b, log_g2, conv_w, ln_w, ln_b, w_out):
    b, _, l, _ = q.shape
    y = bidir_retention(q, k, v, log_g2, s_f, s_b)
    y = y * lax.rsqrt(jnp.mean(y * y, axis=-1, keepdims=True) + EPS)
    y = y.transpose(0, 2, 1, 3).reshape(b, l, RET_W).astype(g.dtype)
    ret_out = jax.nn.silu(g) * y
    conv_out = jax.nn.silu(layer_norm(depthwise_conv(u, conv_w), ln_w, ln_b))
    return jnp.concatenate([ret_out, conv_out], axis=-1) @ w_out


def pool_minus_token(h32):
    b, l, _ = h32.shape
    cs = jnp.concatenate([jnp.zeros((b, 1, POOL_CH), jnp.float32), lax.cumsum(h32, axis=1)], axis=1)
    t = jnp.arange(l)
    outs = []
    for gi, w in enumerate(POOL_WINDOWS):
        left = w // 2
        right = w - 1 - left
        lo = jnp.clip(t - left, 0, l - 1)
        hi = jnp.clip(t + right, 0, l - 1)
        csg = cs[..., gi * POOL_GC:(gi + 1) * POOL_GC]
        mean = (csg[:, hi + 1] - csg[:, lo]) / (hi - lo + 1).astype(jnp.float32)[None, :, None]
        outs.append(mean - h32[..., gi * POOL_GC:(gi + 1) * POOL_GC])
    return jnp.stack(outs, axis=2)


def odd_stream(h, w_in, w_out, pool_w, pool_scale, sg_ln_w, sg_ln_b, sg_w, sg_b):
    b, l, _ = h.shape
    p = h @ w_in
    pc, pd = p[..., :POOL_CH], p[..., POOL_CH:]
    m = pool_minus_token(pc.astype(jnp.float32))
    pool_out = jnp.einsum('blgc,gcd->blgd', m, pool_w.astype(jnp.float32)).reshape(b, l, POOL_CH)
    pool_out = pool_out.astype(h.dtype) * pool_scale
    z = jax.nn.gelu(pd, approximate=False)
    u, v = jnp.split(z, 2, axis=-1)
    v = layer_norm(v, sg_ln_w, sg_ln_b).reshape(b, l // SG_CHUNK, SG_CHUNK, SG_GROUPS, SG_GC)
    s = jnp.einsum('bnpgc,gqp->bnqgc', v, sg_w) + sg_b.T[None, None, :, :, None]
    sg_out = u * s.reshape(b, l, SG_CH)
    return jnp.concatenate([pool_out, sg_out], axis=-1) @ w_out


def setup_inputs(seed: int = 0) -> dict:
    key = jax.random.key(seed)
    ks = jax.random.split(key, 25)
    f32 = jnp.float32
    nrm = lambda k, shape, s: jax.random.normal(k, shape, f32) * s
    gamma0 = 1.0 - 2.0 ** (-5.0 - np.arange(RET_HEADS))
    decay_base = jnp.asarray(np.log(gamma0 / (1.0 - gamma0)), f32)
    return {
        'x': nrm(ks[0], (BATCH, SEQ, D_MODEL), 1.0),
        'c': nrm(ks[1], (BATCH, D_MODEL), 1.0),
        'ctx': nrm(ks[2], (BATCH, CTX_LEN, D_MODEL), 1.0),
        'c_ctx': nrm(ks[3], (D_MODEL,), 1.0),
        'ada_w': nrm(ks[4], (DEPTH, D_MODEL, 6 * D_MODEL), 0.5 * D_MODEL ** -0.5),
        'ada_b': nrm(ks[5], (DEPTH, 6 * D_MODEL), 0.02),
        'norm_w': 1.0 + nrm(ks[6], (DEPTH, 2, D_MODEL), 0.02),
        'even_w_in': nrm(ks[7], (N_EVEN, D_MODEL, EVEN_IN), D_MODEL ** -0.5),
        'even_w_out': nrm(ks[8], (N_EVEN, EVEN_MIX, D_MODEL), EVEN_MIX ** -0.5),
        'ret_decay_logit': decay_base + nrm(ks[9], (N_EVEN, 2, RET_HEADS), 0.1),
        'conv_dw_w': nrm(ks[10], (N_EVEN, CONV_K, CONV_CH), CONV_K ** -0.5),
        'conv_ln_w': 1.0 + nrm(ks[11], (N_EVEN, CONV_CH), 0.02),
        'conv_ln_b': nrm(ks[12], (N_EVEN, CONV_CH), 0.02),
        'odd_w_in': nrm(ks[13], (N_ODD, D_MODEL, ODD_IN), D_MODEL ** -0.5),
        'odd_w_out': nrm(ks[14], (N_ODD, ODD_MIX, D_MODEL), ODD_MIX ** -0.5),
        'pool_w': nrm(ks[15], (N_ODD, POOL_GROUPS, POOL_GC, POOL_GC), POOL_GC ** -0.5),
        'pool_scale': 1.0 + nrm(ks[16], (N_ODD, POOL_CH), 0.02),
        'sg_ln_w': 1.0 + nrm(ks[17], (N_ODD, SG_CH), 0.02),
        'sg_ln_b': nrm(ks[18], (N_ODD, SG_CH), 0.02),
        'sg_w': nrm(ks[19], (N_ODD, SG_GROUPS, SG_CHUNK, SG_CHUNK), 0.5 * SG_CHUNK ** -0.5),
        'sg_b': 1.0 + nrm(ks[20], (N_ODD, SG_GROUPS, SG_CHUNK), 0.02),
        'ffn_w_gate': nrm(ks[21], (DEPTH, D_MODEL, D_FF), D_MODEL ** -0.5),
        'ffn_w_up': nrm(ks[22], (DEPTH, D_MODEL, D_FF), D_MODEL ** -0.5),
        'ffn_w_down': nrm(ks[23], (DEPTH, D_FF, D_MODEL), D_FF ** -0.5),
        'final_norm_w': 1.0 + nrm(ks[24], (D_MODEL,), 0.02),
    }


def reference(x, c, ctx, c_ctx, ada_w, ada_b, norm_w, even_w_in, even_w_out, ret_decay_logit,
              conv_dw_w, conv_ln_w, conv_ln_b, odd_w_in, odd_w_out, pool_w, pool_scale,
              sg_ln_w, sg_ln_b, sg_w, sg_b, ffn_w_gate, ffn_w_up, ffn_w_down, final_norm_w):
    b, l, d = x.shape
    lc = ctx.shape[1]
    rows = l // GRID_W
    grid_r = jnp.broadcast_to(jnp.arange(rows, dtype=jnp.float32)[:, None], (rows, GRID_W)).reshape(-1)
    grid_c = jnp.broadcast_to(jnp.arange(GRID_W, dtype=jnp.float32)[None, :], (rows, GRID_W)).reshape(-1)
    ang_x = rope_angles(jnp.full((l,), lc, jnp.float32), grid_r, grid_c)
    zeros_c = jnp.zeros((lc,), jnp.float32)
    ang_c = rope_angles(jnp.arange(lc, dtype=jnp.float32), zeros_c, zeros_c)
    silu_c = jax.nn.silu(c)
    silu_cc = jax.nn.silu(c_ctx)

    for i in range(DEPTH):
        j = i // 2
        even = i % 2 == 0
        ctx_after = any(m % 2 == 0 for m in range(i + 1, DEPTH))
        use_ctx = even or ctx_after
        mod_x = (silu_c @ ada_w[i] + ada_b[i]).reshape(b, 6, 1, d)
        hx = modulate(rms_norm(x, norm_w[i, 0]), mod_x[:, 0], mod_x[:, 1])
        if use_ctx:
            mod_c = (silu_cc @ ada_w[i] + ada_b[i]).reshape(6, 1, d)
            hc = modulate(rms_norm(ctx, norm_w[i, 0]), mod_c[0], mod_c[1])
        if even:
            log_g2 = jax.nn.log_sigmoid(ret_decay_logit[j].astype(jnp.float32))
            if ctx_after:
                parts_c = even_project(hc, even_w_in[j], ang_c)
                kc, vc = parts_c[1], parts_c[2]
            else:
                kc, vc = retention_kv(hc, even_w_in[j], ang_c)
            s_f, s_b = retention_final_states(kc, vc, log_g2)
            yx = even_mix(*even_project(hx, even_w_in[j], ang_x), s_f, s_b, log_g2,
                          conv_dw_w[j], conv_ln_w[j], conv_ln_b[j], even_w_out[j])
            if ctx_after:
                s_zero = jnp.zeros_like(s_f)
                yc = even_mix(*parts_c, s_zero, s_zero, log_g2,
                              conv_dw_w[j], conv_ln_w[j], conv_ln_b[j], even_w_out[j])
        else:
            yx = odd_stream(hx, odd_w_in[j], odd_w_out[j], pool_w[j], pool_scale[j],
                            sg_ln_w[j], sg_ln_b[j], sg_w[j], sg_b[j])
            if ctx_after:
                yc = odd_stream(hc, odd_w_in[j], odd_w_out[j], pool_w[j], pool_scale[j],
                                sg_ln_w[j], sg_ln_b[j], sg_w[j], sg_b[j])
        x = x + mod_x[:, 2] * yx
        x = x + mod_x[:, 5] * swiglu(modulate(rms_norm(x, norm_w[i, 1]), mod_x[:, 3], mod_x[:, 4]),
                                     ffn_w_gate[i], ffn_w_up[i], ffn_w_down[i])
        if ctx_after:
            ctx = ctx + mod_c[2] * yc
            ctx = ctx + mod_c[5] * swiglu(modulate(rms_norm(ctx, norm_w[i, 1]), mod_c[3], mod_c[4]),
                                          ffn_w_gate[i], ffn_w_up[i], ffn_w_down[i])
    return rms_norm(x, final_norm_w)
```

```python
import numpy as np
from contextlib import ExitStack
import concourse.bass as bass
import concourse.mybir as mybir
from concourse.bass_utils import run_bass_kernel_spmd

F32 = mybir.dt.float32
BF16 = mybir.dt.bfloat16
AF = mybir.ActivationFunctionType
ALU = mybir.AluOpType

ENGS = ("pe", "act", "dve", "pool", "sp")
NCORES = 8
DBG_DUMPS = False
D = 1024
KC = 8
TM = 2048
TCX = 256
TT = TM + TCX
SEQ = 4096
DFF = 2816
NFF = DFF // 128
EPS = 1e-6
BIG = 1.0e6
TILES = [(0, 512), (512, 512), (1024, 512), (1536, 512), (2048, 256)]
CTX_TILE = 4
HC = 15
HP = 8
XW = 1024 + 4 * 2 * HC
XWO = 4 * 2 * HP


class Op:
    __slots__ = ("eng", "fn", "deps", "sig", "done", "is_dma", "inc", "size", "idx")

    def __init__(self, eng, fn):
        self.eng = eng
        self.fn = fn
        self.deps = []
        self.sig = False
        self.done = None
        self.is_dma = None
        self.inc = 1
        self.size = 1 << 20
        self.idx = 0


class Prog:
    def __init__(self):
        self.ops = {e: [] for e in ENGS}
        self.last_w = {}
        self.readers = {}
        self.dma_keys = []

    def add(self, eng, fn, reads=(), writes=(), dma=None, inc=None, size=None):
        op = Op(eng, fn)
        op.idx = len(self.ops[eng])
        if size is not None:
            op.size = size
        if dma is not None:
            op.is_dma = dma
            op.inc = 16 if inc is None else inc
            if dma not in self.dma_keys:
                self.dma_keys.append(dma)
        xk = [k for k in reads if isinstance(k, tuple) and k[0] == "ps"]
        if xk:
            reads = [k for k in reads if k not in xk]
            writes = list(writes) + xk
        deps = []
        for k in reads:
            lw = self.last_w.get(k)
            if lw is not None:
                deps.append(lw)
        for k in writes:
            lw = self.last_w.get(k)
            if lw is not None:
                deps.append(lw)
            deps.extend(self.readers.get(k, ()))
        seen = set()
        for d in deps:
            if id(d) in seen or d is op:
                continue
            seen.add(id(d))
            if d.eng == eng and d.is_dma is None and dma is None:
                if eng == "pe" or d.size >= 512 or op.idx - d.idx > 6:
                    continue
            op.deps.append(d)
            d.sig = True
        for k in reads:
            lst = self.readers.setdefault(k, [])
            if dma is None:
                for i_, r_ in enumerate(lst):
                    if r_.is_dma is None and r_.eng == eng:
                        lst.pop(i_)
                        break
            lst.append(op)
        for k in writes:
            self.last_w[k] = op
            self.readers[k] = []
        self.ops[eng].append(op)
        return op

    def finalize(self):
        cnt = {}
        for e in ENGS:
            for op in self.ops[e]:
                if op.is_dma is not None:
                    key = ("dma", op.is_dma)
                    cnt[key] = cnt.get(key, 0) + op.inc
                    op.done = (key, cnt[key])
        for e in ENGS:
            c = 0
            for op in self.ops[e]:
                if op.is_dma is None and op.sig:
                    c += 1
                    op.done = (("eng", e), c)

    def emit(self, sems, e, h):
        waited = {}
        for op in self.ops[e]:
            need = {}
            for d in op.deps:
                k, v = d.done
                if waited.get(k, 0) >= v:
                    continue
                if need.get(k, 0) < v:
                    need[k] = v
            for k, v in need.items():
                h.wait_ge(sems[k], v)
                waited[k] = v
            ins = op.fn(h)
            if op.is_dma is not None:
                ins.then_inc(sems[op.done[0]], op.inc)
            elif op.sig:
                ins.then_inc(sems[op.done[0]], 1)


class Lay:
    def __init__(self):
        self.off = {}
        self.n = 0

    def add(self, name, w):
        self.off[name] = self.n
        self.n += w

    def o(self, name):
        return self.off[name]


def make_layouts():
    L = Lay()
    L.add("flags", 4)
    L.add("zidx_m", 8)
    L.add("zidx_c", 4)
    L.add("cfidx", 8)
    L.add("logit", 16)
    L.add("svin", 16)
    L.add("adab", 4 * 48)
    L.add("normw", 4 * 16)
    L.add("fnw", 8)
    L.add("convw", 2 * 4 * 31)
    L.add("convlnw", 8)
    L.add("convlnb", 8)
    L.add("poolsc", 8)
    L.add("icm", 64)
    L.add("icc", 64)
    L.add("eps", 1)
    L.add("one", 1)
    L.add("lns", 1)
    LT = Lay()
    LT.add("ropeC", TT)
    LT.add("ropeS", TT)
    LT.add("a1idx", 896)
    LT.add("a2idx", 896)
    LT.add("xifidx", 512)
    LT.add("xibidx", 512)
    LT.add("sgtab", 2 * 1536)
    LT.add("pwsg", 2 * 1024)
    return L, LT


LAY, LAYT = make_layouts()
NCB = 384


def fm(v):
    v = np.asarray(v, np.float32)
    return np.ascontiguousarray(v.reshape(-1, 128).T)


def host_consts(core, inp):
    b, half = core // 2, core % 2
    L, LT = LAY, LAYT
    cst = np.zeros((128, L.n), np.float32)
    tab = np.zeros((128, LT.n), np.float32)

    def put(arr, lay, name, val):
        val = np.asarray(val, np.float32)
        o = lay.o(name)
        arr[:, o:o + val.shape[1]] = val

    pairs = (16, 24, 24)

    def angles(p_seq, p_row, p_col):
        parts = []
        for p_, n in zip((p_seq, p_row, p_col), pairs):
            freq = (np.float32(10000.0) ** (-np.arange(n, dtype=np.float32) / np.float32(n))).astype(np.float32)
            parts.append(p_[:, None].astype(np.float32) * freq[None, :])
        return np.concatenate(parts, axis=-1)

    t = np.arange(half * TM, (half + 1) * TM)
    ang_x = angles(np.full((TM,), float(TCX), np.float32), (t // 64).astype(np.float32), (t % 64).astype(np.float32))
    zc = np.zeros((TCX,), np.float32)
    ang_c = angles(np.arange(TCX, dtype=np.float32), zc, zc)
    ang = np.concatenate([ang_x, ang_c], axis=0).astype(np.float32)
    cos = np.cos(ang).T.astype(np.float32)
    sin = np.sin(ang).T.astype(np.float32)
    put(tab, LT, "ropeC", np.concatenate([cos, cos], axis=0))
    put(tab, LT, "ropeS", np.concatenate([-sin, sin], axis=0))
    p = np.arange(128)[:, None]
    i = np.arange(896)[None, :]
    dlt = i - p - 384
    put(tab, LT, "a1idx", np.where(dlt >= 0, dlt, BIG))
    put(tab, LT, "a2idx", np.where(dlt < 0, -dlt - 1, BIG))
    c = np.arange(512, dtype=np.float32)[None, :] + np.zeros((128, 1), np.float32)
    put(tab, LT, "xifidx", c + 1.0)
    put(tab, LT, "xibidx", 511.0 - c)
    sgt = np.zeros((128, 2 * 1536), np.float32)
    pws = np.zeros((128, 2 * 1024), np.float32)
    pw = np.asarray(inp["pool_w"], np.float32)
    sw = np.asarray(inp["sg_w"], np.float32)
    for l in range(2):
        sgt[:, l * 1536:l * 1536 + 512] = np.asarray(inp["sg_ln_w"][l], np.float32)[None, :]
        sgt[:, l * 1536 + 512:l * 1536 + 1024] = np.asarray(inp["sg_ln_b"][l], np.float32)[None, :]
        sgt[:, l * 1536 + 1024:l * 1536 + 1536] = np.asarray(inp["sg_b"][l], np.float32).reshape(1, 512)
        pws[:, l * 1024:l * 1024 + 512] = pw[l].transpose(1, 0, 2).reshape(128, 512)
        pws[:, l * 1024 + 512:l * 1024 + 1024] = sw[l].transpose(2, 0, 1).reshape(128, 512)
    put(tab, LT, "sgtab", sgt)
    put(tab, LT, "pwsg", pws)

    put(cst, L, "flags", np.tile(np.array([1.0 - half, float(half), 0, 0], np.float32)[None, :], (128, 1)))
    j = np.arange(4)[None, :]
    m = 128 * j + p
    put(cst, L, "zidx_m", np.concatenate([511.0 - m, m], axis=1))
    j2 = np.arange(2)[None, :]
    m2 = 128 * j2 + p
    put(cst, L, "zidx_c", np.concatenate([255.0 - m2, m2], axis=1))
    ti = np.arange(4, dtype=np.float32)
    put(cst, L, "cfidx", np.tile(np.concatenate([512.0 * ti, 512.0 * (3 - ti)])[None, :], (128, 1)))
    put(cst, L, "logit", np.tile(np.asarray(inp["ret_decay_logit"], np.float32).reshape(1, 16), (128, 1)))
    svin = np.zeros((128, 8, 2), np.float32)
    svin[:, :, 0] = fm(inp["c"][b])
    svin[:, :, 1] = fm(inp["c_ctx"])
    put(cst, L, "svin", svin.reshape(128, 16))
    put(cst, L, "adab", np.concatenate([fm(inp["ada_b"][l]) for l in range(4)], axis=1))
    put(cst, L, "normw", np.concatenate([fm(inp["norm_w"][l, s]) for l in range(4) for s in range(2)], axis=1))
    put(cst, L, "fnw", fm(inp["final_norm_w"]))
    cw = np.asarray(inp["conv_dw_w"], np.float32)
    put(cst, L, "convw", np.concatenate([cw[l][:, ch * 128:(ch + 1) * 128].T for l in range(2) for ch in range(4)], axis=1))
    put(cst, L, "convlnw", np.concatenate([fm(inp["conv_ln_w"][l]) for l in range(2)], axis=1))
    put(cst, L, "convlnb", np.concatenate([fm(inp["conv_ln_b"][l]) for l in range(2)], axis=1))
    put(cst, L, "poolsc", np.concatenate([fm(inp["pool_scale"][l]) for l in range(2)], axis=1))

    def invcnt(lo_edge, hi_edge, n):
        out = np.zeros((4, 16), np.float32)
        for gi, w in enumerate((2, 4, 8, 16)):
            left = w // 2
            right = w - 1 - left
            for k in range(8):
                cl = min(left, k) if lo_edge else left
                out[gi, k] = 1.0 / (cl + right + 1)
                tpos = n - 8 + k
                cr = min(right, n - 1 - tpos) if hi_edge else right
                out[gi, 8 + k] = 1.0 / (left + cr + 1)
        return np.tile(out.reshape(1, 64), (128, 1))
    put(cst, L, "icm", invcnt(half == 0, half == 1, TM))
    put(cst, L, "icc", invcnt(True, True, TCX))
    cst[:, L.o("eps")] = EPS
    cst[:, L.o("one")] = 1.0
    cst[:, L.o("lns")] = np.log(128.0 ** -0.5)
    cb = np.zeros((128, NCB), np.float32)
    cb[:, 0:128] = 1.0
    cb[:, 128:256] = np.eye(128, dtype=np.float32)
    for q in range(128):
        cb[(q + 64) % 128, 256 + q] = 1.0
    return cst, tab, cb


def wlay(w):
    w = np.asarray(w, np.float32)
    k, n = w.shape
    return np.ascontiguousarray(w.reshape(k // 128, 128, n).transpose(1, 0, 2))


def build_program(n_layers=4, debug_x=False):
    nc = bass.Bass("TRN2", target_bir_lowering=False)
    pg = Prog()
    es = ExitStack()
    E = es.enter_context

    def din(name, shape):
        return nc.dram_tensor(name, list(shape), F32, kind="ExternalInput").ap()

    xin = din("xin", [128, KC, TT])
    cst_d = din("cst", [128, LAY.n])
    tab_d = din("tab", [128, LAYT.n])
    cstb_d = din("cstb", [128, NCB])
    ada_d = din("ada_w", [4, 128, KC, 3072])
    evin_d = din("even_w_in", [2, 128, KC, 3072])
    evout_d = din("even_w_out", [2, 128, KC, D])
    odin_d = din("odd_w_in", [2, 128, KC, 1536])
    odout_d = din("odd_w_out", [2, 128, KC, D])
    wg_d = din("ffn_w_gate", [4, 128, KC, DFF])
    wu_d = din("ffn_w_up", [4, 128, KC, DFF])
    wd_d = din("ffn_w_down", [4, 128, NFF, D])
    out_d = nc.dram_tensor("out", [128, KC, TM], F32, kind="ExternalOutput").ap()

    def scratch(name, shape, dt=BF16):
        return nc.dram_tensor(name, list(shape), dt).ap()

    q_d = scratch("q_d", [128, 4, TT])
    k_d = scratch("k_d", [128, 4, TT])
    v_d = scratch("v_d", [128, TT // 128, 512])
    g_d = scratch("g_d", [128, 4, TT])
    um_d = scratch("um_d", [128, 4, TM + 2 * HC])
    uc_d = scratch("uc_d", [128, 4, TCX + 2 * HC])
    r_d = scratch("r_d", [128, 4, TT])
    pm_d = scratch("pm_d", [128, 4, TM + 2 * HP])
    pc_d = scratch("pc_d", [128, 4, TCX + 2 * HP])
    xi_d = nc.dram_tensor("xi_d", [128, XW], F32)
    xo_d = nc.dram_tensor("xo_d", [256, XW], F32)
    xio_d = nc.dram_tensor("xio_d", [128, XWO], F32)
    xoo_d = nc.dram_tensor("xoo_d", [256, XWO], F32)
    xm_d = nc.dram_tensor("xm_d", [128, 192], F32)
    xmo_d = nc.dram_tensor("xmo_d", [256, 192], F32)

    def sb(name, shape, dt=F32):
        return E(nc.sbuf_tensor(name, list(shape), dt))

    x_sb = sb("x_sb", [128, KC, TT])
    h_sb = sb("h_sb", [128, KC, TT], BF16)
    cst = sb("cst_sb", [128, LAY.n])
    cstb = sb("cstb_sb", [128, NCB], BF16)
    WSLOT = 8192
    w_sb = [sb("w_sb%d" % i, [128, WSLOT], BF16) for i in range(2)]
    modall = sb("modall", [128, 4, 2, 48])
    modx = sb("modx", [128, 192])
    modg = sb("modg", [128, 2, 192])
    nv_sb = sb("nv_sb", [128, 6, KC, 2])
    sv_sb = sb("sv_sb", [128, KC, 2], BF16)
    rs_sb = sb("rstd", [128, 512])
    TW = 512 + 2 * HP
    t32 = [sb("t32_%d" % i, [128, TW]) for i in range(3)]
    tb16 = [sb("tb16_%d" % i, [128, 512], BF16) for i in range(3)]
    o16 = [sb("o16_%d" % i, [128, 4, 512], BF16) for i in range(2)]
    l16 = [sb("l16_%d" % i, [128, 4, 512 + 2 * HC], BF16) for i in range(2)]
    l16x = sb("l16x", [128, 4, 512], BF16)
    acc32 = sb("acc32", [128, 4, 512])
    accf = acc32[:].rearrange("p a b -> p (a b)")
    mix_sb = sb("mix", [128, 8, 512], BF16)
    ropeT = mix_sb[:, 0:4, :].rearrange("p a b -> p (a b)").bitcast(F32)
    hid_sb = mix_sb[:, 4:8, :]
    lg_sb = sb("lg", [128, 16])
    dtab = sb("dtab", [128, 896], BF16)
    xi_sb = sb("xi", [128, 2, 512], BF16)
    zm_sb = sb("zm", [128, 2, 4, 4])
    zc_sb = sb("zc", [128, 2, 4, 2])
    dec_sb = sb("dec", [128, 2, 4])
    cf_sb = sb("cf", [128, 2, 4, 4])
    kz_sb = [sb("kz%d" % i, [128, 2, 128], BF16) for i in range(2)]
    S_sb = sb("S", [128, 2, 512])
    Sflat = S_sb[:].rearrange("p a b -> p (a b)")
    pwsg = Sflat.bitcast(BF16)[:, 0:1024]
    xh_sb = sb("xh", [128, 8 * HC])
    sm = sb("small", [128, 8])

    psA = [E(nc.psum_tensor("psA%d" % i, [128, 512], F32)) for i in range(7)]
    psT = E(nc.psum_tensor("psT", [128, 1024], BF16))
    PST = 7

    def C(name, a=0, n=1):
        o = LAY.o(name) + a
        return cst[:, o:o + n]

    ones = cstb[:, 0:128]
    ident = cstb[:, 128:256]
    perm = cstb[:, 256:384]
    epsc = C("eps")
    fA = C("flags", 0)
    fB = C("flags", 1)

    def PS(i):
        return ("ps", i)

    def fsz(ap):
        n = 1
        for v in list(ap.shape)[1:]:
            n *= int(v)
        return n

    def mm(out, lhsT, rhs, start, stop, reads, writes):
        pg.add("pe", lambda h: h.matmul(out, lhsT=lhsT, rhs=rhs, start=start, stop=stop), reads=reads, writes=writes)

    def act(out, in_, func, reads, writes, bias=None, scale=None, accum=None):
        kw = {}
        if bias is not None:
            kw["bias"] = bias
        if scale is not None:
            kw["scale"] = scale
        if accum is not None:
            kw["accum_out"] = accum
        pg.add("act", lambda h: h.activation(out=out, in_=in_, func=func, **kw), reads=reads, writes=writes, size=fsz(out))

    def tt(eng, out, in0, in1, op, reads, writes):
        pg.add(eng, lambda h: h.tensor_tensor(out=out, in0=in0, in1=in1, op=op), reads=reads, writes=writes, size=fsz(out))

    def ts(eng, out, in0, s1, s2, op0, op1, reads, writes):
        if s2 is None:
            pg.add(eng, lambda h: h.tensor_scalar(out=out, in0=in0, scalar1=s1, scalar2=None, op0=op0), reads=reads, writes=writes, size=fsz(out))
        else:
            pg.add(eng, lambda h: h.tensor_scalar(out=out, in0=in0, scalar1=s1, scalar2=s2, op0=op0, op1=op1), reads=reads, writes=writes, size=fsz(out))

    def stt(eng, out, in0, scalar, in1, op0, op1, reads, writes):
        pg.add(eng, lambda h: h.scalar_tensor_tensor(out=out, in0=in0, scalar=scalar, in1=in1, op0=op0, op1=op1), reads=reads, writes=writes, size=fsz(out))

    def cp(eng, out, in_, reads, writes):
        pg.add(eng, lambda h: h.tensor_copy(out=out, in_=in_), reads=reads, writes=writes, size=fsz(out))

    def recip(ap, key):
        pg.add("dve", lambda h: h.reciprocal(out=ap, in_=ap), [key], [key], size=fsz(ap))

    dma_prev = {}

    def dma(eng, out, in_, reads, writes, key):
        op = pg.add(eng, lambda h: h.dma_start(out=out, in_=in_), reads=reads, writes=writes, dma=key)
        prev = dma_prev.get(key)
        if prev is not None and prev not in op.deps:
            op.deps.append(prev)
            prev.sig = True
        dma_prev[key] = op
        return op

    dbg_list = []

    def dbg_dump(name, ap, keys):
        if not (debug_x and DBG_DUMPS):
            return
        shp = list(ap.shape)
        dt = ap.dtype
        dd = nc.dram_tensor("dbg_" + name, shp, dt, kind="ExternalOutput").ap()
        idx = tuple(slice(None) for _ in shp)
        dma("sp", dd[idx], ap, keys, ["dbgk_" + name], "dbg%d" % (len(dbg_list) % 4))
        dbg_list.append("dbgk_" + name)

    stc = {"n": 0}

    def stkey():
        stc["n"] += 1
        return "st%d" % (stc["n"] % 4)

    ldc = {"n": 0}

    def ldkey():
        ldc["n"] += 1
        return "ld%d" % (ldc["n"] % 4)

    dma("sp", cst[:], cst_d[:, :], [], ["cst"], "cst")
    dma("pool", cstb[:], cstb_d[:, :], [], ["cstb"], "cstb")
    for c in range(KC):
        dma("sp", x_sb[:, c, :], xin[:, c, :], [], [("x", c, t) for t in range(5)], ("xin", c % 4))
    pg.add("dve", lambda h: h.memset(o16[0][:, :, 0:16], 0.0), [], [("o16", 0)])
    dma("sp", uc_d[:, :, 0:HC], o16[0][:, :, 0:HC], [("o16", 0)], ["uc_h0"], stkey())
    dma("sp", uc_d[:, :, HC + TCX:], o16[0][:, :, 0:HC], [("o16", 0)], ["uc_h1"], stkey())
    dma("sp", pc_d[:, :, 0:HP], o16[0][:, :, 0:HP], [("o16", 0)], ["pc_h0"], stkey())
    dma("sp", pc_d[:, :, HP + TCX:], o16[0][:, :, 0:HP], [("o16", 0)], ["pc_h1"], stkey())
    act(sv_sb[:].rearrange("p a b -> p (a b)"), C("svin", 0, 16), AF.Silu, ["cst"], ["sv"])
    act(lg_sb[:], C("logit", 0, 16), AF.Exp, ["cst"], ["lg"], scale=-1.0)
    act(lg_sb[:], lg_sb[:], AF.Ln, ["lg", "cst"], ["lg"], bias=C("one"))
    ts("dve", lg_sb[:], lg_sb[:], -1.0, None, ALU.mult, None, ["lg"], ["lg"])

    wstate = {"n": 0}

    def wload(parts, slot=None):
        s = wstate["n"] % 2 if slot is None else slot
        wstate["n"] = s + 1
        views = []
        off = 0
        for pi, ap in enumerate(parts):
            shp = list(ap.shape)
            n = shp[1] * shp[2]
            v = w_sb[s][:, off:off + n].rearrange("p (a b) -> p a b", a=shp[1])
            dma("pool", v, ap, [], [("W", s)], ("W", s, pi))
            views.append(v)
            off += n
        assert off <= WSLOT
        return s, views

    def mod_prologue():
        for li in range(4):
            for pi in range(3):
                s, (wv,) = wload([ada_d[li, :, :, pi * 1024:(pi + 1) * 1024]])
                for cc in range(8):
                    bank = cc % 2
                    for kc in range(KC):
                        mm(psA[bank][:, 0:2], wv[:, kc, cc * 128:(cc + 1) * 128], sv_sb[:, kc, :],
                           kc == 0, kc == KC - 1, [("W", s), "sv"], [PS(bank)])
                    o = li * 48 + (pi * 8 + cc) * 2
                    cp("dve", modx[:, o:o + 2], psA[bank][:, 0:2], [PS(bank)], ["modx"])
        dma("pool", xm_d.ap(), modx[:, :], ["modx"], ["xm_d"], "xst0")
        allgather(xm_d, xmo_d, ["xm_d"], "xmo_d")
        dma("pool", modg[:], xmo_d.ap().rearrange("(r p) n -> p r n", p=128), ["xmo_d"], ["modg"], "xld0")
        G = modg[:].rearrange("p r (l c v) -> p r l c v", l=4, c=24)
        for li in range(4):
            ab = C("adab", li * 48, 48).rearrange("p (r c) -> p r c", r=2)
            for s_ in range(2):
                mx = modall[:, li, s_, :].rearrange("p (r c) -> p r c", r=2)
                tt("dve", mx, G[:, :, li, :, s_], ab, ALU.add, ["modg", "cst"], ["mod"])

    def mod_layer(li):
        for s_ in range(2):
            M = modall[:, li, s_, :]
            stt("dve", nv_sb[:, 0, :, s_], M[:, 8:16], 1.0, C("normw", li * 16, 8), ALU.add, ALU.mult, ["mod", "cst"], ["nv"])
            cp("dve", nv_sb[:, 1, :, s_], M[:, 0:8], ["mod"], ["nv"])
            cp("dve", nv_sb[:, 2, :, s_], M[:, 16:24], ["mod"], ["nv"])
            stt("dve", nv_sb[:, 3, :, s_], M[:, 32:40], 1.0, C("normw", li * 16 + 8, 8), ALU.add, ALU.mult, ["mod", "cst"], ["nv"])
            cp("dve", nv_sb[:, 4, :, s_], M[:, 24:32], ["mod"], ["nv"])
            cp("dve", nv_sb[:, 5, :, s_], M[:, 40:48], ["mod"], ["nv"])

    def tile_stream(ti):
        return 1 if ti == CTX_TILE else 0

    def rstd_from_ps(bank, n, scale):
        act(rs_sb[:, 0:n], psA[bank][:, 0:n], AF.Ln, [PS(bank), "cst"], ["rs"], bias=epsc, scale=scale)
        act(rs_sb[:, 0:n], rs_sb[:, 0:n], AF.Exp, ["rs"], ["rs"], scale=-0.5)

    def sumsq_x(ti):
        t0, n = TILES[ti]
        for c in range(KC):
            q = tb16[c % 2]
            act(q[:, 0:n], x_sb[:, c, t0:t0 + n], AF.Square, [("x", c, ti)], [("tb16", c % 2)])
            mm(psA[6][:, 0:n], ones, q[:, 0:n], c == 0, c == KC - 1, [("tb16", c % 2), "cstb"], [PS(6)])
        rstd_from_ps(6, n, 1.0 / D)

    def norm_tile(ti, which):
        t0, n = TILES[ti]
        s_ = tile_stream(ti)
        sumsq_x(ti)
        for c in range(KC):
            tmp = t32[c % 2]
            tt("dve", tmp[:, 0:n], x_sb[:, c, t0:t0 + n], rs_sb[:, 0:n], ALU.mult, [("x", c, ti), "rs"], [("t32", c % 2)])
            a = nv_sb[:, 3 * which, c, s_:s_ + 1]
            b = nv_sb[:, 3 * which + 1, c, s_:s_ + 1]
            act(h_sb[:, c, t0:t0 + n], tmp[:, 0:n], AF.Identity, [("t32", c % 2), "nv"], [("h", ti)], bias=b, scale=a)

    def final_tile(ti):
        t0, n = TILES[ti]
        sumsq_x(ti)
        for c in range(KC):
            tmp = t32[c % 2]
            tt("dve", tmp[:, 0:n], x_sb[:, c, t0:t0 + n], rs_sb[:, 0:n], ALU.mult, [("x", c, ti), "rs"], [("t32", c % 2)])
            act(acc32[:, c % 4, 0:n], tmp[:, 0:n], AF.Identity, [("t32", c % 2), "cst"], [("acc", c % 4)], scale=C("fnw", c))
            dma("sp", out_d[:, c, t0:t0 + n], acc32[:, c % 4, 0:n], [("acc", c % 4)], [("out", ti, c)], "out%d" % (c % 4))

    def proj_fm(bank, wv, s, col0, ti):
        t0, n = TILES[ti]
        for kc in range(KC):
            mm(psA[bank][:, 0:n], wv[:, kc, col0:col0 + 128], h_sb[:, kc, t0:t0 + n], kc == 0, kc == KC - 1,
               [("W", s), ("h", ti)], [PS(bank)])

    def load_rope(ti):
        t0, n = TILES[ti]
        dma("sp", ropeT[:, 0:n], tab_d[:, LAYT.o("ropeC") + t0:LAYT.o("ropeC") + t0 + n], [], [("mixs", 0), ("mixs", 1)], ldkey())
        dma("sp", ropeT[:, 512:512 + n], tab_d[:, LAYT.o("ropeS") + t0:LAYT.o("ropeS") + t0 + n], [], [("mixs", 2), ("mixs", 3)], ldkey())

    rope_ctr = {"n": 0}

    def rope(bank, ti, out_ap, okey):
        t0, n = TILES[ti]
        i = rope_ctr["n"] % 2
        rope_ctr["n"] += 1
        tb = tb16[i]
        act(tb[:, 0:n], psA[bank][:, 0:n], AF.Copy, [PS(bank)], [("tb16", i)])
        tt("dve", t32[2][:, 0:n], psA[bank][:, 0:n], ropeT[:, 0:n], ALU.mult, [PS(bank), ("mixs", 0), ("mixs", 1)], [("t32", 2)])
        mm(psA[5][:, 0:n], perm, tb[:, 0:n], True, True, [("tb16", i), "cstb"], [PS(5)])
        tt("dve", t32[i][:, 0:n], psA[5][:, 0:n], ropeT[:, 512:512 + n], ALU.mult, [PS(5), ("mixs", 2), ("mixs", 3)], [("t32", i)])
        tt("dve", out_ap, t32[i][:, 0:n], t32[2][:, 0:n], ALU.add, [("t32", i), ("t32", 2)], [okey])

    RG = [[0, 1], [2, 3], [4, 5], [6, 7]]

    def allgather(src, dst, skeys, dkey, groups=None):
        groups = RG if groups is None else groups
        pg.add("pool", lambda h: h.collective_compute("AllGather", ALU.bypass, replica_groups=groups,
                                                      ins=[src.ap().opt()], outs=[dst.ap().opt()]),
               list(skeys), [dkey], dma="cc", inc=1)

    def even_small_tables(j):
        lo = 8 * j
        for hd in range(4):
            for d_ in range(2):
                lx = lg_sb[:, lo + 4 * d_ + hd:lo + 4 * d_ + hd + 1]
                act(zm_sb[:, d_, hd, :], C("zidx_m", d_ * 4, 4), AF.Exp, ["cst", "lg"], ["zm"], scale=lx)
                act(zc_sb[:, d_, hd, :], C("zidx_c", d_ * 2, 2), AF.Exp, ["cst", "lg"], ["zm"], scale=lx)
                act(dec_sb[:, d_, hd:hd + 1], lx, AF.Exp, ["lg"], ["dec"], scale=512.0)
                act(cf_sb[:, d_, :, hd], C("cfidx", d_ * 4, 4), AF.Exp, ["cst", "lg"], ["cf"], scale=lx)
        ts("dve", cf_sb[:, 0, :, :], cf_sb[:, 0, :, :], fB, None, ALU.mult, None, ["cf", "cst"], ["cf"])
        ts("dve", cf_sb[:, 1, :, :], cf_sb[:, 1, :, :], fA, None, ALU.mult, None, ["cf", "cst"], ["cf"])

    def head_tables(j, hd):
        lo = 8 * j
        lf = lg_sb[:, lo + hd:lo + hd + 1]
        lb = lg_sb[:, lo + 4 + hd:lo + 4 + hd + 1]
        acck = [("acc", i) for i in range(4)]
        dma("sp", accf[:, 0:896], tab_d[:, LAYT.o("a1idx"):LAYT.o("a1idx") + 896], [], acck, ldkey())
        dma("sp", accf[:, 1024:1920], tab_d[:, LAYT.o("a2idx"):LAYT.o("a2idx") + 896], [], acck, ldkey())
        act(accf[:, 0:896], accf[:, 0:896], AF.Exp, acck + ["lg", "cst"], acck, scale=lf, bias=C("lns"))
        act(accf[:, 1024:1920], accf[:, 1024:1920], AF.Exp, acck + ["lg", "cst"], acck, scale=lb, bias=C("lns"))
        tt("dve", dtab[:, :], accf[:, 0:896], accf[:, 1024:1920], ALU.add, acck, ["dtab"])
        dma("sp", accf[:, 0:512], tab_d[:, LAYT.o("xifidx"):LAYT.o("xifidx") + 512], [], acck, ldkey())
        dma("sp", accf[:, 512:1024], tab_d[:, LAYT.o("xibidx"):LAYT.o("xibidx") + 512], [], acck, ldkey())
        act(xi_sb[:, 0, :], accf[:, 0:512], AF.Exp, acck + ["lg", "cst"], ["xi"], scale=lf, bias=C("lns"))
        act(xi_sb[:, 1, :], accf[:, 512:1024], AF.Exp, acck + ["lg", "cst"], ["xi"], scale=lb, bias=C("lns"))

    def even_mixer(li, full_ctx):
        j = li // 2
        even_small_tables(j)
        order = [CTX_TILE, 0, 1, 2, 3]
        tiles_full = order if full_ctx else [0, 1, 2, 3]
        for ti in order:
            if ti not in tiles_full:
                norm_tile(ti, 0)
        norm_tile(tiles_full[0], 0)
        s2, (wa, wgb) = wload([evin_d[j, :, :, 2048:2560], evin_d[j, :, :, 2560:3072]])
        for i_, ti in enumerate(tiles_full):
            if i_ + 1 < len(tiles_full):
                norm_tile(tiles_full[i_ + 1], 0)
            t0, n = TILES[ti]
            ub = o16[ti % 2]
            for ch in range(4):
                b0 = (2 * ch) % 4
                b1 = b0 + 1
                proj_fm(b0, wa, s2, ch * 128, ti)
                proj_fm(b1, wgb, s2, ch * 128, ti)
                act(t32[ch % 2][:, 0:n], psA[b1][:, 0:n], AF.Sigmoid, [PS(b1)], [("t32", ch % 2)])
                tt("dve", ub[:, ch, 0:n], psA[b0][:, 0:n], t32[ch % 2][:, 0:n], ALU.mult, [PS(b0), ("t32", ch % 2)], [("o16", ti % 2)])
            if ti == CTX_TILE:
                dma("sp", uc_d[:, :, HC:HC + n], ub[:, :, 0:n], [("o16", ti % 2)], [("u_d", ti)], stkey())
            else:
                dma("sp", um_d[:, :, HC + t0:HC + t0 + n], ub[:, :, 0:n], [("o16", ti % 2)], [("u_d", ti)], stkey())
            if ti == 0:
                cp("dve", xh_sb[:, 0:4 * HC].rearrange("p (a b) -> p a b", a=4), ub[:, :, 0:HC], [("o16", ti % 2)], ["xh"])
            if ti == 3:
                cp("dve", xh_sb[:, 4 * HC:8 * HC].rearrange("p (a b) -> p a b", a=4), ub[:, :, n - HC:n], [("o16", ti % 2)], ["xh"])
        s3, (wk, wvv) = wload([evin_d[j, :, :, 512:1024], evin_d[j, :, :, 1024:1536]], slot=1 - s2)
        so = 1 - s3
        sp32 = w_sb[so][:, :].bitcast(F32).rearrange("p (a b c) -> p a b c", a=2, b=4)
        SPK = ("W", so)
        for ti in order:
            t0, n = TILES[ti]
            nb = n // 128
            load_rope(ti)
            ob = o16[0]
            for jb in range(nb):
                bank = jb % 2
                for kc in range(KC):
                    mm(psA[bank][:, :], h_sb[:, kc, t0 + jb * 128:t0 + (jb + 1) * 128], wvv[:, kc, :], kc == 0, kc == KC - 1,
                       [("W", s3), ("h", ti)], [PS(bank)])
                act(ob[:, jb, :], psA[bank][:, :], AF.Copy, [PS(bank)], [("o16", 0)])
            dma("sp", v_d[:, t0 // 128:t0 // 128 + nb, :], ob[:, 0:nb, :], [("o16", 0)], [("v_d", ti)], stkey())
            kb = o16[1]
            zt = zc_sb if ti == CTX_TILE else zm_sb
            for hd in range(4):
                bank = 2 + hd % 2
                proj_fm(bank, wk, s3, hd * 128, ti)
                rope(bank, ti, kb[:, hd, 0:n], ("o16", 1))
                for jb in range(nb):
                    pg.add("pe", lambda h, hd=hd, jb=jb: h.transpose(psT[:, jb * 128:(jb + 1) * 128], kb[:, hd, jb * 128:(jb + 1) * 128], ident),
                           [("o16", 1), "cstb"], [PS(PST)])
                for d_ in range(2):
                    kvbank = 4 if d_ == 0 else 6
                    for jb in range(nb):
                        kz = kz_sb[jb % 2]
                        ts("dve", kz[:, d_, :], psT[:, jb * 128:(jb + 1) * 128], zt[:, d_, hd, jb:jb + 1], None, ALU.mult, None,
                           [PS(PST), "zm"], [("kz", jb % 2, d_)])
                        mm(psA[kvbank][:, hd * 128:(hd + 1) * 128], kz[:, d_, :], ob[:, jb, hd * 128:(hd + 1) * 128], jb == 0, jb == nb - 1,
                           [("kz", jb % 2, d_), ("o16", 0)], [PS(kvbank)])
            dma("sp", k_d[:, :, t0:t0 + n], kb[:, :, 0:n], [("o16", 1)], [("k_d", ti)], stkey())
            if ti == CTX_TILE:
                ts("dve", S_sb[:, 0, :], psA[4][:, :], fA, None, ALU.mult, None, [PS(4), "cst"], ["S"])
                ts("dve", S_sb[:, 1, :], psA[6][:, :], fB, None, ALU.mult, None, [PS(6), "cst"], ["S"])
            else:
                cp("dve", sp32[:, 0, ti, :], S_sb[:, 0, :], ["S"], [SPK])
                for hd in range(4):
                    hs = slice(hd * 128, (hd + 1) * 128)
                    stt("dve", S_sb[:, 0, hs], S_sb[:, 0, hs], dec_sb[:, 0, hd:hd + 1], psA[4][:, hs], ALU.mult, ALU.add,
                        ["S", "dec", PS(4)], ["S"])
                cp("dve", acc32[:, ti, :], psA[6][:, :], [PS(6)], [("acc", ti)])
        for ti in (3, 2, 1, 0):
            cp("dve", sp32[:, 1, ti, :], S_sb[:, 1, :], ["S"], [SPK])
            for hd in range(4):
                hs = slice(hd * 128, (hd + 1) * 128)
                stt("dve", S_sb[:, 1, hs], S_sb[:, 1, hs], dec_sb[:, 1, hd:hd + 1], acc32[:, ti, hs], ALU.mult, ALU.add,
                    ["S", "dec", ("acc", ti)], ["S"])
        sq_, (wq, wgt) = wload([evin_d[j, :, :, 0:512], evin_d[j, :, :, 1536:2048]], slot=s3)
        cwo = j * 124
        HK = [("h", t) for t in range(5)]
        dgf = h_sb[:].rearrange("p a b -> p (a b)")
        dma("pool", xi_d.ap()[:, 0:1024], Sflat, ["S"], ["xi_d"], "xst0")
        dma("pool", xi_d.ap()[:, 1024:XW], xh_sb[:, :], ["xh"], ["xi_d2"], "xst1")
        allgather(xi_d, xo_d, ["xi_d", "xi_d2"], "xo_d")
        acck = [("acc", i) for i in range(4)]
        dma("pool", accf[:, 0:512], xo_d.ap()[0:128, 0:512], ["xo_d"], acck, "xld0")
        dma("pool", accf[:, 512:1024], xo_d.ap()[128:256, 512:1024], ["xo_d"], acck, "xld1")
        dma("pool", accf[:, 1024:1024 + 4 * HC], xo_d.ap()[0:128, 1024 + 4 * HC:XW], ["xo_d"], acck, "xld0")
        dma("pool", accf[:, 1024 + 4 * HC:XW], xo_d.ap()[128:256, 1024:1024 + 4 * HC], ["xo_d"], acck, "xld1")
        for i_, ti in enumerate(tiles_full):
            t0, n = TILES[ti]
            load_rope(ti)
            for hd in range(4):
                bank = hd % 2
                proj_fm(bank, wq, sq_, hd * 128, ti)
                rope(bank, ti, o16[0][:, hd, 0:n], ("o16", 0))
                bank = 2 + hd % 2
                proj_fm(bank, wgt, sq_, hd * 128, ti)
                act(o16[1][:, hd, 0:n], psA[bank][:, 0:n], AF.Silu, [PS(bank)], [("o16", 1)])
            dma("sp", q_d[:, :, t0:t0 + n], o16[0][:, :, 0:n], [("o16", 0)], [("q_d", ti)], stkey())
            dma("sp", g_d[:, :, t0:t0 + n], o16[1][:, :, 0:n], [("o16", 1)], [("g_d", ti)], stkey())
        for d_ in range(2):
            for ti in range(4):
                for hd in range(4):
                    hs = slice(hd * 128, (hd + 1) * 128)
                    stt("dve", o16[d_][:, ti, hs], accf[:, d_ * 512 + hd * 128:d_ * 512 + (hd + 1) * 128], cf_sb[:, d_, ti, hd:hd + 1],
                        sp32[:, d_, ti, hs], ALU.mult, ALU.add, acck + ["cf", SPK], [("o16", d_)])
        hl = tb16[2][:, 0:8 * HC].rearrange("p (a b) -> p a b", a=4)
        ts("dve", hl[:, :, 0:HC], accf[:, 1024:1024 + 4 * HC].rearrange("p (a b) -> p a b", a=4), fB, None, ALU.mult, None,
           acck + ["cst"], [("tb16", 2)])
        ts("dve", hl[:, :, HC:2 * HC], accf[:, 1024 + 4 * HC:XW].rearrange("p (a b) -> p a b", a=4), fA, None, ALU.mult, None,
           acck + ["cst"], [("tb16", 2)])
        dma("sp", um_d[:, :, 0:HC], hl[:, :, 0:HC], [("tb16", 2)], [("u_d", 0)], stkey())
        dma("sp", um_d[:, :, HC + TM:], hl[:, :, HC:2 * HC], [("tb16", 2)], [("u_d", 3)], stkey())
        s4, (wo,) = wload([evout_d[j, :, :, :]], slot=so)
        l16b = l16x
        LQ = [l16[0][:, 0, 0:512], l16b[:, 0, 0:512]]
        LK = [l16[0][:, 1, 0:512], l16b[:, 1, 0:512]]
        LG = [l16[0][:, 2, 0:512], l16b[:, 2, 0:512]]
        RB = [l16[0][:, 3, 0:512], l16b[:, 3, 0:512]]
        LV = [l16[1][:, :, 0:128], l16[1][:, :, 128:256]]
        its = [(hd, ti) for hd in range(4) for ti in tiles_full]

        def ret_loads(it):
            hd, ti = its[it]
            sset = it % 2
            t0, n = TILES[ti]
            nb = n // 128
            hs = slice(hd * 128, (hd + 1) * 128)
            dma("sp", LQ[sset][:, 0:n], q_d[:, hd, t0:t0 + n], [("q_d", ti)], [("lq", sset)], ldkey())
            dma("sp", LK[sset][:, 0:n], k_d[:, hd, t0:t0 + n], [("k_d", ti)], [("lk", sset)], ldkey())
            dma("sp", LV[sset][:, 0:nb, :], v_d[:, t0 // 128:t0 // 128 + nb, hs], [("v_d", ti)], [("lv", sset)], ldkey())

        def ret_load_g(it):
            hd, ti = its[it]
            sset = it % 2
            t0, n = TILES[ti]
            dma("sp", LG[sset][:, 0:n], g_d[:, hd, t0:t0 + n], [("g_d", ti)], [("lg_", sset)], ldkey())

        L0K = [("lq", 0), ("lk", 0), ("lg_", 0), ("rb", 0)]
        L1K = [("lv", 0), ("lv", 1)]
        def stage_a(it):
            hd, ti = its[it]
            sset = it % 2
            hs = slice(hd * 128, (hd + 1) * 128)
            if ti == tiles_full[0]:
                head_tables(j, hd)
            t0, n = TILES[ti]
            nb = n // 128
            inter = ti != CTX_TILE
            lq, lk, lv = LQ[sset], LK[sset], LV[sset]
            if inter:
                tt("dve", tb16[0][:, 0:n], lq[:, 0:n], xi_sb[:, 0, 0:n], ALU.mult, [("lq", sset), "xi"], [("tb16", 0)])
                tt("dve", tb16[1][:, 0:n], lq[:, 0:n], xi_sb[:, 1, 0:n], ALU.mult, [("lq", sset), "xi"], [("tb16", 1)])
            for jb in range(nb):
                mm(psA[jb][:, 0:n], lk[:, jb * 128:(jb + 1) * 128], lq[:, 0:n], True, True, [("lk", sset), ("lq", sset)], [PS(jb)])
                st0 = 384 - 128 * jb
                tt("dve", mix_sb[:, jb, 0:n], psA[jb][:, 0:n], dtab[:, st0:st0 + n], ALU.mult, [PS(jb), "dtab"], [("mixs", jb)])
            yb = 4 + it % 2
            nmm = nb + (2 if inter else 0)
            for jb in range(nb):
                mm(psA[yb][:, 0:n], lv[:, jb, :], mix_sb[:, jb, 0:n], jb == 0, jb == nmm - 1, [("lv", sset), ("mixs", jb)], [PS(yb)])
            if inter:
                mm(psA[yb][:, 0:n], o16[0][:, ti, hs], tb16[0][:, 0:n], False, False, [("o16", 0), ("tb16", 0)], [PS(yb)])
                mm(psA[yb][:, 0:n], o16[1][:, ti, hs], tb16[1][:, 0:n], False, True, [("o16", 1), ("tb16", 1)], [PS(yb)])

        def stage_b(it):
            hd, ti = its[it]
            sset = it % 2
            t0, n = TILES[ti]
            yb = 4 + it % 2
            lgt, rb = LG[sset], RB[sset]
            act(tb16[2][:, 0:n], psA[yb][:, 0:n], AF.Square, [PS(yb)], [("tb16", 2)])
            mm(psA[6][:, 0:n], ones, tb16[2][:, 0:n], True, True, [("tb16", 2), "cstb"], [PS(6)])
            rstd_from_ps(6, n, 1.0 / 128)
            tt("dve", t32[0][:, 0:n], psA[yb][:, 0:n], rs_sb[:, 0:n], ALU.mult, [PS(yb), "rs"], [("t32", 0)])
            tt("dve", rb[:, 0:n], t32[0][:, 0:n], lgt[:, 0:n], ALU.mult, [("t32", 0), ("lg_", sset)], [("rb", sset)])
            dma("sp", r_d[:, hd, t0:t0 + n], rb[:, 0:n], [("rb", sset)], [("r_d", ti, hd)], stkey())
            per = (124 + len(its) - 1) // len(its)
            for idx in range(it * per, min(124, (it + 1) * per)):
                ts("dve", dgf[:, idx * 128:(idx + 1) * 128], ident, C("convw", cwo + idx), None, ALU.mult, None, ["cstb", "cst"], HK)

        ret_loads(0)
        ret_load_g(0)
        if len(its) > 1:
            ret_loads(1)
            ret_load_g(1)
        stage_a(0)
        for it in range(len(its)):
            if it + 2 < len(its):
                ret_loads(it + 2)
            if it + 1 < len(its):
                stage_a(it + 1)
            stage_b(it)
            if it + 2 < len(its):
                ret_load_g(it + 2)
        for ti in tiles_full:
            t0, n = TILES[ti]
            lu, lr = l16[0], l16[1]
            lk0 = L0K
            if ti == CTX_TILE:
                dma("sp", lu[:, :, 0:n + 2 * HC], uc_d[:, :, :], [("u_d", ti), "uc_h0", "uc_h1"], lk0, ldkey())
            else:
                rk = [("u_d", ti)] + ([("u_d", ti - 1)] if ti > 0 else []) + ([("u_d", ti + 1)] if ti < 3 else [])
                dma("sp", lu[:, :, 0:n + 2 * HC], um_d[:, :, t0:t0 + n + 2 * HC], rk, lk0, ldkey())
            dma("sp", lr[:, :, 0:n], r_d[:, :, t0:t0 + n], [("r_d", ti, hd) for hd in range(4)], L1K, ldkey())
            for ch in range(4):
                for k in range(31):
                    idx = ch * 31 + k
                    mm(psA[ch][:, 0:n], dgf[:, idx * 128:(idx + 1) * 128], lu[:, ch, k:k + n], k == 0, k == 30, HK + lk0, [PS(ch)])
            for ch in range(4):
                act(tb16[ch % 2][:, 0:n], psA[ch][:, 0:n], AF.Copy, [PS(ch)], [("tb16", ch % 2)])
                mm(psA[4][:, 0:n], ones, tb16[ch % 2][:, 0:n], ch == 0, ch == 3, [("tb16", ch % 2), "cstb"], [PS(4)])
            for ch in range(4):
                act(tb16[ch % 2][:, 0:n], psA[ch][:, 0:n], AF.Square, [PS(ch)], [("tb16", ch % 2)])
                mm(psA[5][:, 0:n], ones, tb16[ch % 2][:, 0:n], ch == 0, ch == 3, [("tb16", ch % 2), "cstb"], [PS(5)])
            act(t32[0][:, 0:n], psA[4][:, 0:n], AF.Copy, [PS(4)], [("t32", 0)], scale=1.0 / 512)
            tt("dve", t32[1][:, 0:n], t32[0][:, 0:n], t32[0][:, 0:n], ALU.mult, [("t32", 0)], [("t32", 1)])
            stt("dve", t32[1][:, 0:n], psA[5][:, 0:n], 1.0 / 512, t32[1][:, 0:n], ALU.mult, ALU.subtract, [PS(5), ("t32", 1)], [("t32", 1)])
            act(rs_sb[:, 0:n], t32[1][:, 0:n], AF.Ln, [("t32", 1), "cst"], ["rs"], bias=epsc)
            act(rs_sb[:, 0:n], rs_sb[:, 0:n], AF.Exp, ["rs"], ["rs"], scale=-0.5)
            for ch in range(4):
                tt("dve", acc32[:, ch, 0:n], psA[ch][:, 0:n], t32[0][:, 0:n], ALU.subtract, [PS(ch), ("t32", 0)], [("acc", ch)])
                tt("dve", acc32[:, ch, 0:n], acc32[:, ch, 0:n], rs_sb[:, 0:n], ALU.mult, [("acc", ch), "rs"], [("acc", ch)])
                act(mix_sb[:, 4 + ch, 0:n], acc32[:, ch, 0:n], AF.Silu, [("acc", ch), "cst"], [("mixs", 4 + ch)],
                    bias=C("convlnb", j * 4 + ch), scale=C("convlnw", j * 4 + ch))
            wout_tile(ti, wo, s4, [lr[:, kc, :] for kc in range(4)] + [mix_sb[:, 4 + kc, :] for kc in range(4)],
                      [("lv", 0)] * 4 + [("mixs", 4 + kc) for kc in range(4)])

    def wout_tile(ti, wo, s, rhs_list, rkeys):
        t0, n = TILES[ti]
        s_ = tile_stream(ti)
        for oc in range(KC):
            bank = 4 + oc % 3
            for kc in range(KC):
                mm(psA[bank][:, 0:n], wo[:, kc, oc * 128:(oc + 1) * 128], rhs_list[kc][:, 0:n], kc == 0, kc == KC - 1,
                   [("W", s), rkeys[kc]], [PS(bank)])
            stt("dve", x_sb[:, oc, t0:t0 + n], psA[bank][:, 0:n], nv_sb[:, 2, oc, s_:s_ + 1], x_sb[:, oc, t0:t0 + n], ALU.mult, ALU.add,
                [PS(bank), "nv"], [("x", oc, ti)])

    def pool_window(ti, lp, lkeys):
        t0, n = TILES[ti]
        W_ = n + 2 * HP
        icn = "icc" if ti == CTX_TILE else "icm"
        for g, w in enumerate((2, 4, 8, 16)):
            left = w // 2
            right = w - 1 - left
            cur, oth = 0, 1
            tt("dve", t32[cur][:, 1:W_], lp[:, g, 1:W_], lp[:, g, 0:W_ - 1], ALU.add, lkeys, [("t32", cur)])
            k = 2
            lo = 1
            while k < w:
                tt("dve", t32[oth][:, lo + k:W_], t32[cur][:, lo + k:W_], t32[cur][:, lo:W_ - k], ALU.add, [("t32", cur)], [("t32", oth)])
                lo += k
                k *= 2
                cur, oth = oth, cur
            e0 = HP + right
            ts("dve", acc32[:, g, 0:n], t32[cur][:, e0:e0 + n], 1.0 / w, None, ALU.mult, None, [("t32", cur)], [("acc", g)])
            if ti == 0 or ti == CTX_TILE:
                tt("dve", acc32[:, g, 0:8], t32[cur][:, e0:e0 + 8], C(icn, g * 16, 8), ALU.mult, [("t32", cur), "cst"], [("acc", g)])
            if ti == 3 or ti == CTX_TILE:
                tt("dve", acc32[:, g, n - 8:n], t32[cur][:, e0 + n - 8:e0 + n], C(icn, g * 16 + 8, 8), ALU.mult, [("t32", cur), "cst"], [("acc", g)])
            tt("dve", mix_sb[:, g, 0:n], acc32[:, g, 0:n], lp[:, g, HP:HP + n], ALU.subtract, [("acc", g)] + lkeys, [("mixs", g)])

    def odd_mixer(li, tiles):
        j = li // 2
        norm_tile(tiles[0], 0)
        dma("pool", pwsg, tab_d[:, LAYT.o("pwsg") + j * 1024:LAYT.o("pwsg") + (j + 1) * 1024], [], ["S"], "tbl")
        acck = [("acc", i) for i in range(4)]
        dma("sp", accf[:, 0:1536], tab_d[:, LAYT.o("sgtab") + j * 1536:LAYT.o("sgtab") + (j + 1) * 1536], [], acck, ldkey())
        lnw = accf[:, 0:512]
        lnb = accf[:, 512:1024]
        s, (wpc,) = wload([odin_d[j, :, :, 0:512]])
        for i_, ti in enumerate(tiles):
            if i_ + 1 < len(tiles):
                norm_tile(tiles[i_ + 1], 0)
            t0, n = TILES[ti]
            pb = o16[ti % 2]
            for g in range(4):
                bank = g % 4
                proj_fm(bank, wpc, s, g * 128, ti)
                act(pb[:, g, 0:n], psA[bank][:, 0:n], AF.Copy, [PS(bank)], [("o16", ti % 2)])
            if ti == CTX_TILE:
                dma("sp", pc_d[:, :, HP:HP + n], pb[:, :, 0:n], [("o16", ti % 2)], [("p_d", ti)], stkey())
            else:
                dma("sp", pm_d[:, :, HP + t0:HP + t0 + n], pb[:, :, 0:n], [("o16", ti % 2)], [("p_d", ti)], stkey())
            if ti == 0:
                cp("dve", xh_sb[:, 0:4 * HP].rearrange("p (a b) -> p a b", a=4), pb[:, :, 0:HP], [("o16", ti % 2)], ["xh"])
            if ti == 3:
                cp("dve", xh_sb[:, 4 * HP:8 * HP].rearrange("p (a b) -> p a b", a=4), pb[:, :, n - HP:n], [("o16", ti % 2)], ["xh"])
        s, (wu_, wv_) = wload([odin_d[j, :, :, 512:1024], odin_d[j, :, :, 1024:1536]])
        dma("pool", xio_d.ap(), xh_sb[:, 0:XWO], ["xh"], ["xio_d"], "xst0")
        allgather(xio_d, xoo_d, ["xio_d"], "xoo_d")
        dma("pool", rs_sb[:, 0:4 * HP], xoo_d.ap()[0:128, 4 * HP:8 * HP], ["xoo_d"], ["rs"], "xld0")
        dma("pool", rs_sb[:, 4 * HP:8 * HP], xoo_d.ap()[128:256, 0:4 * HP], ["xoo_d"], ["rs"], "xld1")
        hl = tb16[2][:, 0:8 * HP].rearrange("p (a b) -> p a b", a=4)
        ts("dve", hl[:, :, 0:HP], rs_sb[:, 0:4 * HP].rearrange("p (a b) -> p a b", a=4), fB, None, ALU.mult, None, ["rs", "cst"], [("tb16", 2)])
        ts("dve", hl[:, :, HP:2 * HP], rs_sb[:, 4 * HP:8 * HP].rearrange("p (a b) -> p a b", a=4), fA, None, ALU.mult, None, ["rs", "cst"], [("tb16", 2)])
        dma("sp", pm_d[:, :, 0:HP], hl[:, :, 0:HP], [("tb16", 2)], [("p_d", 0)], stkey())
        dma("sp", pm_d[:, :, HP + TM:], hl[:, :, HP:2 * HP], [("tb16", 2)], [("p_d", 3)], stkey())
        for ti in tiles:
            t0, n = TILES[ti]
            nb = n // 128
            ub = o16[0]
            sb_ = o16[1]
            for g in range(4):
                bank = g % 2
                proj_fm(bank, wu_, s, g * 128, ti)
                act(ub[:, g, 0:n], psA[bank][:, 0:n], AF.Gelu, [PS(bank)], [("o16", 0)])
            def S1(jb, ti=ti, t0=t0):
                bank = 2 + jb % 2
                for kc in range(KC):
                    mm(psA[bank][:, :], h_sb[:, kc, t0 + jb * 128:t0 + (jb + 1) * 128], wv_[:, kc, :], kc == 0, kc == KC - 1,
                       [("W", s), ("h", ti)], [PS(bank)])
                z = t32[jb % 2]
                zk = ("t32", jb % 2)
                pb_ = jb % 2
                sq = l16x[:, pb_, :]
                sqk = ("sqx", pb_)
                smo = 4 * pb_
                smk = ("sm", pb_)
                act(z[:, 0:512], psA[bank][:, :], AF.Gelu, [PS(bank)], [zk])
                act(sq, z[:, 0:512], AF.Square, [zk], [sqk])
            def S2(jb, ti=ti, t0=t0):
                z = t32[jb % 2]
                zk = ("t32", jb % 2)
                pb_ = jb % 2
                sq = l16x[:, pb_, :]
                sqk = ("sqx", pb_)
                smo = 4 * pb_
                smk = ("sm", pb_)
                pg.add("dve", lambda h, z=z, smo=smo: h.reduce_sum(out=sm[:, smo:smo + 1], in_=z[:, 0:512], axis=mybir.AxisListType.X), [zk], [smk], size=1)
                pg.add("dve", lambda h, sq=sq, smo=smo: h.reduce_sum(out=sm[:, smo + 1:smo + 2], in_=sq, axis=mybir.AxisListType.X), [sqk], [smk], size=1)
                ts("dve", sm[:, smo + 2:smo + 3], sm[:, smo:smo + 1], 1.0 / 512, None, ALU.mult, None, [smk], [smk])
                tt("dve", sm[:, smo + 3:smo + 4], sm[:, smo + 2:smo + 3], sm[:, smo + 2:smo + 3], ALU.mult, [smk], [smk])
                stt("dve", sm[:, smo + 3:smo + 4], sm[:, smo + 1:smo + 2], 1.0 / 512, sm[:, smo + 3:smo + 4], ALU.mult, ALU.subtract, [smk], [smk])
                act(sm[:, smo + 3:smo + 4], sm[:, smo + 3:smo + 4], AF.Sqrt, [smk, "cst"], [smk], bias=epsc)
                recip(sm[:, smo + 3:smo + 4], smk)
                ts("dve", z[:, 0:512], z[:, 0:512], sm[:, smo + 2:smo + 3], sm[:, smo + 3:smo + 4], ALU.subtract, ALU.mult, [smk, zk], [zk])
                tt("dve", z[:, 0:512], z[:, 0:512], lnw, ALU.mult, [zk] + acck, [zk])
                vb = tb16[jb % 2]
                tt("dve", vb[:, :], z[:, 0:512], lnb, ALU.add, [zk] + acck, [("tb16", jb % 2)])
                for g in range(4):
                    bank2 = 4 + g % 2
                    tg = t32[2][:, g * 128:(g + 1) * 128]
                    mm(psA[bank2][:, 0:128], vb[:, g * 128:(g + 1) * 128], pwsg[:, 512 + g * 128:512 + (g + 1) * 128], True, True,
                       [("tb16", jb % 2), "S"], [PS(bank2)])
                    tt("dve", tg, psA[bank2][:, 0:128], accf[:, 1024 + g * 128:1024 + (g + 1) * 128], ALU.add,
                       [PS(bank2)] + acck, [("t32", 2, g)])
                    tt("dve", sb_[:, g, jb * 128:(jb + 1) * 128], tg, ub[:, g, jb * 128:(jb + 1) * 128], ALU.mult,
                       [("t32", 2, g), ("o16", 0)], [("o16", 1)])
            S1(0)
            for jb in range(nb):
                if jb + 1 < nb:
                    S1(jb + 1)
                S2(jb)
            dma("sp", r_d[:, :, t0:t0 + n], sb_[:, :, 0:n], [("o16", 1)], [("r_d", ti)], stkey())
        s, (wo,) = wload([odout_d[j, :, :, :]])
        for ti in tiles:
            t0, n = TILES[ti]
            lp, lr = l16[0], l16[1]
            lk0 = [("lq", 0), ("lk", 0), ("lg_", 0), ("rb", 0)]
            if ti == CTX_TILE:
                dma("sp", lp[:, :, 0:n + 2 * HP], pc_d[:, :, :], [("p_d", ti), "pc_h0", "pc_h1"], lk0, ldkey())
            else:
                rk = [("p_d", ti)] + ([("p_d", ti - 1)] if ti > 0 else []) + ([("p_d", ti + 1)] if ti < 3 else [])
                dma("sp", lp[:, :, 0:n + 2 * HP], pm_d[:, :, t0:t0 + n + 2 * HP], rk, lk0, ldkey())
            dma("sp", lr[:, :, 0:n], r_d[:, :, t0:t0 + n], [("r_d", ti)], [("lv", 0), ("lv", 1)], ldkey())
            pool_window(ti, lp, lk0)
            for g in range(4):
                bank = g % 2
                mm(psA[bank][:, 0:n], pwsg[:, g * 128:(g + 1) * 128], mix_sb[:, g, 0:n], True, True, [("mixs", g), "S"], [PS(bank)])
                ts("dve", mix_sb[:, 4 + g, 0:n], psA[bank][:, 0:n], C("poolsc", j * 4 + g), None, ALU.mult, None,
                   [PS(bank), "cst"], [("mixs", 4 + g)])
            if ti == 0:
                dbg_dump("mix", mix_sb[:, :, :], [("mixs", i) for i in range(8)])
                dbg_dump("lp", lp[:, :, :], lk0)
                dbg_dump("lr", lr[:, :, :], [("l16", 1)])
            wout_tile(ti, wo, s, [mix_sb[:, 4 + kc, :] for kc in range(4)] + [lr[:, kc, :] for kc in range(4)],
                      [("mixs", 4 + kc) for kc in range(4)] + [("lv", 0)] * 4)

    def ffn(li, tiles):
        norm_tile(tiles[0], 1)
        for f0 in range(0, NFF, 2):
            s, (wgv, wuv, wdv) = wload([wg_d[li, :, :, f0 * 128:(f0 + 2) * 128], wu_d[li, :, :, f0 * 128:(f0 + 2) * 128],
                                        wd_d[li, :, f0:f0 + 2, :]])
            for i_, ti in enumerate(tiles):
                if f0 == 0 and i_ + 1 < len(tiles):
                    norm_tile(tiles[i_ + 1], 1)
                t0, n = TILES[ti]
                s_ = tile_stream(ti)
                hsel = (f0 // 2 + ti) % 2
                for f in range(2):
                    bg, bu = 2 * f, 2 * f + 1
                    col = f * 128
                    for kc in range(KC):
                        mm(psA[bg][:, 0:n], wgv[:, kc, col:col + 128], h_sb[:, kc, t0:t0 + n], kc == 0, kc == KC - 1, [("W", s), ("h", ti)], [PS(bg)])
                    for kc in range(KC):
                        mm(psA[bu][:, 0:n], wuv[:, kc, col:col + 128], h_sb[:, kc, t0:t0 + n], kc == 0, kc == KC - 1, [("W", s), ("h", ti)], [PS(bu)])
                    act(t32[f][:, 0:n], psA[bg][:, 0:n], AF.Silu, [PS(bg)], [("t32", f)])
                    tt("dve", hid_sb[:, 2 * hsel + f, 0:n], psA[bu][:, 0:n], t32[f][:, 0:n], ALU.mult, [PS(bu), ("t32", f)], [("mixs", 4 + 2 * hsel + f)])
                for oc in range(KC):
                    bank = 4 + oc % 3
                    for fc in range(2):
                        mm(psA[bank][:, 0:n], wdv[:, fc, oc * 128:(oc + 1) * 128], hid_sb[:, 2 * hsel + fc, 0:n], fc == 0, fc == 1,
                           [("W", s), ("mixs", 4 + 2 * hsel + fc)], [PS(bank)])
                    stt("dve", x_sb[:, oc, t0:t0 + n], psA[bank][:, 0:n], nv_sb[:, 5, oc, s_:s_ + 1], x_sb[:, oc, t0:t0 + n], ALU.mult, ALU.add,
                        [PS(bank), "nv"], [("x", oc, ti)])

    mod_prologue()
    for li in range(n_layers):
        mod_layer(li)
        ctx_after = any(m % 2 == 0 for m in range(li + 1, 4))
        tl = [0, 1, 2, 3] + ([CTX_TILE] if ctx_after else [])
        if li % 2 == 0:
            even_mixer(li, ctx_after)
        else:
            odd_mixer(li, tl)
        ffn(li, tl)

    okeys = []
    for ti in range(4):
        t0, n = TILES[ti]
        if debug_x:
            for c in range(KC):
                dma("sp", out_d[:, c, t0:t0 + n], x_sb[:, c, t0:t0 + n], [("x", c, ti)], [("out", ti, c)], "out%d" % (c % 4))
        else:
            final_tile(ti)
        okeys += [("out", ti, c) for c in range(KC)]
    pg.add("sp", lambda h: h.nop(), okeys + dbg_list, [])

    pg.finalize()
    sems = {}
    for e in ENGS:
        sems[("eng", e)] = E(nc.semaphore("s_" + e))
    for i, k in enumerate(pg.dma_keys):
        sems[("dma", k)] = E(nc.semaphore("d%d" % i))
    block = E(nc.Block())

    @block.tensor
    def _(h):
        pg.emit(sems, "pe", h)

    @block.scalar
    def _(h):
        pg.emit(sems, "act", h)

    @block.vector
    def _(h):
        pg.emit(sems, "dve", h)

    @block.gpsimd
    def _(h):
        pg.emit(sems, "pool", h)

    @block.sync
    def _(h):
        pg.emit(sems, "sp", h)

    es.close()
    return nc


_NC_CACHE = {}


def prep_inputs(inp):
    x = np.asarray(inp["x"], np.float32)
    ctx = np.asarray(inp["ctx"], np.float32)
    shared = {
        "even_w_in": np.stack([wlay(inp["even_w_in"][l]) for l in range(2)]),
        "even_w_out": np.stack([wlay(inp["even_w_out"][l]) for l in range(2)]),
        "odd_w_in": np.stack([wlay(inp["odd_w_in"][l]) for l in range(2)]),
        "odd_w_out": np.stack([wlay(inp["odd_w_out"][l]) for l in range(2)]),
        "ffn_w_gate": np.stack([wlay(inp["ffn_w_gate"][l]) for l in range(4)]),
        "ffn_w_up": np.stack([wlay(inp["ffn_w_up"][l]) for l in range(4)]),
        "ffn_w_down": np.stack([wlay(inp["ffn_w_down"][l]) for l in range(4)]),
    }
    in_maps = []
    for core in range(NCORES):
        b, half = core // 2, core % 2
        xm = x[b, half * TM:(half + 1) * TM, :]
        xa = np.concatenate([xm, ctx[b]], axis=0)
        xt = np.ascontiguousarray(xa.T.reshape(KC, 128, TT).transpose(1, 0, 2))
        cst, tab, cb = host_consts(core, inp)
        m = {"xin": xt, "cst": cst, "tab": tab, "cstb": cb,
             "ada_w": np.stack([wlay(np.asarray(inp["ada_w"][l], np.float32)[:, half * 3072:(half + 1) * 3072]) for l in range(4)])}
        m.update(shared)
        in_maps.append(m)
    return in_maps


def assemble(results):
    out = np.zeros((4, SEQ, D), np.float32)
    for core in range(NCORES):
        b, half = core // 2, core % 2
        o = np.asarray(results[core]["out"], np.float32)
        out[b, half * TM:(half + 1) * TM, :] = o.transpose(1, 0, 2).reshape(D, TM).T
    return out


def kernel(**inputs):
    key = "full"
    if key not in _NC_CACHE:
        _NC_CACHE[key] = build_program()
    nc = _NC_CACHE[key]
    in_maps = prep_inputs(inputs)
    res = run_bass_kernel_spmd(nc, in_maps, core_ids=list(range(NCORES)))
    return assemble(res.results)
```

```python
import numpy as np
from contextlib import ExitStack
import concourse.bass as bass
import concourse.mybir as mybir
from concourse.bass_utils import run_bass_kernel_spmd

F32 = mybir.dt.float32
BF16 = mybir.dt.bfloat16
AF = mybir.ActivationFunctionType
ALU = mybir.AluOpType

ENGS = ("pe", "act", "dve", "pool", "sp")
NCORES = 8
DBG_DUMPS = False
D = 1024
KC = 8
TM = 2048
TCX = 256
TT = TM + TCX
SEQ = 4096
DFF = 2816
NFF = DFF // 128
EPS = 1e-6
BIG = 1.0e6
TILES = [(0, 512), (512, 512), (1024, 512), (1536, 512), (2048, 256)]
CTX_TILE = 4
HC = 15
HP = 8
XW = 1024 + 4 * 2 * HC
XWO = 4 * 2 * HP


class Op:
    __slots__ = ("eng", "fn", "deps", "sig", "done", "is_dma", "inc", "size", "idx")

    def __init__(self, eng, fn):
        self.eng = eng
        self.fn = fn
        self.deps = []
        self.sig = False
        self.done = None
        self.is_dma = None
        self.inc = 1
        self.size = 1 << 20
        self.idx = 0


class Prog:
    def __init__(self):
        self.ops = {e: [] for e in ENGS}
        self.last_w = {}
        self.readers = {}
        self.dma_keys = []

    def add(self, eng, fn, reads=(), writes=(), dma=None, inc=None, size=None):
        op = Op(eng, fn)
        op.idx = len(self.ops[eng])
        if size is not None:
            op.size = size
        if dma is not None:
            op.is_dma = dma
            op.inc = 16 if inc is None else inc
            if dma not in self.dma_keys:
                self.dma_keys.append(dma)
        xk = [k for k in reads if isinstance(k, tuple) and k[0] == "ps"]
        if xk:
            reads = [k for k in reads if k not in xk]
            writes = list(writes) + xk
        deps = []
        for k in reads:
            lw = self.last_w.get(k)
            if lw is not None:
                deps.append(lw)
        for k in writes:
            lw = self.last_w.get(k)
            if lw is not None:
                deps.append(lw)
            deps.extend(self.readers.get(k, ()))
        seen = set()
        for d in deps:
            if id(d) in seen or d is op:
                continue
            seen.add(id(d))
            if d.eng == eng and d.is_dma is None and dma is None:
                if eng == "pe" or d.size >= 512 or op.idx - d.idx > 6:
                    continue
            op.deps.append(d)
            d.sig = True
        for k in reads:
            lst = self.readers.setdefault(k, [])
            if dma is None:
                for i_, r_ in enumerate(lst):
                    if r_.is_dma is None and r_.eng == eng:
                        lst.pop(i_)
                        break
            lst.append(op)
        for k in writes:
            self.last_w[k] = op
            self.readers[k] = []
        self.ops[eng].append(op)
        return op

    def finalize(self):
        cnt = {}
        for e in ENGS:
            for op in self.ops[e]:
                if op.is_dma is not None:
                    key = ("dma", op.is_dma)
                    cnt[key] = cnt.get(key, 0) + op.inc
                    op.done = (key, cnt[key])
        for e in ENGS:
            c = 0
            for op in self.ops[e]:
                if op.is_dma is None and op.sig:
                    c += 1
                    op.done = (("eng", e), c)

    def emit(self, sems, e, h):
        waited = {}
        for op in self.ops[e]:
            need = {}
            for d in op.deps:
                k, v = d.done
                if waited.get(k, 0) >= v:
                    continue
                if need.get(k, 0) < v:
                    need[k] = v
            for k, v in need.items():
                h.wait_ge(sems[k], v)
                waited[k] = v
            ins = op.fn(h)
            if op.is_dma is not None:
                ins.then_inc(sems[op.done[0]], op.inc)
            elif op.sig:
                ins.then_inc(sems[op.done[0]], 1)


class Lay:
    def __init__(self):
        self.off = {}
        self.n = 0

    def add(self, name, w):
        self.off[name] = self.n
        self.n += w

    def o(self, name):
        return self.off[name]


def make_layouts():
    L = Lay()
    L.add("flags", 4)
    L.add("zidx_m", 8)
    L.add("zidx_c", 4)
    L.add("cfidx", 8)
    L.add("logit", 16)
    L.add("svin", 16)
    L.add("adab", 4 * 48)
    L.add("normw", 4 * 16)
    L.add("fnw", 8)
    L.add("convw", 2 * 4 * 31)
    L.add("convlnw", 8)
    L.add("convlnb", 8)
    L.add("poolsc", 8)
    L.add("icm", 64)
    L.add("icc", 64)
    L.add("eps", 1)
    L.add("one", 1)
    L.add("lns", 1)
    LT = Lay()
    LT.add("ropeC", TT)
    LT.add("ropeS", TT)
    LT.add("a1idx", 896)
    LT.add("a2idx", 896)
    LT.add("xifidx", 512)
    LT.add("xibidx", 512)
    LT.add("sgtab", 2 * 1536)
    LT.add("pwsg", 2 * 1024)
    return L, LT


LAY, LAYT = make_layouts()
NCB = 384


def fm(v):
    v = np.asarray(v, np.float32)
    return np.ascontiguousarray(v.reshape(-1, 128).T)


def host_consts(core, inp):
    b, half = core // 2, core % 2
    L, LT = LAY, LAYT
    cst = np.zeros((128, L.n), np.float32)
    tab = np.zeros((128, LT.n), np.float32)

    def put(arr, lay, name, val):
        val = np.asarray(val, np.float32)
        o = lay.o(name)
        arr[:, o:o + val.shape[1]] = val

    pairs = (16, 24, 24)

    def angles(p_seq, p_row, p_col):
        parts = []
        for p_, n in zip((p_seq, p_row, p_col), pairs):
            freq = (np.float32(10000.0) ** (-np.arange(n, dtype=np.float32) / np.float32(n))).astype(np.float32)
            parts.append(p_[:, None].astype(np.float32) * freq[None, :])
        return np.concatenate(parts, axis=-1)

    t = np.arange(half * TM, (half + 1) * TM)
    ang_x = angles(np.full((TM,), float(TCX), np.float32), (t // 64).astype(np.float32), (t % 64).astype(np.float32))
    zc = np.zeros((TCX,), np.float32)
    ang_c = angles(np.arange(TCX, dtype=np.float32), zc, zc)
    ang = np.concatenate([ang_x, ang_c], axis=0).astype(np.float32)
    cos = np.cos(ang).T.astype(np.float32)
    sin = np.sin(ang).T.astype(np.float32)
    put(tab, LT, "ropeC", np.concatenate([cos, cos], axis=0))
    put(tab, LT, "ropeS", np.concatenate([-sin, sin], axis=0))
    p = np.arange(128)[:, None]
    i = np.arange(896)[None, :]
    dlt = i - p - 384
    put(tab, LT, "a1idx", np.where(dlt >= 0, dlt, BIG))
    put(tab, LT, "a2idx", np.where(dlt < 0, -dlt - 1, BIG))
    c = np.arange(512, dtype=np.float32)[None, :] + np.zeros((128, 1), np.float32)
    put(tab, LT, "xifidx", c + 1.0)
    put(tab, LT, "xibidx", 511.0 - c)
    sgt = np.zeros((128, 2 * 1536), np.float32)
    pws = np.zeros((128, 2 * 1024), np.float32)
    pw = np.asarray(inp["pool_w"], np.float32)
    sw = np.asarray(inp["sg_w"], np.float32)
    for l in range(2):
        sgt[:, l * 1536:l * 1536 + 512] = np.asarray(inp["sg_ln_w"][l], np.float32)[None, :]
        sgt[:, l * 1536 + 512:l * 1536 + 1024] = np.asarray(inp["sg_ln_b"][l], np.float32)[None, :]
        sgt[:, l * 1536 + 1024:l * 1536 + 1536] = np.asarray(inp["sg_b"][l], np.float32).reshape(1, 512)
        pws[:, l * 1024:l * 1024 + 512] = pw[l].transpose(1, 0, 2).reshape(128, 512)
        pws[:, l * 1024 + 512:l * 1024 + 1024] = sw[l].transpose(2, 0, 1).reshape(128, 512)
    put(tab, LT, "sgtab", sgt)
    put(tab, LT, "pwsg", pws)

    put(cst, L, "flags", np.tile(np.array([1.0 - half, float(half), 0, 0], np.float32)[None, :], (128, 1)))
    j = np.arange(4)[None, :]
    m = 128 * j + p
    put(cst, L, "zidx_m", np.concatenate([511.0 - m, m], axis=1))
    j2 = np.arange(2)[None, :]
    m2 = 128 * j2 + p
    put(cst, L, "zidx_c", np.concatenate([255.0 - m2, m2], axis=1))
    ti = np.arange(4, dtype=np.float32)
    put(cst, L, "cfidx", np.tile(np.concatenate([512.0 * ti, 512.0 * (3 - ti)])[None, :], (128, 1)))
    put(cst, L, "logit", np.tile(np.asarray(inp["ret_decay_logit"], np.float32).reshape(1, 16), (128, 1)))
    svin = np.zeros((128, 8, 2), np.float32)
    svin[:, :, 0] = fm(inp["c"][b])
    svin[:, :, 1] = fm(inp["c_ctx"])
    put(cst, L, "svin", svin.reshape(128, 16))
    put(cst, L, "adab", np.concatenate([fm(inp["ada_b"][l]) for l in range(4)], axis=1))
    put(cst, L, "normw", np.concatenate([fm(inp["norm_w"][l, s]) for l in range(4) for s in range(2)], axis=1))
    put(cst, L, "fnw", fm(inp["final_norm_w"]))
    cw = np.asarray(inp["conv_dw_w"], np.float32)
    put(cst, L, "convw", np.concatenate([cw[l][:, ch * 128:(ch + 1) * 128].T for l in range(2) for ch in range(4)], axis=1))
    put(cst, L, "convlnw", np.concatenate([fm(inp["conv_ln_w"][l]) for l in range(2)], axis=1))
    put(cst, L, "convlnb", np.concatenate([fm(inp["conv_ln_b"][l]) for l in range(2)], axis=1))
    put(cst, L, "poolsc", np.concatenate([fm(inp["pool_scale"][l]) for l in range(2)], axis=1))

    def invcnt(lo_edge, hi_edge, n):
        out = np.zeros((4, 16), np.float32)
        for gi, w in enumerate((2, 4, 8, 16)):
            left = w // 2
            right = w - 1 - left
            for k in range(8):
                cl = min(left, k) if lo_edge else left
                out[gi, k] = 1.0 / (cl + right + 1)
                tpos = n - 8 + k
                cr = min(right, n - 1 - tpos) if hi_edge else right
                out[gi, 8 + k] = 1.0 / (left + cr + 1)
        return np.tile(out.reshape(1, 64), (128, 1))
    put(cst, L, "icm", invcnt(half == 0, half == 1, TM))
    put(cst, L, "icc", invcnt(True, True, TCX))
    cst[:, L.o("eps")] = EPS
    cst[:, L.o("one")] = 1.0
    cst[:, L.o("lns")] = np.log(128.0 ** -0.5)
    cb = np.zeros((128, NCB), np.float32)
    cb[:, 0:128] = 1.0
    cb[:, 128:256] = np.eye(128, dtype=np.float32)
    for q in range(128):
        cb[(q + 64) % 128, 256 + q] = 1.0
    return cst, tab, cb


def wlay(w):
    w = np.asarray(w, np.float32)
    k, n = w.shape
    return np.ascontiguousarray(w.reshape(k // 128, 128, n).transpose(1, 0, 2))


def build_program(n_layers=4, debug_x=False):
    nc = bass.Bass("TRN2", target_bir_lowering=False)
    pg = Prog()
    es = ExitStack()
    E = es.enter_context

    def din(name, shape):
        return nc.dram_tensor(name, list(shape), F32, kind="ExternalInput").ap()

    xin = din("xin", [128, KC, TT])
    cst_d = din("cst", [128, LAY.n])
    tab_d = din("tab", [128, LAYT.n])
    cstb_d = din("cstb", [128, NCB])
    ada_d = din("ada_w", [4, 128, KC, 3072])
    evin_d = din("even_w_in", [2, 128, KC, 3072])
    evout_d = din("even_w_out", [2, 128, KC, D])
    odin_d = din("odd_w_in", [2, 128, KC, 1536])
    odout_d = din("odd_w_out", [2, 128, KC, D])
    wg_d = din("ffn_w_gate", [4, 128, KC, DFF])
    wu_d = din("ffn_w_up", [4, 128, KC, DFF])
    wd_d = din("ffn_w_down", [4, 128, NFF, D])
    out_d = nc.dram_tensor("out", [128, KC, TM], F32, kind="ExternalOutput").ap()

    def scratch(name, shape, dt=BF16):
        return nc.dram_tensor(name, list(shape), dt).ap()

    q_d = scratch("q_d", [128, 4, TT])
    k_d = scratch("k_d", [128, 4, TT])
    v_d = scratch("v_d", [128, TT // 128, 512])
    g_d = scratch("g_d", [128, 4, TT])
    um_d = scratch("um_d", [128, 4, TM + 2 * HC])
    uc_d = scratch("uc_d", [128, 4, TCX + 2 * HC])
    r_d = scratch("r_d", [128, 4, TT])
    pm_d = scratch("pm_d", [128, 4, TM + 2 * HP])
    pc_d = scratch("pc_d", [128, 4, TCX + 2 * HP])
    xi_d = nc.dram_tensor("xi_d", [128, XW], F32)
    xo_d = nc.dram_tensor("xo_d", [256, XW], F32)
    xio_d = nc.dram_tensor("xio_d", [128, XWO], F32)
    xoo_d = nc.dram_tensor("xoo_d", [256, XWO], F32)
    xm_d = nc.dram_tensor("xm_d", [128, 192], F32)
    xmo_d = nc.dram_tensor("xmo_d", [256, 192], F32)

    def sb(name, shape, dt=F32):
        return E(nc.sbuf_tensor(name, list(shape), dt))

    x_sb = sb("x_sb", [128, KC, TT])
    h_sb = sb("h_sb", [128, KC, TT], BF16)
    cst = sb("cst_sb", [128, LAY.n])
    cstb = sb("cstb_sb", [128, NCB], BF16)
    WSLOT = 8192
    w_sb = [sb("w_sb%d" % i, [128, WSLOT], BF16) for i in range(2)]
    modall = sb("modall", [128, 4, 2, 48])
    modx = sb("modx", [128, 192])
    modg = sb("modg", [128, 2, 192])
    nv_sb = sb("nv_sb", [128, 6, KC, 2])
    sv_sb = sb("sv_sb", [128, KC, 2], BF16)
    rs_sb = sb("rstd", [128, 512])
    TW = 512 + 2 * HP
    t32 = [sb("t32_%d" % i, [128, TW]) for i in range(3)]
    tb16 = [sb("tb16_%d" % i, [128, 512], BF16) for i in range(3)]
    o16 = [sb("o16_%d" % i, [128, 4, 512], BF16) for i in range(2)]
    l16 = [sb("l16_%d" % i, [128, 4, 512 + 2 * HC], BF16) for i in range(2)]
    l16x = sb("l16x", [128, 4, 512], BF16)
    acc32 = sb("acc32", [128, 4, 512])
    accf = acc32[:].rearrange("p a b -> p (a b)")
    mix_sb = sb("mix", [128, 8, 512], BF16)
    ropeT = mix_sb[:, 0:4, :].rearrange("p a b -> p (a b)").bitcast(F32)
    hid_sb = mix_sb[:, 4:8, :]
    lg_sb = sb("lg", [128, 16])
    dtab = sb("dtab", [128, 896], BF16)
    xi_sb = sb("xi", [128, 2, 512], BF16)
    zm_sb = sb("zm", [128, 2, 4, 4])
    zc_sb = sb("zc", [128, 2, 4, 2])
    dec_sb = sb("dec", [128, 2, 4])
    cf_sb = sb("cf", [128, 2, 4, 4])
    kz_sb = [sb("kz%d" % i, [128, 2, 128], BF16) for i in range(2)]
    S_sb = sb("S", [128, 2, 512])
    Sflat = S_sb[:].rearrange("p a b -> p (a b)")
    pwsg = Sflat.bitcast(BF16)[:, 0:1024]
    xh_sb = sb("xh", [128, 8 * HC])
    sm = sb("small", [128, 8])

    psA = [E(nc.psum_tensor("psA%d" % i, [128, 512], F32)) for i in range(7)]
    psT = E(nc.psum_tensor("psT", [128, 1024], BF16))
    PST = 7

    def C(name, a=0, n=1):
        o = LAY.o(name) + a
        return cst[:, o:o + n]

    ones = cstb[:, 0:128]
    ident = cstb[:, 128:256]
    perm = cstb[:, 256:384]
    epsc = C("eps")
    fA = C("flags", 0)
    fB = C("flags", 1)

    def PS(i):
        return ("ps", i)

    def fsz(ap):
        n = 1
        for v in list(ap.shape)[1:]:
            n *= int(v)
        return n

    def mm(out, lhsT, rhs, start, stop, reads, writes):
        pg.add("pe", lambda h: h.matmul(out, lhsT=lhsT, rhs=rhs, start=start, stop=stop), reads=reads, writes=writes)

    def act(out, in_, func, reads, writes, bias=None, scale=None, accum=None):
        kw = {}
        if bias is not None:
            kw["bias"] = bias
        if scale is not None:
            kw["scale"] = scale
        if accum is not None:
            kw["accum_out"] = accum
        pg.add("act", lambda h: h.activation(out=out, in_=in_, func=func, **kw), reads=reads, writes=writes, size=fsz(out))

    def tt(eng, out, in0, in1, op, reads, writes):
        pg.add(eng, lambda h: h.tensor_tensor(out=out, in0=in0, in1=in1, op=op), reads=reads, writes=writes, size=fsz(out))

    def ts(eng, out, in0, s1, s2, op0, op1, reads, writes):
        if s2 is None:
            pg.add(eng, lambda h: h.tensor_scalar(out=out, in0=in0, scalar1=s1, scalar2=None, op0=op0), reads=reads, writes=writes, size=fsz(out))
        else:
            pg.add(eng, lambda h: h.tensor_scalar(out=out, in0=in0, scalar1=s1, scalar2=s2, op0=op0, op1=op1), reads=reads, writes=writes, size=fsz(out))

    def stt(eng, out, in0, scalar, in1, op0, op1, reads, writes):
        pg.add(eng, lambda h: h.scalar_tensor_tensor(out=out, in0=in0, scalar=scalar, in1=in1, op0=op0, op1=op1), reads=reads, writes=writes, size=fsz(out))

    def cp(eng, out, in_, reads, writes):
        pg.add(eng, lambda h: h.tensor_copy(out=out, in_=in_), reads=reads, writes=writes, size=fsz(out))

    def recip(ap, key):
        pg.add("dve", lambda h: h.reciprocal(out=ap, in_=ap), [key], [key], size=fsz(ap))

    dma_prev = {}

    def dma(eng, out, in_, reads, writes, key):
        op = pg.add(eng, lambda h: h.dma_start(out=out, in_=in_), reads=reads, writes=writes, dma=key)
        prev = dma_prev.get(key)
        if prev is not None and prev not in op.deps:
            op.deps.append(prev)
            prev.sig = True
        dma_prev[key] = op
        return op

    dbg_list = []

    def dbg_dump(name, ap, keys):
        if not (debug_x and DBG_DUMPS):
            return
        shp = list(ap.shape)
        dt = ap.dtype
        dd = nc.dram_tensor("dbg_" + name, shp, dt, kind="ExternalOutput").ap()
        idx = tuple(slice(None) for _ in shp)
        dma("sp", dd[idx], ap, keys, ["dbgk_" + name], "dbg%d" % (len(dbg_list) % 4))
        dbg_list.append("dbgk_" + name)

    stc = {"n": 0}

    def stkey():
        stc["n"] += 1
        return "st%d" % (stc["n"] % 4)

    ldc = {"n": 0}

    def ldkey():
        ldc["n"] += 1
        return "ld%d" % (ldc["n"] % 4)

    dma("sp", cst[:], cst_d[:, :], [], ["cst"], "cst")
    dma("pool", cstb[:], cstb_d[:, :], [], ["cstb"], "cstb")
    for c in range(KC):
        dma("sp", x_sb[:, c, :], xin[:, c, :], [], [("x", c, t) for t in range(5)], ("xin", c % 4))
    pg.add("dve", lambda h: h.memset(o16[0][:, :, 0:16], 0.0), [], [("o16", 0)])
    dma("sp", uc_d[:, :, 0:HC], o16[0][:, :, 0:HC], [("o16", 0)], ["uc_h0"], stkey())
    dma("sp", uc_d[:, :, HC + TCX:], o16[0][:, :, 0:HC], [("o16", 0)], ["uc_h1"], stkey())
    dma("sp", pc_d[:, :, 0:HP], o16[0][:, :, 0:HP], [("o16", 0)], ["pc_h0"], stkey())
    dma("sp", pc_d[:, :, HP + TCX:], o16[0][:, :, 0:HP], [("o16", 0)], ["pc_h1"], stkey())
    act(sv_sb[:].rearrange("p a b -> p (a b)"), C("svin", 0, 16), AF.Silu, ["cst"], ["sv"])
    act(lg_sb[:], C("logit", 0, 16), AF.Exp, ["cst"], ["lg"], scale=-1.0)
    act(lg_sb[:], lg_sb[:], AF.Ln, ["lg", "cst"], ["lg"], bias=C("one"))
    ts("dve", lg_sb[:], lg_sb[:], -1.0, None, ALU.mult, None, ["lg"], ["lg"])

    wstate = {"n": 0}

    def wload(parts, slot=None):
        s = wstate["n"] % 2 if slot is None else slot
        wstate["n"] = s + 1
        views = []
        off = 0
        for pi, ap in enumerate(parts):
            shp = list(ap.shape)
            n = shp[1] * shp[2]
            v = w_sb[s][:, off:off + n].rearrange("p (a b) -> p a b", a=shp[1])
            dma("pool", v, ap, [], [("W", s)], ("W", s, pi))
            views.append(v)
            off += n
        assert off <= WSLOT
        return s, views

    def mod_prologue():
        for li in range(4):
            for pi in range(3):
                s, (wv,) = wload([ada_d[li, :, :, pi * 1024:(pi + 1) * 1024]])
                for cc in range(8):
                    bank = cc % 2
                    for kc in range(KC):
                        mm(psA[bank][:, 0:2], wv[:, kc, cc * 128:(cc + 1) * 128], sv_sb[:, kc, :],
                           kc == 0, kc == KC - 1, [("W", s), "sv"], [PS(bank)])
                    o = li * 48 + (pi * 8 + cc) * 2
                    cp("dve", modx[:, o:o + 2], psA[bank][:, 0:2], [PS(bank)], ["modx"])
        dma("pool", xm_d.ap(), modx[:, :], ["modx"], ["xm_d"], "xst0")
        allgather(xm_d, xmo_d, ["xm_d"], "xmo_d")
        dma("pool", modg[:], xmo_d.ap().rearrange("(r p) n -> p r n", p=128), ["xmo_d"], ["modg"], "xld0")
        G = modg[:].rearrange("p r (l c v) -> p r l c v", l=4, c=24)
        for li in range(4):
            ab = C("adab", li * 48, 48).rearrange("p (r c) -> p r c", r=2)
            for s_ in range(2):
                mx = modall[:, li, s_, :].rearrange("p (r c) -> p r c", r=2)
                tt("dve", mx, G[:, :, li, :, s_], ab, ALU.add, ["modg", "cst"], ["mod"])

    def mod_layer(li):
        for s_ in range(2):
            M = modall[:, li, s_, :]
            stt("dve", nv_sb[:, 0, :, s_], M[:, 8:16], 1.0, C("normw", li * 16, 8), ALU.add, ALU.mult, ["mod", "cst"], ["nv"])
            cp("dve", nv_sb[:, 1, :, s_], M[:, 0:8], ["mod"], ["nv"])
            cp("dve", nv_sb[:, 2, :, s_], M[:, 16:24], ["mod"], ["nv"])
            stt("dve", nv_sb[:, 3, :, s_], M[:, 32:40], 1.0, C("normw", li * 16 + 8, 8), ALU.add, ALU.mult, ["mod", "cst"], ["nv"])
            cp("dve", nv_sb[:, 4, :, s_], M[:, 24:32], ["mod"], ["nv"])
            cp("dve", nv_sb[:, 5, :, s_], M[:, 40:48], ["mod"], ["nv"])

    def tile_stream(ti):
        return 1 if ti == CTX_TILE else 0

    def rstd_from_ps(bank, n, scale):
        act(rs_sb[:, 0:n], psA[bank][:, 0:n], AF.Ln, [PS(bank), "cst"], ["rs"], bias=epsc, scale=scale)
        act(rs_sb[:, 0:n], rs_sb[:, 0:n], AF.Exp, ["rs"], ["rs"], scale=-0.5)

    def sumsq_x(ti):
        t0, n = TILES[ti]
        for c in range(KC):
            q = tb16[c % 2]
            act(q[:, 0:n], x_sb[:, c, t0:t0 + n], AF.Square, [("x", c, ti)], [("tb16", c % 2)])
            mm(psA[6][:, 0:n], ones, q[:, 0:n], c == 0, c == KC - 1, [("tb16", c % 2), "cstb"], [PS(6)])
        rstd_from_ps(6, n, 1.0 / D)

    def norm_tile(ti, which):
        t0, n = TILES[ti]
        s_ = tile_stream(ti)
        sumsq_x(ti)
        for c in range(KC):
            tmp = t32[c % 2]
            tt("dve", tmp[:, 0:n], x_sb[:, c, t0:t0 + n], rs_sb[:, 0:n], ALU.mult, [("x", c, ti), "rs"], [("t32", c % 2)])
            a = nv_sb[:, 3 * which, c, s_:s_ + 1]
            b = nv_sb[:, 3 * which + 1, c, s_:s_ + 1]
            act(h_sb[:, c, t0:t0 + n], tmp[:, 0:n], AF.Identity, [("t32", c % 2), "nv"], [("h", ti)], bias=b, scale=a)

    def final_tile(ti):
        t0, n = TILES[ti]
        sumsq_x(ti)
        for c in range(KC):
            tmp = t32[c % 2]
            tt("dve", tmp[:, 0:n], x_sb[:, c, t0:t0 + n], rs_sb[:, 0:n], ALU.mult, [("x", c, ti), "rs"], [("t32", c % 2)])
            act(acc32[:, c % 4, 0:n], tmp[:, 0:n], AF.Identity, [("t32", c % 2), "cst"], [("acc", c % 4)], scale=C("fnw", c))
            dma("sp", out_d[:, c, t0:t0 + n], acc32[:, c % 4, 0:n], [("acc", c % 4)], [("out", ti, c)], "out%d" % (c % 4))

    def proj_fm(bank, wv, s, col0, ti):
        t0, n = TILES[ti]
        for kc in range(KC):
            mm(psA[bank][:, 0:n], wv[:, kc, col0:col0 + 128], h_sb[:, kc, t0:t0 + n], kc == 0, kc == KC - 1,
               [("W", s), ("h", ti)], [PS(bank)])

    def load_rope(ti):
        t0, n = TILES[ti]
        dma("sp", ropeT[:, 0:n], tab_d[:, LAYT.o("ropeC") + t0:LAYT.o("ropeC") + t0 + n], [], [("mixs", 0), ("mixs", 1)], ldkey())
        dma("sp", ropeT[:, 512:512 + n], tab_d[:, LAYT.o("ropeS") + t0:LAYT.o("ropeS") + t0 + n], [], [("mixs", 2), ("mixs", 3)], ldkey())

    rope_ctr = {"n": 0}

    def rope(bank, ti, out_ap, okey):
        t0, n = TILES[ti]
        i = rope_ctr["n"] % 2
        rope_ctr["n"] += 1
        tb = tb16[i]
        act(tb[:, 0:n], psA[bank][:, 0:n], AF.Copy, [PS(bank)], [("tb16", i)])
        tt("dve", t32[2][:, 0:n], psA[bank][:, 0:n], ropeT[:, 0:n], ALU.mult, [PS(bank), ("mixs", 0), ("mixs", 1)], [("t32", 2)])
        mm(psA[5][:, 0:n], perm, tb[:, 0:n], True, True, [("tb16", i), "cstb"], [PS(5)])
        tt("dve", t32[i][:, 0:n], psA[5][:, 0:n], ropeT[:, 512:512 + n], ALU.mult, [PS(5), ("mixs", 2), ("mixs", 3)], [("t32", i)])
        tt("dve", out_ap, t32[i][:, 0:n], t32[2][:, 0:n], ALU.add, [("t32", i), ("t32", 2)], [okey])

    RG = [[0, 1], [2, 3], [4, 5], [6, 7]]

    def allgather(src, dst, skeys, dkey, groups=None):
        groups = RG if groups is None else groups
        pg.add("pool", lambda h: h.collective_compute("AllGather", ALU.bypass, replica_groups=groups,
                                                      ins=[src.ap().opt()], outs=[dst.ap().opt()]),
               list(skeys), [dkey], dma="cc", inc=1)

    def even_small_tables(j):
        lo = 8 * j
        for hd in range(4):
            for d_ in range(2):
                lx = lg_sb[:, lo + 4 * d_ + hd:lo + 4 * d_ + hd + 1]
                act(zm_sb[:, d_, hd, :], C("zidx_m", d_ * 4, 4), AF.Exp, ["cst", "lg"], ["zm"], scale=lx)
                act(zc_sb[:, d_, hd, :], C("zidx_c", d_ * 2, 2), AF.Exp, ["cst", "lg"], ["zm"], scale=lx)
                act(dec_sb[:, d_, hd:hd + 1], lx, AF.Exp, ["lg"], ["dec"], scale=512.0)
                act(cf_sb[:, d_, :, hd], C("cfidx", d_ * 4, 4), AF.Exp, ["cst", "lg"], ["cf"], scale=lx)
        ts("dve", cf_sb[:, 0, :, :], cf_sb[:, 0, :, :], fB, None, ALU.mult, None, ["cf", "cst"], ["cf"])
        ts("dve", cf_sb[:, 1, :, :], cf_sb[:, 1, :, :], fA, None, ALU.mult, None, ["cf", "cst"], ["cf"])

    def head_tables(j, hd):
        lo = 8 * j
        lf = lg_sb[:, lo + hd:lo + hd + 1]
        lb = lg_sb[:, lo + 4 + hd:lo + 4 + hd + 1]
        acck = [("acc", i) for i in range(4)]
        dma("sp", accf[:, 0:896], tab_d[:, LAYT.o("a1idx"):LAYT.o("a1idx") + 896], [], acck, ldkey())
        dma("sp", accf[:, 1024:1920], tab_d[:, LAYT.o("a2idx"):LAYT.o("a2idx") + 896], [], acck, ldkey())
        act(accf[:, 0:896], accf[:, 0:896], AF.Exp, acck + ["lg", "cst"], acck, scale=lf, bias=C("lns"))
        act(accf[:, 1024:1920], accf[:, 1024:1920], AF.Exp, acck + ["lg", "cst"], acck, scale=lb, bias=C("lns"))
        tt("dve", dtab[:, :], accf[:, 0:896], accf[:, 1024:1920], ALU.add, acck, ["dtab"])
        dma("sp", accf[:, 0:512], tab_d[:, LAYT.o("xifidx"):LAYT.o("xifidx") + 512], [], acck, ldkey())
        dma("sp", accf[:, 512:1024], tab_d[:, LAYT.o("xibidx"):LAYT.o("xibidx") + 512], [], acck, ldkey())
        act(xi_sb[:, 0, :], accf[:, 0:512], AF.Exp, acck + ["lg", "cst"], ["xi"], scale=lf, bias=C("lns"))
        act(xi_sb[:, 1, :], accf[:, 512:1024], AF.Exp, acck + ["lg", "cst"], ["xi"], scale=lb, bias=C("lns"))

    def even_mixer(li, full_ctx):
        j = li // 2
        even_small_tables(j)
        order = [CTX_TILE, 0, 1, 2, 3]
        tiles_full = order if full_ctx else [0, 1, 2, 3]
        for ti in order:
            if ti not in tiles_full:
                norm_tile(ti, 0)
        norm_tile(tiles_full[0], 0)
        s2, (wa, wgb) = wload([evin_d[j, :, :, 2048:2560], evin_d[j, :, :, 2560:3072]])
        for i_, ti in enumerate(tiles_full):
            if i_ + 1 < len(tiles_full):
                norm_tile(tiles_full[i_ + 1], 0)
            t0, n = TILES[ti]
            ub = o16[ti % 2]
            for ch in range(4):
                b0 = (2 * ch) % 4
                b1 = b0 + 1
                proj_fm(b0, wa, s2, ch * 128, ti)
                proj_fm(b1, wgb, s2, ch * 128, ti)
                act(t32[ch % 2][:, 0:n], psA[b1][:, 0:n], AF.Sigmoid, [PS(b1)], [("t32", ch % 2)])
                tt("dve", ub[:, ch, 0:n], psA[b0][:, 0:n], t32[ch % 2][:, 0:n], ALU.mult, [PS(b0), ("t32", ch % 2)], [("o16", ti % 2)])
            if ti == CTX_TILE:
                dma("sp", uc_d[:, :, HC:HC + n], ub[:, :, 0:n], [("o16", ti % 2)], [("u_d", ti)], stkey())
            else:
                dma("sp", um_d[:, :, HC + t0:HC + t0 + n], ub[:, :, 0:n], [("o16", ti % 2)], [("u_d", ti)], stkey())
            if ti == 0:
                cp("dve", xh_sb[:, 0:4 * HC].rearrange("p (a b) -> p a b", a=4), ub[:, :, 0:HC], [("o16", ti % 2)], ["xh"])
            if ti == 3:
                cp("dve", xh_sb[:, 4 * HC:8 * HC].rearrange("p (a b) -> p a b", a=4), ub[:, :, n - HC:n], [("o16", ti % 2)], ["xh"])
        s3, (wk, wvv) = wload([evin_d[j, :, :, 512:1024], evin_d[j, :, :, 1024:1536]], slot=1 - s2)
        so = 1 - s3
        sp32 = w_sb[so][:, :].bitcast(F32).rearrange("p (a b c) -> p a b c", a=2, b=4)
        SPK = ("W", so)
        for ti in order:
            t0, n = TILES[ti]
            nb = n // 128
            load_rope(ti)
            ob = o16[0]
            for jb in range(nb):
                bank = jb % 2
                for kc in range(KC):
                    mm(psA[bank][:, :], h_sb[:, kc, t0 + jb * 128:t0 + (jb + 1) * 128], wvv[:, kc, :], kc == 0, kc == KC - 1,
                       [("W", s3), ("h", ti)], [PS(bank)])
                act(ob[:, jb, :], psA[bank][:, :], AF.Copy, [PS(bank)], [("o16", 0)])
            dma("sp", v_d[:, t0 // 128:t0 // 128 + nb, :], ob[:, 0:nb, :], [("o16", 0)], [("v_d", ti)], stkey())
            kb = o16[1]
            zt = zc_sb if ti == CTX_TILE else zm_sb
            for hd in range(4):
                bank = 2 + hd % 2
                proj_fm(bank, wk, s3, hd * 128, ti)
                rope(bank, ti, kb[:, hd, 0:n], ("o16", 1))
                for jb in range(nb):
                    pg.add("pe", lambda h, hd=hd, jb=jb: h.transpose(psT[:, jb * 128:(jb + 1) * 128], kb[:, hd, jb * 128:(jb + 1) * 128], ident),
                           [("o16", 1), "cstb"], [PS(PST)])
                for d_ in range(2):
                    kvbank = 4 if d_ == 0 else 6
                    for jb in range(nb):
                        kz = kz_sb[jb % 2]
                        ts("dve", kz[:, d_, :], psT[:, jb * 128:(jb + 1) * 128], zt[:, d_, hd, jb:jb + 1], None, ALU.mult, None,
                           [PS(PST), "zm"], [("kz", jb % 2, d_)])
                        mm(psA[kvbank][:, hd * 128:(hd + 1) * 128], kz[:, d_, :], ob[:, jb, hd * 128:(hd + 1) * 128], jb == 0, jb == nb - 1,
                           [("kz", jb % 2, d_), ("o16", 0)], [PS(kvbank)])
            dma("sp", k_d[:, :, t0:t0 + n], kb[:, :, 0:n], [("o16", 1)], [("k_d", ti)], stkey())
            if ti == CTX_TILE:
                ts("dve", S_sb[:, 0, :], psA[4][:, :], fA, None, ALU.mult, None, [PS(4), "cst"], ["S"])
                ts("dve", S_sb[:, 1, :], psA[6][:, :], fB, None, ALU.mult, None, [PS(6), "cst"], ["S"])
            else:
                cp("dve", sp32[:, 0, ti, :], S_sb[:, 0, :], ["S"], [SPK])
                for hd in range(4):
                    hs = slice(hd * 128, (hd + 1) * 128)
                    stt("dve", S_sb[:, 0, hs], S_sb[:, 0, hs], dec_sb[:, 0, hd:hd + 1], psA[4][:, hs], ALU.mult, ALU.add,
                        ["S", "dec", PS(4)], ["S"])
                cp("dve", acc32[:, ti, :], psA[6][:, :], [PS(6)], [("acc", ti)])
        for ti in (3, 2, 1, 0):
            cp("dve", sp32[:, 1, ti, :], S_sb[:, 1, :], ["S"], [SPK])
            for hd in range(4):
                hs = slice(hd * 128, (hd + 1) * 128)
                stt("dve", S_sb[:, 1, hs], S_sb[:, 1, hs], dec_sb[:, 1, hd:hd + 1], acc32[:, ti, hs], ALU.mult, ALU.add,
                    ["S", "dec", ("acc", ti)], ["S"])
        sq_, (wq, wgt) = wload([evin_d[j, :, :, 0:512], evin_d[j, :, :, 1536:2048]], slot=s3)
        cwo = j * 124
        HK = [("h", t) for t in range(5)]
        dgf = h_sb[:].rearrange("p a b -> p (a b)")
        dma("pool", xi_d.ap()[:, 0:1024], Sflat, ["S"], ["xi_d"], "xst0")
        dma("pool", xi_d.ap()[:, 1024:XW], xh_sb[:, :], ["xh"], ["xi_d2"], "xst1")
        allgather(xi_d, xo_d, ["xi_d", "xi_d2"], "xo_d")
        acck = [("acc", i) for i in range(4)]
        dma("pool", accf[:, 0:512], xo_d.ap()[0:128, 0:512], ["xo_d"], acck, "xld0")
        dma("pool", accf[:, 512:1024], xo_d.ap()[128:256, 512:1024], ["xo_d"], acck, "xld1")
        dma("pool", accf[:, 1024:1024 + 4 * HC], xo_d.ap()[0:128, 1024 + 4 * HC:XW], ["xo_d"], acck, "xld0")
        dma("pool", accf[:, 1024 + 4 * HC:XW], xo_d.ap()[128:256, 1024:1024 + 4 * HC], ["xo_d"], acck, "xld1")
        for i_, ti in enumerate(tiles_full):
            t0, n = TILES[ti]
            load_rope(ti)
            for hd in range(4):
                bank = hd % 2
                proj_fm(bank, wq, sq_, hd * 128, ti)
                rope(bank, ti, o16[0][:, hd, 0:n], ("o16", 0))
                bank = 2 + hd % 2
                proj_fm(bank, wgt, sq_, hd * 128, ti)
                act(o16[1][:, hd, 0:n], psA[bank][:, 0:n], AF.Silu, [PS(bank)], [("o16", 1)])
            dma("sp", q_d[:, :, t0:t0 + n], o16[0][:, :, 0:n], [("o16", 0)], [("q_d", ti)], stkey())
            dma("sp", g_d[:, :, t0:t0 + n], o16[1][:, :, 0:n], [("o16", 1)], [("g_d", ti)], stkey())
        for d_ in range(2):
            for ti in range(4):
                for hd in range(4):
                    hs = slice(hd * 128, (hd + 1) * 128)
                    stt("dve", o16[d_][:, ti, hs], accf[:, d_ * 512 + hd * 128:d_ * 512 + (hd + 1) * 128], cf_sb[:, d_, ti, hd:hd + 1],
                        sp32[:, d_, ti, hs], ALU.mult, ALU.add, acck + ["cf", SPK], [("o16", d_)])
        hl = tb16[2][:, 0:8 * HC].rearrange("p (a b) -> p a b", a=4)
        ts("dve", hl[:, :, 0:HC], accf[:, 1024:1024 + 4 * HC].rearrange("p (a b) -> p a b", a=4), fB, None, ALU.mult, None,
           acck + ["cst"], [("tb16", 2)])
        ts("dve", hl[:, :, HC:2 * HC], accf[:, 1024 + 4 * HC:XW].rearrange("p (a b) -> p a b", a=4), fA, None, ALU.mult, None,
           acck + ["cst"], [("tb16", 2)])
        dma("sp", um_d[:, :, 0:HC], hl[:, :, 0:HC], [("tb16", 2)], [("u_d", 0)], stkey())
        dma("sp", um_d[:, :, HC + TM:], hl[:, :, HC:2 * HC], [("tb16", 2)], [("u_d", 3)], stkey())
        s4, (wo,) = wload([evout_d[j, :, :, :]], slot=so)
        l16b = l16x
        LQ = [l16[0][:, 0, 0:512], l16b[:, 0, 0:512]]
        LK = [l16[0][:, 1, 0:512], l16b[:, 1, 0:512]]
        LG = [l16[0][:, 2, 0:512], l16b[:, 2, 0:512]]
        RB = [l16[0][:, 3, 0:512], l16b[:, 3, 0:512]]
        LV = [l16[1][:, :, 0:128], l16[1][:, :, 128:256]]
        its = [(hd, ti) for hd in range(4) for ti in tiles_full]

        def ret_loads(it):
            hd, ti = its[it]
            sset = it % 2
            t0, n = TILES[ti]
            nb = n // 128
            hs = slice(hd * 128, (hd + 1) * 128)
            dma("sp", LQ[sset][:, 0:n], q_d[:, hd, t0:t0 + n], [("q_d", ti)], [("lq", sset)], ldkey())
            dma("sp", LK[sset][:, 0:n], k_d[:, hd, t0:t0 + n], [("k_d", ti)], [("lk", sset)], ldkey())
            dma("sp", LV[sset][:, 0:nb, :], v_d[:, t0 // 128:t0 // 128 + nb, hs], [("v_d", ti)], [("lv", sset)], ldkey())

        def ret_load_g(it):
            hd, ti = its[it]
            sset = it % 2
            t0, n = TILES[ti]
            dma("sp", LG[sset][:, 0:n], g_d[:, hd, t0:t0 + n], [("g_d", ti)], [("lg_", sset)], ldkey())

        L0K = [("lq", 0), ("lk", 0), ("lg_", 0), ("rb", 0)]
        L1K = [("lv", 0), ("lv", 1)]
        def stage_a(it):
            hd, ti = its[it]
            sset = it % 2
            hs = slice(hd * 128, (hd + 1) * 128)
            if ti == tiles_full[0]:
                head_tables(j, hd)
            t0, n = TILES[ti]
            nb = n // 128
            inter = ti != CTX_TILE
            lq, lk, lv = LQ[sset], LK[sset], LV[sset]
            if inter:
                tt("dve", tb16[0][:, 0:n], lq[:, 0:n], xi_sb[:, 0, 0:n], ALU.mult, [("lq", sset), "xi"], [("tb16", 0)])
                tt("dve", tb16[1][:, 0:n], lq[:, 0:n], xi_sb[:, 1, 0:n], ALU.mult, [("lq", sset), "xi"], [("tb16", 1)])
            for jb in range(nb):
                mm(psA[jb][:, 0:n], lk[:, jb * 128:(jb + 1) * 128], lq[:, 0:n], True, True, [("lk", sset), ("lq", sset)], [PS(jb)])
                st0 = 384 - 128 * jb
                tt("dve", mix_sb[:, jb, 0:n], psA[jb][:, 0:n], dtab[:, st0:st0 + n], ALU.mult, [PS(jb), "dtab"], [("mixs", jb)])
            yb = 4 + it % 2
            nmm = nb + (2 if inter else 0)
            for jb in range(nb):
                mm(psA[yb][:, 0:n], lv[:, jb, :], mix_sb[:, jb, 0:n], jb == 0, jb == nmm - 1, [("lv", sset), ("mixs", jb)], [PS(yb)])
            if inter:
                mm(psA[yb][:, 0:n], o16[0][:, ti, hs], tb16[0][:, 0:n], False, False, [("o16", 0), ("tb16", 0)], [PS(yb)])
                mm(psA[yb][:, 0:n], o16[1][:, ti, hs], tb16[1][:, 0:n], False, True, [("o16", 1), ("tb16", 1)], [PS(yb)])

        def stage_b(it):
            hd, ti = its[it]
            sset = it % 2
            t0, n = TILES[ti]
            yb = 4 + it % 2
            lgt, rb = LG[sset], RB[sset]
            act(tb16[2][:, 0:n], psA[yb][:, 0:n], AF.Square, [PS(yb)], [("tb16", 2)])
            mm(psA[6][:, 0:n], ones, tb16[2][:, 0:n], True, True, [("tb16", 2), "cstb"], [PS(6)])
            rstd_from_ps(6, n, 1.0 / 128)
            tt("dve", t32[0][:, 0:n], psA[yb][:, 0:n], rs_sb[:, 0:n], ALU.mult, [PS(yb), "rs"], [("t32", 0)])
            tt("dve", rb[:, 0:n], t32[0][:, 0:n], lgt[:, 0:n], ALU.mult, [("t32", 0), ("lg_", sset)], [("rb", sset)])
            dma("sp", r_d[:, hd, t0:t0 + n], rb[:, 0:n], [("rb", sset)], [("r_d", ti, hd)], stkey())
            per = (124 + len(its) - 1) // len(its)
            for idx in range(it * per, min(124, (it + 1) * per)):
                ts("dve", dgf[:, idx * 128:(idx + 1) * 128], ident, C("convw", cwo + idx), None, ALU.mult, None, ["cstb", "cst"], HK)

        ret_loads(0)
        ret_load_g(0)
        if len(its) > 1:
            ret_loads(1)
            ret_load_g(1)
        stage_a(0)
        for it in range(len(its)):
            if it + 2 < len(its):
                ret_loads(it + 2)
            if it + 1 < len(its):
                stage_a(it + 1)
            stage_b(it)
            if it + 2 < len(its):
                ret_load_g(it + 2)
        for ti in tiles_full:
            t0, n = TILES[ti]
            lu, lr = l16[0], l16[1]
            lk0 = L0K
            if ti == CTX_TILE:
                dma("sp", lu[:, :, 0:n + 2 * HC], uc_d[:, :, :], [("u_d", ti), "uc_h0", "uc_h1"], lk0, ldkey())
            else:
                rk = [("u_d", ti)] + ([("u_d", ti - 1)] if ti > 0 else []) + ([("u_d", ti + 1)] if ti < 3 else [])
                dma("sp", lu[:, :, 0:n + 2 * HC], um_d[:, :, t0:t0 + n + 2 * HC], rk, lk0, ldkey())
            dma("sp", lr[:, :, 0:n], r_d[:, :, t0:t0 + n], [("r_d", ti, hd) for hd in range(4)], L1K, ldkey())
            for ch in range(4):
                for k in range(31):
                    idx = ch * 31 + k
                    mm(psA[ch][:, 0:n], dgf[:, idx * 128:(idx + 1) * 128], lu[:, ch, k:k + n], k == 0, k == 30, HK + lk0, [PS(ch)])
            for ch in range(4):
                act(tb16[ch % 2][:, 0:n], psA[ch][:, 0:n], AF.Copy, [PS(ch)], [("tb16", ch % 2)])
                mm(psA[4][:, 0:n], ones, tb16[ch % 2][:, 0:n], ch == 0, ch == 3, [("tb16", ch % 2), "cstb"], [PS(4)])
            for ch in range(4):
                act(tb16[ch % 2][:, 0:n], psA[ch][:, 0:n], AF.Square, [PS(ch)], [("tb16", ch % 2)])
                mm(psA[5][:, 0:n], ones, tb16[ch % 2][:, 0:n], ch == 0, ch == 3, [("tb16", ch % 2), "cstb"], [PS(5)])
            act(t32[0][:, 0:n], psA[4][:, 0:n], AF.Copy, [PS(4)], [("t32", 0)], scale=1.0 / 512)
            tt("dve", t32[1][:, 0:n], t32[0][:, 0:n], t32[0][:, 0:n], ALU.mult, [("t32", 0)], [("t32", 1)])
            stt("dve", t32[1][:, 0:n], psA[5][:, 0:n], 1.0 / 512, t32[1][:, 0:n], ALU.mult, ALU.subtract, [PS(5), ("t32", 1)], [("t32", 1)])
            act(rs_sb[:, 0:n], t32[1][:, 0:n], AF.Ln, [("t32", 1), "cst"], ["rs"], bias=epsc)
            act(rs_sb[:, 0:n], rs_sb[:, 0:n], AF.Exp, ["rs"], ["rs"], scale=-0.5)
            for ch in range(4):
                tt("dve", acc32[:, ch, 0:n], psA[ch][:, 0:n], t32[0][:, 0:n], ALU.subtract, [PS(ch), ("t32", 0)], [("acc", ch)])
                tt("dve", acc32[:, ch, 0:n], acc32[:, ch, 0:n], rs_sb[:, 0:n], ALU.mult, [("acc", ch), "rs"], [("acc", ch)])
                act(mix_sb[:, 4 + ch, 0:n], acc32[:, ch, 0:n], AF.Silu, [("acc", ch), "cst"], [("mixs", 4 + ch)],
                    bias=C("convlnb", j * 4 + ch), scale=C("convlnw", j * 4 + ch))
            wout_tile(ti, wo, s4, [lr[:, kc, :] for kc in range(4)] + [mix_sb[:, 4 + kc, :] for kc in range(4)],
                      [("lv", 0)] * 4 + [("mixs", 4 + kc) for kc in range(4)])

    def wout_tile(ti, wo, s, rhs_list, rkeys):
        t0, n = TILES[ti]
        s_ = tile_stream(ti)
        for oc in range(KC):
            bank = 4 + oc % 3
            for kc in range(KC):
                mm(psA[bank][:, 0:n], wo[:, kc, oc * 128:(oc + 1) * 128], rhs_list[kc][:, 0:n], kc == 0, kc == KC - 1,
                   [("W", s), rkeys[kc]], [PS(bank)])
            stt("dve", x_sb[:, oc, t0:t0 + n], psA[bank][:, 0:n], nv_sb[:, 2, oc, s_:s_ + 1], x_sb[:, oc, t0:t0 + n], ALU.mult, ALU.add,
                [PS(bank), "nv"], [("x", oc, ti)])

    def pool_window(ti, lp, lkeys):
        t0, n = TILES[ti]
        W_ = n + 2 * HP
        icn = "icc" if ti == CTX_TILE else "icm"
        for g, w in enumerate((2, 4, 8, 16)):
            left = w // 2
            right = w - 1 - left
            cur, oth = 0, 1
            tt("dve", t32[cur][:, 1:W_], lp[:, g, 1:W_], lp[:, g, 0:W_ - 1], ALU.add, lkeys, [("t32", cur)])
            k = 2
            lo = 1
            while k < w:
                tt("dve", t32[oth][:, lo + k:W_], t32[cur][:, lo + k:W_], t32[cur][:, lo:W_ - k], ALU.add, [("t32", cur)], [("t32", oth)])
                lo += k
                k *= 2
                cur, oth = oth, cur
            e0 = HP + right
            ts("dve", acc32[:, g, 0:n], t32[cur][:, e0:e0 + n], 1.0 / w, None, ALU.mult, None, [("t32", cur)], [("acc", g)])
            if ti == 0 or ti == CTX_TILE:
                tt("dve", acc32[:, g, 0:8], t32[cur][:, e0:e0 + 8], C(icn, g * 16, 8), ALU.mult, [("t32", cur), "cst"], [("acc", g)])
            if ti == 3 or ti == CTX_TILE:
                tt("dve", acc32[:, g, n - 8:n], t32[cur][:, e0 + n - 8:e0 + n], C(icn, g * 16 + 8, 8), ALU.mult, [("t32", cur), "cst"], [("acc", g)])
            tt("dve", mix_sb[:, g, 0:n], acc32[:, g, 0:n], lp[:, g, HP:HP + n], ALU.subtract, [("acc", g)] + lkeys, [("mixs", g)])

    def odd_mixer(li, tiles):
        j = li // 2
        norm_tile(tiles[0], 0)
        dma("pool", pwsg, tab_d[:, LAYT.o("pwsg") + j * 1024:LAYT.o("pwsg") + (j + 1) * 1024], [], ["S"], "tbl")
        acck = [("acc", i) for i in range(4)]
        dma("sp", accf[:, 0:1536], tab_d[:, LAYT.o("sgtab") + j * 1536:LAYT.o("sgtab") + (j + 1) * 1536], [], acck, ldkey())
        lnw = accf[:, 0:512]
        lnb = accf[:, 512:1024]
        s, (wpc,) = wload([odin_d[j, :, :, 0:512]])
        for i_, ti in enumerate(tiles):
            if i_ + 1 < len(tiles):
                norm_tile(tiles[i_ + 1], 0)
            t0, n = TILES[ti]
            pb = o16[ti % 2]
            for g in range(4):
                bank = g % 4
                proj_fm(bank, wpc, s, g * 128, ti)
                act(pb[:, g, 0:n], psA[bank][:, 0:n], AF.Copy, [PS(bank)], [("o16", ti % 2)])
            if ti == CTX_TILE:
                dma("sp", pc_d[:, :, HP:HP + n], pb[:, :, 0:n], [("o16", ti % 2)], [("p_d", ti)], stkey())
            else:
                dma("sp", pm_d[:, :, HP + t0:HP + t0 + n], pb[:, :, 0:n], [("o16", ti % 2)], [("p_d", ti)], stkey())
            if ti == 0:
                cp("dve", xh_sb[:, 0:4 * HP].rearrange("p (a b) -> p a b", a=4), pb[:, :, 0:HP], [("o16", ti % 2)], ["xh"])
            if ti == 3:
                cp("dve", xh_sb[:, 4 * HP:8 * HP].rearrange("p (a b) -> p a b", a=4), pb[:, :, n - HP:n], [("o16", ti % 2)], ["xh"])
        s, (wu_, wv_) = wload([odin_d[j, :, :, 512:1024], odin_d[j, :, :, 1024:1536]])
        dma("pool", xio_d.ap(), xh_sb[:, 0:XWO], ["xh"], ["xio_d"], "xst0")
        allgather(xio_d, xoo_d, ["xio_d"], "xoo_d")
        dma("pool", rs_sb[:, 0:4 * HP], xoo_d.ap()[0:128, 4 * HP:8 * HP], ["xoo_d"], ["rs"], "xld0")
        dma("pool", rs_sb[:, 4 * HP:8 * HP], xoo_d.ap()[128:256, 0:4 * HP], ["xoo_d"], ["rs"], "xld1")
        hl = tb16[2][:, 0:8 * HP].rearrange("p (a b) -> p a b", a=4)
        ts("dve", hl[:, :, 0:HP], rs_sb[:, 0:4 * HP].rearrange("p (a b) -> p a b", a=4), fB, None, ALU.mult, None, ["rs", "cst"], [("tb16", 2)])
        ts("dve", hl[:, :, HP:2 * HP], rs_sb[:, 4 * HP:8 * HP].rearrange("p (a b) -> p a b", a=4), fA, None, ALU.mult, None, ["rs", "cst"], [("tb16", 2)])
        dma("sp", pm_d[:, :, 0:HP], hl[:, :, 0:HP], [("tb16", 2)], [("p_d", 0)], stkey())
        dma("sp", pm_d[:, :, HP + TM:], hl[:, :, HP:2 * HP], [("tb16", 2)], [("p_d", 3)], stkey())
        for ti in tiles:
            t0, n = TILES[ti]
            nb = n // 128
            ub = o16[0]
            sb_ = o16[1]
            for g in range(4):
                bank = g % 2
                proj_fm(bank, wu_, s, g * 128, ti)
                act(ub[:, g, 0:n], psA[bank][:, 0:n], AF.Gelu, [PS(bank)], [("o16", 0)])
            def S1(jb, ti=ti, t0=t0):
                bank = 2 + jb % 2
                for kc in range(KC):
                    mm(psA[bank][:, :], h_sb[:, kc, t0 + jb * 128:t0 + (jb + 1) * 128], wv_[:, kc, :], kc == 0, kc == KC - 1,
                       [("W", s), ("h", ti)], [PS(bank)])
                z = t32[jb % 2]
                zk = ("t32", jb % 2)
                pb_ = jb % 2
                sq = l16x[:, pb_, :]
                sqk = ("sqx", pb_)
                smo = 4 * pb_
                smk = ("sm", pb_)
                act(z[:, 0:512], psA[bank][:, :], AF.Gelu, [PS(bank)], [zk])
                act(sq, z[:, 0:512], AF.Square, [zk], [sqk])
            def S2(jb, ti=ti, t0=t0):
                z = t32[jb % 2]
                zk = ("t32", jb % 2)
                pb_ = jb % 2
                sq = l16x[:, pb_, :]
                sqk = ("sqx", pb_)
                smo = 4 * pb_
                smk = ("sm", pb_)
                pg.add("dve", lambda h, z=z, smo=smo: h.reduce_sum(out=sm[:, smo:smo + 1], in_=z[:, 0:512], axis=mybir.AxisListType.X), [zk], [smk], size=1)
                pg.add("dve", lambda h, sq=sq, smo=smo: h.reduce_sum(out=sm[:, smo + 1:smo + 2], in_=sq, axis=mybir.AxisListType.X), [sqk], [smk], size=1)
                ts("dve", sm[:, smo + 2:smo + 3], sm[:, smo:smo + 1], 1.0 / 512, None, ALU.mult, None, [smk], [smk])
                tt("dve", sm[:, smo + 3:smo + 4], sm[:, smo + 2:smo + 3], sm[:, smo + 2:smo + 3], ALU.mult, [smk], [smk])
                stt("dve", sm[:, smo + 3:smo + 4], sm[:, smo + 1:smo + 2], 1.0 / 512, sm[:, smo + 3:smo + 4], ALU.mult, ALU.subtract, [smk], [smk])
                act(sm[:, smo + 3:smo + 4], sm[:, smo + 3:smo + 4], AF.Sqrt, [smk, "cst"], [smk], bias=epsc)
                recip(sm[:, smo + 3:smo + 4], smk)
                ts("dve", z[:, 0:512], z[:, 0:512], sm[:, smo + 2:smo + 3], sm[:, smo + 3:smo + 4], ALU.subtract, ALU.mult, [smk, zk], [zk])
                tt("dve", z[:, 0:512], z[:, 0:512], lnw, ALU.mult, [zk] + acck, [zk])
                vb = tb16[jb % 2]
                tt("dve", vb[:, :], z[:, 0:512], lnb, ALU.add, [zk] + acck, [("tb16", jb % 2)])
                bank2 = 4 + jb % 2
                for g in range(4):
                    mm(psA[bank2][:, g * 128:(g + 1) * 128], vb[:, g * 128:(g + 1) * 128], pwsg[:, 512 + g * 128:512 + (g + 1) * 128], True, True,
                       [("tb16", jb % 2), "S"], [PS(bank2)])
                tt("dve", t32[2][:, 0:512], psA[bank2][:, 0:512], accf[:, 1024:1536], ALU.add, [PS(bank2)] + acck, [("t32", 2)])
                tt("dve", sb_[:, :, jb * 128:(jb + 1) * 128], t32[2][:, 0:512].rearrange("p (g q) -> p g q", g=4),
                   ub[:, :, jb * 128:(jb + 1) * 128], ALU.mult, [("t32", 2), ("o16", 0)], [("o16", 1)])
            S1(0)
            for jb in range(nb):
                if jb + 1 < nb:
                    S1(jb + 1)
                S2(jb)
            dma("sp", r_d[:, :, t0:t0 + n], sb_[:, :, 0:n], [("o16", 1)], [("r_d", ti)], stkey())
        s, (wo,) = wload([odout_d[j, :, :, :]])
        for ti in tiles:
            t0, n = TILES[ti]
            lp, lr = l16[0], l16[1]
            lk0 = [("lq", 0), ("lk", 0), ("lg_", 0), ("rb", 0)]
            if ti == CTX_TILE:
                dma("sp", lp[:, :, 0:n + 2 * HP], pc_d[:, :, :], [("p_d", ti), "pc_h0", "pc_h1"], lk0, ldkey())
            else:
                rk = [("p_d", ti)] + ([("p_d", ti - 1)] if ti > 0 else []) + ([("p_d", ti + 1)] if ti < 3 else [])
                dma("sp", lp[:, :, 0:n + 2 * HP], pm_d[:, :, t0:t0 + n + 2 * HP], rk, lk0, ldkey())
            dma("sp", lr[:, :, 0:n], r_d[:, :, t0:t0 + n], [("r_d", ti)], [("lv", 0), ("lv", 1)], ldkey())
            pool_window(ti, lp, lk0)
            for g in range(4):
                bank = g % 2
                mm(psA[bank][:, 0:n], pwsg[:, g * 128:(g + 1) * 128], mix_sb[:, g, 0:n], True, True, [("mixs", g), "S"], [PS(bank)])
                ts("dve", mix_sb[:, 4 + g, 0:n], psA[bank][:, 0:n], C("poolsc", j * 4 + g), None, ALU.mult, None,
                   [PS(bank), "cst"], [("mixs", 4 + g)])
            if ti == 0:
                dbg_dump("mix", mix_sb[:, :, :], [("mixs", i) for i in range(8)])
                dbg_dump("lp", lp[:, :, :], lk0)
                dbg_dump("lr", lr[:, :, :], [("l16", 1)])
            wout_tile(ti, wo, s, [mix_sb[:, 4 + kc, :] for kc in range(4)] + [lr[:, kc, :] for kc in range(4)],
                      [("mixs", 4 + kc) for kc in range(4)] + [("lv", 0)] * 4)

    def ffn(li, tiles):
        norm_tile(tiles[0], 1)
        for f0 in range(0, NFF, 2):
            s, (wgv, wuv, wdv) = wload([wg_d[li, :, :, f0 * 128:(f0 + 2) * 128], wu_d[li, :, :, f0 * 128:(f0 + 2) * 128],
                                        wd_d[li, :, f0:f0 + 2, :]])
            for i_, ti in enumerate(tiles):
                if f0 == 0 and i_ + 1 < len(tiles):
                    norm_tile(tiles[i_ + 1], 1)
                t0, n = TILES[ti]
                s_ = tile_stream(ti)
                hsel = (f0 // 2 + ti) % 2
                for f in range(2):
                    bg, bu = 2 * f, 2 * f + 1
                    col = f * 128
                    for kc in range(KC):
                        mm(psA[bg][:, 0:n], wgv[:, kc, col:col + 128], h_sb[:, kc, t0:t0 + n], kc == 0, kc == KC - 1, [("W", s), ("h", ti)], [PS(bg)])
                    for kc in range(KC):
                        mm(psA[bu][:, 0:n], wuv[:, kc, col:col + 128], h_sb[:, kc, t0:t0 + n], kc == 0, kc == KC - 1, [("W", s), ("h", ti)], [PS(bu)])
                    act(t32[f][:, 0:n], psA[bg][:, 0:n], AF.Silu, [PS(bg)], [("t32", f)])
                    tt("dve", hid_sb[:, 2 * hsel + f, 0:n], psA[bu][:, 0:n], t32[f][:, 0:n], ALU.mult, [PS(bu), ("t32", f)], [("mixs", 4 + 2 * hsel + f)])
                for oc in range(KC):
                    bank = 4 + oc % 3
                    for fc in range(2):
                        mm(psA[bank][:, 0:n], wdv[:, fc, oc * 128:(oc + 1) * 128], hid_sb[:, 2 * hsel + fc, 0:n], fc == 0, fc == 1,
                           [("W", s), ("mixs", 4 + 2 * hsel + fc)], [PS(bank)])
                    stt("dve", x_sb[:, oc, t0:t0 + n], psA[bank][:, 0:n], nv_sb[:, 5, oc, s_:s_ + 1], x_sb[:, oc, t0:t0 + n], ALU.mult, ALU.add,
                        [PS(bank), "nv"], [("x", oc, ti)])

    mod_prologue()
    for li in range(n_layers):
        mod_layer(li)
        ctx_after = any(m % 2 == 0 for m in range(li + 1, 4))
        tl = [0, 1, 2, 3] + ([CTX_TILE] if ctx_after else [])
        if li % 2 == 0:
            even_mixer(li, ctx_after)
        else:
            odd_mixer(li, tl)
        ffn(li, tl)

    okeys = []
    for ti in range(4):
        t0, n = TILES[ti]
        if debug_x:
            for c in range(KC):
                dma("sp", out_d[:, c, t0:t0 + n], x_sb[:, c, t0:t0 + n], [("x", c, ti)], [("out", ti, c)], "out%d" % (c % 4))
        else:
            final_tile(ti)
        okeys += [("out", ti, c) for c in range(KC)]
    pg.add("sp", lambda h: h.nop(), okeys + dbg_list, [])

    pg.finalize()
    sems = {}
    for e in ENGS:
        sems[("eng", e)] = E(nc.semaphore("s_" + e))
    for i, k in enumerate(pg.dma_keys):
        sems[("dma", k)] = E(nc.semaphore("d%d" % i))
    block = E(nc.Block())

    @block.tensor
    def _(h):
        pg.emit(sems, "pe", h)

    @block.scalar
    def _(h):
        pg.emit(sems, "act", h)

    @block.vector
    def _(h):
        pg.emit(sems, "dve", h)

    @block.gpsimd
    def _(h):
        pg.emit(sems, "pool", h)

    @block.sync
    def _(h):
        pg.emit(sems, "sp", h)

    es.close()
    return nc


_NC_CACHE = {}


def prep_inputs(inp):
    x = np.asarray(inp["x"], np.float32)
    ctx = np.asarray(inp["ctx"], np.float32)
    shared = {
        "even_w_in": np.stack([wlay(inp["even_w_in"][l]) for l in range(2)]),
        "even_w_out": np.stack([wlay(inp["even_w_out"][l]) for l in range(2)]),
        "odd_w_in": np.stack([wlay(inp["odd_w_in"][l]) for l in range(2)]),
        "odd_w_out": np.stack([wlay(inp["odd_w_out"][l]) for l in range(2)]),
        "ffn_w_gate": np.stack([wlay(inp["ffn_w_gate"][l]) for l in range(4)]),
        "ffn_w_up": np.stack([wlay(inp["ffn_w_up"][l]) for l in range(4)]),
        "ffn_w_down": np.stack([wlay(inp["ffn_w_down"][l]) for l in range(4)]),
    }
    in_maps = []
    for core in range(NCORES):
        b, half = core // 2, core % 2
        xm = x[b, half * TM:(half + 1) * TM, :]
        xa = np.concatenate([xm, ctx[b]], axis=0)
        xt = np.ascontiguousarray(xa.T.reshape(KC, 128, TT).transpose(1, 0, 2))
        cst, tab, cb = host_consts(core, inp)
        m = {"xin": xt, "cst": cst, "tab": tab, "cstb": cb,
             "ada_w": np.stack([wlay(np.asarray(inp["ada_w"][l], np.float32)[:, half * 3072:(half + 1) * 3072]) for l in range(4)])}
        m.update(shared)
        in_maps.append(m)
    return in_maps


def assemble(results):
    out = np.zeros((4, SEQ, D), np.float32)
    for core in range(NCORES):
        b, half = core // 2, core % 2
        o = np.asarray(results[core]["out"], np.float32)
        out[b, half * TM:(half + 1) * TM, :] = o.transpose(1, 0, 2).reshape(D, TM).T
    return out


def kernel(**inputs):
    key = "full"
    if key not in _NC_CACHE:
        _NC_CACHE[key] = build_program()
    nc = _NC_CACHE[key]
    in_maps = prep_inputs(inputs)
    res = run_bass_kernel_spmd(nc, in_maps, core_ids=list(range(NCORES)))
    return assemble(res.results)
```

```python
import numpy as np
from contextlib import ExitStack
import concourse.bass as bass
import concourse.mybir as mybir
from concourse.bass_utils import run_bass_kernel_spmd

F32 = mybir.dt.float32
BF16 = mybir.dt.bfloat16
AF = mybir.ActivationFunctionType
ALU = mybir.AluOpType

ENGS = ("pe", "act", "dve", "pool", "sp")
NCORES = 8
DBG_DUMPS = False
D = 1024
KC = 8
TM = 2048
TCX = 256
TT = TM + TCX
SEQ = 4096
DFF = 2816
NFF = DFF // 128
EPS = 1e-6
BIG = 1.0e6
TILES = [(0, 512), (512, 512), (1024, 512), (1536, 512), (2048, 256)]
CTX_TILE = 4
HC = 15
HP = 8
XW = 1024 + 4 * 2 * HC
XWO = 4 * 2 * HP


class Op:
    __slots__ = ("eng", "fn", "deps", "sig", "done", "is_dma", "inc", "size", "idx")

    def __init__(self, eng, fn):
        self.eng = eng
        self.fn = fn
        self.deps = []
        self.sig = False
        self.done = None
        self.is_dma = None
        self.inc = 1
        self.size = 1 << 20
        self.idx = 0


class Prog:
    def __init__(self):
        self.ops = {e: [] for e in ENGS}
        self.last_w = {}
        self.readers = {}
        self.dma_keys = []

    def add(self, eng, fn, reads=(), writes=(), dma=None, inc=None, size=None):
        op = Op(eng, fn)
        op.idx = len(self.ops[eng])
        if size is not None:
            op.size = size
        if dma is not None:
            op.is_dma = dma
            op.inc = 16 if inc is None else inc
            if dma not in self.dma_keys:
                self.dma_keys.append(dma)
        xk = [k for k in reads if isinstance(k, tuple) and k[0] == "ps"]
        if xk:
            reads = [k for k in reads if k not in xk]
            writes = list(writes) + xk
        deps = []
        for k in reads:
            lw = self.last_w.get(k)
            if lw is not None:
                deps.append(lw)
        for k in writes:
            lw = self.last_w.get(k)
            if lw is not None:
                deps.append(lw)
            deps.extend(self.readers.get(k, ()))
        seen = set()
        for d in deps:
            if id(d) in seen or d is op:
                continue
            seen.add(id(d))
            if d.eng == eng and d.is_dma is None and dma is None:
                if eng == "pe" or d.size >= 512 or op.idx - d.idx > 6:
                    continue
            op.deps.append(d)
            d.sig = True
        for k in reads:
            lst = self.readers.setdefault(k, [])
            if dma is None:
                for i_, r_ in enumerate(lst):
                    if r_.is_dma is None and r_.eng == eng:
                        lst.pop(i_)
                        break
            lst.append(op)
        for k in writes:
            self.last_w[k] = op
            self.readers[k] = []
        self.ops[eng].append(op)
        return op

    def finalize(self):
        cnt = {}
        for e in ENGS:
            for op in self.ops[e]:
                if op.is_dma is not None:
                    key = ("dma", op.is_dma)
                    cnt[key] = cnt.get(key, 0) + op.inc
                    op.done = (key, cnt[key])
        for e in ENGS:
            c = 0
            for op in self.ops[e]:
                if op.is_dma is None and op.sig:
                    c += 1
                    op.done = (("eng", e), c)

    def emit(self, sems, e, h):
        waited = {}
        for op in self.ops[e]:
            need = {}
            for d in op.deps:
                k, v = d.done
                if waited.get(k, 0) >= v:
                    continue
                if need.get(k, 0) < v:
                    need[k] = v
            for k, v in need.items():
                h.wait_ge(sems[k], v)
                waited[k] = v
            ins = op.fn(h)
            if op.is_dma is not None:
                ins.then_inc(sems[op.done[0]], op.inc)
            elif op.sig:
                ins.then_inc(sems[op.done[0]], 1)


class Lay:
    def __init__(self):
        self.off = {}
        self.n = 0

    def add(self, name, w):
        self.off[name] = self.n
        self.n += w

    def o(self, name):
        return self.off[name]


def make_layouts():
    L = Lay()
    L.add("flags", 4)
    L.add("zidx_m", 8)
    L.add("zidx_c", 4)
    L.add("cfidx", 8)
    L.add("logit", 16)
    L.add("svin", 16)
    L.add("adab", 4 * 48)
    L.add("normw", 4 * 16)
    L.add("fnw", 8)
    L.add("convw", 2 * 4 * 31)
    L.add("convlnw", 8)
    L.add("convlnb", 8)
    L.add("poolsc", 8)
    L.add("icm", 64)
    L.add("icc", 64)
    L.add("eps", 1)
    L.add("one", 1)
    L.add("lns", 1)
    LT = Lay()
    LT.add("ropeC", TT)
    LT.add("ropeS", TT)
    LT.add("a1idx", 896)
    LT.add("a2idx", 896)
    LT.add("xifidx", 512)
    LT.add("xibidx", 512)
    LT.add("sgtab", 2 * 1536)
    LT.add("pwsg", 2 * 1024)
    return L, LT


LAY, LAYT = make_layouts()
NCB = 384


def fm(v):
    v = np.asarray(v, np.float32)
    return np.ascontiguousarray(v.reshape(-1, 128).T)


def host_consts(core, inp):
    b, half = core // 2, core % 2
    L, LT = LAY, LAYT
    cst = np.zeros((128, L.n), np.float32)
    tab = np.zeros((128, LT.n), np.float32)

    def put(arr, lay, name, val):
        val = np.asarray(val, np.float32)
        o = lay.o(name)
        arr[:, o:o + val.shape[1]] = val

    pairs = (16, 24, 24)

    def angles(p_seq, p_row, p_col):
        parts = []
        for p_, n in zip((p_seq, p_row, p_col), pairs):
            freq = (np.float32(10000.0) ** (-np.arange(n, dtype=np.float32) / np.float32(n))).astype(np.float32)
            parts.append(p_[:, None].astype(np.float32) * freq[None, :])
        return np.concatenate(parts, axis=-1)

    t = np.arange(half * TM, (half + 1) * TM)
    ang_x = angles(np.full((TM,), float(TCX), np.float32), (t // 64).astype(np.float32), (t % 64).astype(np.float32))
    zc = np.zeros((TCX,), np.float32)
    ang_c = angles(np.arange(TCX, dtype=np.float32), zc, zc)
    ang = np.concatenate([ang_x, ang_c], axis=0).astype(np.float32)
    cos = np.cos(ang).T.astype(np.float32)
    sin = np.sin(ang).T.astype(np.float32)
    put(tab, LT, "ropeC", np.concatenate([cos, cos], axis=0))
    put(tab, LT, "ropeS", np.concatenate([-sin, sin], axis=0))
    p = np.arange(128)[:, None]
    i = np.arange(896)[None, :]
    dlt = i - p - 384
    put(tab, LT, "a1idx", np.where(dlt >= 0, dlt, BIG))
    put(tab, LT, "a2idx", np.where(dlt < 0, -dlt - 1, BIG))
    c = np.arange(512, dtype=np.float32)[None, :] + np.zeros((128, 1), np.float32)
    put(tab, LT, "xifidx", c + 1.0)
    put(tab, LT, "xibidx", 511.0 - c)
    sgt = np.zeros((128, 2 * 1536), np.float32)
    pws = np.zeros((128, 2 * 1024), np.float32)
    pw = np.asarray(inp["pool_w"], np.float32)
    sw = np.asarray(inp["sg_w"], np.float32)
    for l in range(2):
        sgt[:, l * 1536:l * 1536 + 512] = np.asarray(inp["sg_ln_w"][l], np.float32)[None, :]
        sgt[:, l * 1536 + 512:l * 1536 + 1024] = np.asarray(inp["sg_ln_b"][l], np.float32)[None, :]
        sgt[:, l * 1536 + 1024:l * 1536 + 1536] = np.asarray(inp["sg_b"][l], np.float32).reshape(1, 512)
        pws[:, l * 1024:l * 1024 + 512] = pw[l].transpose(1, 0, 2).reshape(128, 512)
        pws[:, l * 1024 + 512:l * 1024 + 1024] = sw[l].transpose(2, 0, 1).reshape(128, 512)
    put(tab, LT, "sgtab", sgt)
    put(tab, LT, "pwsg", pws)

    put(cst, L, "flags", np.tile(np.array([1.0 - half, float(half), 0, 0], np.float32)[None, :], (128, 1)))
    j = np.arange(4)[None, :]
    m = 128 * j + p
    put(cst, L, "zidx_m", np.concatenate([511.0 - m, m], axis=1))
    j2 = np.arange(2)[None, :]
    m2 = 128 * j2 + p
    put(cst, L, "zidx_c", np.concatenate([255.0 - m2, m2], axis=1))
    ti = np.arange(4, dtype=np.float32)
    put(cst, L, "cfidx", np.tile(np.concatenate([512.0 * ti, 512.0 * (3 - ti)])[None, :], (128, 1)))
    put(cst, L, "logit", np.tile(np.asarray(inp["ret_decay_logit"], np.float32).reshape(1, 16), (128, 1)))
    svin = np.zeros((128, 8, 2), np.float32)
    svin[:, :, 0] = fm(inp["c"][b])
    svin[:, :, 1] = fm(inp["c_ctx"])
    put(cst, L, "svin", svin.reshape(128, 16))
    put(cst, L, "adab", np.concatenate([fm(inp["ada_b"][l]) for l in range(4)], axis=1))
    put(cst, L, "normw", np.concatenate([fm(inp["norm_w"][l, s]) for l in range(4) for s in range(2)], axis=1))
    put(cst, L, "fnw", fm(inp["final_norm_w"]))
    cw = np.asarray(inp["conv_dw_w"], np.float32)
    put(cst, L, "convw", np.concatenate([cw[l][:, ch * 128:(ch + 1) * 128].T for l in range(2) for ch in range(4)], axis=1))
    put(cst, L, "convlnw", np.concatenate([fm(inp["conv_ln_w"][l]) for l in range(2)], axis=1))
    put(cst, L, "convlnb", np.concatenate([fm(inp["conv_ln_b"][l]) for l in range(2)], axis=1))
    put(cst, L, "poolsc", np.concatenate([fm(inp["pool_scale"][l]) for l in range(2)], axis=1))

    def invcnt(lo_edge, hi_edge, n):
        out = np.zeros((4, 16), np.float32)
        for gi, w in enumerate((2, 4, 8, 16)):
            left = w // 2
            right = w - 1 - left
            for k in range(8):
                cl = min(left, k) if lo_edge else left
                out[gi, k] = 1.0 / (cl + right + 1)
                tpos = n - 8 + k
                cr = min(right, n - 1 - tpos) if hi_edge else right
                out[gi, 8 + k] = 1.0 / (left + cr + 1)
        return np.tile(out.reshape(1, 64), (128, 1))
    put(cst, L, "icm", invcnt(half == 0, half == 1, TM))
    put(cst, L, "icc", invcnt(True, True, TCX))
    cst[:, L.o("eps")] = EPS
    cst[:, L.o("one")] = 1.0
    cst[:, L.o("lns")] = np.log(128.0 ** -0.5)
    cb = np.zeros((128, NCB), np.float32)
    cb[:, 0:128] = 1.0
    cb[:, 128:256] = np.eye(128, dtype=np.float32)
    for q in range(128):
        cb[(q + 64) % 128, 256 + q] = 1.0
    return cst, tab, cb


def wlay(w):
    w = np.asarray(w, np.float32)
    k, n = w.shape
    return np.ascontiguousarray(w.reshape(k // 128, 128, n).transpose(1, 0, 2))


def build_program(n_layers=4, debug_x=False):
    nc = bass.Bass("TRN2", target_bir_lowering=False)
    pg = Prog()
    es = ExitStack()
    E = es.enter_context

    def din(name, shape):
        return nc.dram_tensor(name, list(shape), F32, kind="ExternalInput").ap()

    xin = din("xin", [128, KC, TT])
    cst_d = din("cst", [128, LAY.n])
    tab_d = din("tab", [128, LAYT.n])
    cstb_d = din("cstb", [128, NCB])
    ada_d = din("ada_w", [4, 128, KC, 3072])
    evin_d = din("even_w_in", [2, 128, KC, 3072])
    evout_d = din("even_w_out", [2, 128, KC, D])
    odin_d = din("odd_w_in", [2, 128, KC, 1536])
    odout_d = din("odd_w_out", [2, 128, KC, D])
    wg_d = din("ffn_w_gate", [4, 128, KC, DFF])
    wu_d = din("ffn_w_up", [4, 128, KC, DFF])
    wd_d = din("ffn_w_down", [4, 128, NFF, D])
    out_d = nc.dram_tensor("out", [128, KC, TM], F32, kind="ExternalOutput").ap()

    def scratch(name, shape, dt=BF16):
        return nc.dram_tensor(name, list(shape), dt).ap()

    q_d = scratch("q_d", [128, 4, TT])
    k_d = scratch("k_d", [128, 4, TT])
    v_d = scratch("v_d", [128, TT // 128, 512])
    g_d = scratch("g_d", [128, 4, TT])
    um_d = scratch("um_d", [128, 4, TM + 2 * HC])
    uc_d = scratch("uc_d", [128, 4, TCX + 2 * HC])
    r_d = scratch("r_d", [128, 4, TT])
    pm_d = scratch("pm_d", [128, 4, TM + 2 * HP])
    pc_d = scratch("pc_d", [128, 4, TCX + 2 * HP])
    xi_d = nc.dram_tensor("xi_d", [128, XW], F32)
    xo_d = nc.dram_tensor("xo_d", [256, XW], F32)
    xio_d = nc.dram_tensor("xio_d", [128, XWO], F32)
    xoo_d = nc.dram_tensor("xoo_d", [256, XWO], F32)
    xm_d = nc.dram_tensor("xm_d", [128, 192], F32)
    xmo_d = nc.dram_tensor("xmo_d", [256, 192], F32)

    def sb(name, shape, dt=F32):
        return E(nc.sbuf_tensor(name, list(shape), dt))

    x_sb = sb("x_sb", [128, KC, TT])
    h_sb = sb("h_sb", [128, KC, TT], BF16)
    cst = sb("cst_sb", [128, LAY.n])
    cstb = sb("cstb_sb", [128, NCB], BF16)
    WSLOT = 8192
    w_sb = [sb("w_sb%d" % i, [128, WSLOT], BF16) for i in range(2)]
    modall = sb("modall", [128, 4, 2, 48])
    modx = sb("modx", [128, 192])
    modg = sb("modg", [128, 2, 192])
    nv_sb = sb("nv_sb", [128, 6, KC, 2])
    sv_sb = sb("sv_sb", [128, KC, 2], BF16)
    rs_sb = sb("rstd", [128, 512])
    TW = 512 + 2 * HP
    t32 = [sb("t32_%d" % i, [128, TW]) for i in range(3)]
    tb16 = [sb("tb16_%d" % i, [128, 512], BF16) for i in range(3)]
    o16 = [sb("o16_%d" % i, [128, 4, 512], BF16) for i in range(2)]
    l16 = [sb("l16_%d" % i, [128, 4, 512 + 2 * HC], BF16) for i in range(2)]
    l16x = sb("l16x", [128, 4, 512], BF16)
    acc32 = sb("acc32", [128, 4, 512])
    accf = acc32[:].rearrange("p a b -> p (a b)")
    mix_sb = sb("mix", [128, 8, 512], BF16)
    ropeT = mix_sb[:, 0:4, :].rearrange("p a b -> p (a b)").bitcast(F32)
    hid_sb = mix_sb[:, 4:8, :]
    lg_sb = sb("lg", [128, 16])
    dtab = sb("dtab", [128, 896], BF16)
    xi_sb = sb("xi", [128, 2, 512], BF16)
    zm_sb = sb("zm", [128, 2, 4, 4])
    zc_sb = sb("zc", [128, 2, 4, 2])
    dec_sb = sb("dec", [128, 2, 4])
    cf_sb = sb("cf", [128, 2, 4, 4])
    kz_sb = [sb("kz%d" % i, [128, 2, 128], BF16) for i in range(2)]
    S_sb = sb("S", [128, 2, 512])
    Sflat = S_sb[:].rearrange("p a b -> p (a b)")
    pwsg = Sflat.bitcast(BF16)[:, 0:1024]
    xh_sb = sb("xh", [128, 8 * HC])
    sm = sb("small", [128, 8])

    psA = [E(nc.psum_tensor("psA%d" % i, [128, 512], F32)) for i in range(7)]
    psT = E(nc.psum_tensor("psT", [128, 1024], BF16))
    PST = 7

    def C(name, a=0, n=1):
        o = LAY.o(name) + a
        return cst[:, o:o + n]

    ones = cstb[:, 0:128]
    ident = cstb[:, 128:256]
    perm = cstb[:, 256:384]
    epsc = C("eps")
    fA = C("flags", 0)
    fB = C("flags", 1)

    def PS(i):
        return ("ps", i)

    def fsz(ap):
        n = 1
        for v in list(ap.shape)[1:]:
            n *= int(v)
        return n

    def mm(out, lhsT, rhs, start, stop, reads, writes):
        pg.add("pe", lambda h: h.matmul(out, lhsT=lhsT, rhs=rhs, start=start, stop=stop), reads=reads, writes=writes)

    def act(out, in_, func, reads, writes, bias=None, scale=None, accum=None):
        kw = {}
        if bias is not None:
            kw["bias"] = bias
        if scale is not None:
            kw["scale"] = scale
        if accum is not None:
            kw["accum_out"] = accum
        pg.add("act", lambda h: h.activation(out=out, in_=in_, func=func, **kw), reads=reads, writes=writes, size=fsz(out))

    def tt(eng, out, in0, in1, op, reads, writes):
        pg.add(eng, lambda h: h.tensor_tensor(out=out, in0=in0, in1=in1, op=op), reads=reads, writes=writes, size=fsz(out))

    def ts(eng, out, in0, s1, s2, op0, op1, reads, writes):
        if s2 is None:
            pg.add(eng, lambda h: h.tensor_scalar(out=out, in0=in0, scalar1=s1, scalar2=None, op0=op0), reads=reads, writes=writes, size=fsz(out))
        else:
            pg.add(eng, lambda h: h.tensor_scalar(out=out, in0=in0, scalar1=s1, scalar2=s2, op0=op0, op1=op1), reads=reads, writes=writes, size=fsz(out))

    def stt(eng, out, in0, scalar, in1, op0, op1, reads, writes):
        pg.add(eng, lambda h: h.scalar_tensor_tensor(out=out, in0=in0, scalar=scalar, in1=in1, op0=op0, op1=op1), reads=reads, writes=writes, size=fsz(out))

    def cp(eng, out, in_, reads, writes):
        pg.add(eng, lambda h: h.tensor_copy(out=out, in_=in_), reads=reads, writes=writes, size=fsz(out))

    def recip(ap, key):
        pg.add("dve", lambda h: h.reciprocal(out=ap, in_=ap), [key], [key], size=fsz(ap))

    dma_prev = {}

    def dma(eng, out, in_, reads, writes, key):
        op = pg.add(eng, lambda h: h.dma_start(out=out, in_=in_), reads=reads, writes=writes, dma=key)
        prev = dma_prev.get(key)
        if prev is not None and prev not in op.deps:
            op.deps.append(prev)
            prev.sig = True
        dma_prev[key] = op
        return op

    dbg_list = []

    def dbg_dump(name, ap, keys):
        if not (debug_x and DBG_DUMPS):
            return
        shp = list(ap.shape)
        dt = ap.dtype
        dd = nc.dram_tensor("dbg_" + name, shp, dt, kind="ExternalOutput").ap()
        idx = tuple(slice(None) for _ in shp)
        dma("sp", dd[idx], ap, keys, ["dbgk_" + name], "dbg%d" % (len(dbg_list) % 4))
        dbg_list.append("dbgk_" + name)

    stc = {"n": 0}

    def stkey():
        stc["n"] += 1
        return "st%d" % (stc["n"] % 4)

    ldc = {"n": 0}

    def ldkey():
        ldc["n"] += 1
        return "ld%d" % (ldc["n"] % 4)

    dma("sp", cst[:], cst_d[:, :], [], ["cst"], "cst")
    dma("pool", cstb[:], cstb_d[:, :], [], ["cstb"], "cstb")
    for c in range(KC):
        dma("sp", x_sb[:, c, :], xin[:, c, :], [], [("x", c, t) for t in range(5)], ("xin", c % 4))
    pg.add("dve", lambda h: h.memset(o16[0][:, :, 0:16], 0.0), [], [("o16", 0)])
    dma("sp", uc_d[:, :, 0:HC], o16[0][:, :, 0:HC], [("o16", 0)], ["uc_h0"], stkey())
    dma("sp", uc_d[:, :, HC + TCX:], o16[0][:, :, 0:HC], [("o16", 0)], ["uc_h1"], stkey())
    dma("sp", pc_d[:, :, 0:HP], o16[0][:, :, 0:HP], [("o16", 0)], ["pc_h0"], stkey())
    dma("sp", pc_d[:, :, HP + TCX:], o16[0][:, :, 0:HP], [("o16", 0)], ["pc_h1"], stkey())
    act(sv_sb[:].rearrange("p a b -> p (a b)"), C("svin", 0, 16), AF.Silu, ["cst"], ["sv"])
    act(lg_sb[:], C("logit", 0, 16), AF.Exp, ["cst"], ["lg"], scale=-1.0)
    act(lg_sb[:], lg_sb[:], AF.Ln, ["lg", "cst"], ["lg"], bias=C("one"))
    ts("dve", lg_sb[:], lg_sb[:], -1.0, None, ALU.mult, None, ["lg"], ["lg"])

    wstate = {"n": 0}

    def wload(parts, slot=None):
        s = wstate["n"] % 2 if slot is None else slot
        wstate["n"] = s + 1
        views = []
        off = 0
        for pi, ap in enumerate(parts):
            shp = list(ap.shape)
            n = shp[1] * shp[2]
            v = w_sb[s][:, off:off + n].rearrange("p (a b) -> p a b", a=shp[1])
            dma("pool", v, ap, [], [("W", s)], ("W", s, pi))
            views.append(v)
            off += n
        assert off <= WSLOT
        return s, views

    def mod_prologue():
        for li in range(4):
            for pi in range(3):
                s, (wv,) = wload([ada_d[li, :, :, pi * 1024:(pi + 1) * 1024]])
                for cc in range(8):
                    bank = cc % 2
                    for kc in range(KC):
                        mm(psA[bank][:, 0:2], wv[:, kc, cc * 128:(cc + 1) * 128], sv_sb[:, kc, :],
                           kc == 0, kc == KC - 1, [("W", s), "sv"], [PS(bank)])
                    o = li * 48 + (pi * 8 + cc) * 2
                    cp("dve", modx[:, o:o + 2], psA[bank][:, 0:2], [PS(bank)], ["modx"])
        dma("pool", xm_d.ap(), modx[:, :], ["modx"], ["xm_d"], "xst0")
        allgather(xm_d, xmo_d, ["xm_d"], "xmo_d")
        dma("pool", modg[:], xmo_d.ap().rearrange("(r p) n -> p r n", p=128), ["xmo_d"], ["modg"], "xld0")
        G = modg[:].rearrange("p r (l c v) -> p r l c v", l=4, c=24)
        for li in range(4):
            ab = C("adab", li * 48, 48).rearrange("p (r c) -> p r c", r=2)
            for s_ in range(2):
                mx = modall[:, li, s_, :].rearrange("p (r c) -> p r c", r=2)
                tt("dve", mx, G[:, :, li, :, s_], ab, ALU.add, ["modg", "cst"], ["mod"])

    def mod_layer(li):
        for s_ in range(2):
            M = modall[:, li, s_, :]
            stt("dve", nv_sb[:, 0, :, s_], M[:, 8:16], 1.0, C("normw", li * 16, 8), ALU.add, ALU.mult, ["mod", "cst"], ["nv"])
            cp("dve", nv_sb[:, 1, :, s_], M[:, 0:8], ["mod"], ["nv"])
            cp("dve", nv_sb[:, 2, :, s_], M[:, 16:24], ["mod"], ["nv"])
            stt("dve", nv_sb[:, 3, :, s_], M[:, 32:40], 1.0, C("normw", li * 16 + 8, 8), ALU.add, ALU.mult, ["mod", "cst"], ["nv"])
            cp("dve", nv_sb[:, 4, :, s_], M[:, 24:32], ["mod"], ["nv"])
            cp("dve", nv_sb[:, 5, :, s_], M[:, 40:48], ["mod"], ["nv"])

    def tile_stream(ti):
        return 1 if ti == CTX_TILE else 0

    def rstd_from_ps(bank, n, scale):
        act(rs_sb[:, 0:n], psA[bank][:, 0:n], AF.Ln, [PS(bank), "cst"], ["rs"], bias=epsc, scale=scale)
        act(rs_sb[:, 0:n], rs_sb[:, 0:n], AF.Exp, ["rs"], ["rs"], scale=-0.5)

    def sumsq_x(ti):
        t0, n = TILES[ti]
        for c in range(KC):
            q = tb16[c % 2]
            act(q[:, 0:n], x_sb[:, c, t0:t0 + n], AF.Square, [("x", c, ti)], [("tb16", c % 2)])
            mm(psA[6][:, 0:n], ones, q[:, 0:n], c == 0, c == KC - 1, [("tb16", c % 2), "cstb"], [PS(6)])
        rstd_from_ps(6, n, 1.0 / D)

    def norm_tile(ti, which):
        t0, n = TILES[ti]
        s_ = tile_stream(ti)
        sumsq_x(ti)
        for c in range(KC):
            tmp = t32[c % 2]
            tt("dve", tmp[:, 0:n], x_sb[:, c, t0:t0 + n], rs_sb[:, 0:n], ALU.mult, [("x", c, ti), "rs"], [("t32", c % 2)])
            a = nv_sb[:, 3 * which, c, s_:s_ + 1]
            b = nv_sb[:, 3 * which + 1, c, s_:s_ + 1]
            act(h_sb[:, c, t0:t0 + n], tmp[:, 0:n], AF.Identity, [("t32", c % 2), "nv"], [("h", ti)], bias=b, scale=a)

    def final_tile(ti):
        t0, n = TILES[ti]
        sumsq_x(ti)
        for c in range(KC):
            tmp = t32[c % 2]
            tt("dve", tmp[:, 0:n], x_sb[:, c, t0:t0 + n], rs_sb[:, 0:n], ALU.mult, [("x", c, ti), "rs"], [("t32", c % 2)])
            act(acc32[:, c % 4, 0:n], tmp[:, 0:n], AF.Identity, [("t32", c % 2), "cst"], [("acc", c % 4)], scale=C("fnw", c))
            dma("sp", out_d[:, c, t0:t0 + n], acc32[:, c % 4, 0:n], [("acc", c % 4)], [("out", ti, c)], "out%d" % (c % 4))

    def proj_fm(bank, wv, s, col0, ti):
        t0, n = TILES[ti]
        for kc in range(KC):
            mm(psA[bank][:, 0:n], wv[:, kc, col0:col0 + 128], h_sb[:, kc, t0:t0 + n], kc == 0, kc == KC - 1,
               [("W", s), ("h", ti)], [PS(bank)])

    def load_rope(ti):
        t0, n = TILES[ti]
        dma("sp", ropeT[:, 0:n], tab_d[:, LAYT.o("ropeC") + t0:LAYT.o("ropeC") + t0 + n], [], [("mixs", 0), ("mixs", 1)], ldkey())
        dma("sp", ropeT[:, 512:512 + n], tab_d[:, LAYT.o("ropeS") + t0:LAYT.o("ropeS") + t0 + n], [], [("mixs", 2), ("mixs", 3)], ldkey())

    rope_ctr = {"n": 0}

    def rope(bank, ti, out_ap, okey):
        t0, n = TILES[ti]
        i = rope_ctr["n"] % 2
        rope_ctr["n"] += 1
        tb = tb16[i]
        act(tb[:, 0:n], psA[bank][:, 0:n], AF.Copy, [PS(bank)], [("tb16", i)])
        tt("dve", t32[2][:, 0:n], psA[bank][:, 0:n], ropeT[:, 0:n], ALU.mult, [PS(bank), ("mixs", 0), ("mixs", 1)], [("t32", 2)])
        mm(psA[5][:, 0:n], perm, tb[:, 0:n], True, True, [("tb16", i), "cstb"], [PS(5)])
        tt("dve", t32[i][:, 0:n], psA[5][:, 0:n], ropeT[:, 512:512 + n], ALU.mult, [PS(5), ("mixs", 2), ("mixs", 3)], [("t32", i)])
        tt("dve", out_ap, t32[i][:, 0:n], t32[2][:, 0:n], ALU.add, [("t32", i), ("t32", 2)], [okey])

    RG = [[0, 1], [2, 3], [4, 5], [6, 7]]

    def allgather(src, dst, skeys, dkey, groups=None):
        groups = RG if groups is None else groups
        pg.add("pool", lambda h: h.collective_compute("AllGather", ALU.bypass, replica_groups=groups,
                                                      ins=[src.ap().opt()], outs=[dst.ap().opt()]),
               list(skeys), [dkey], dma="cc", inc=1)

    def even_small_tables(j):
        lo = 8 * j
        for hd in range(4):
            for d_ in range(2):
                lx = lg_sb[:, lo + 4 * d_ + hd:lo + 4 * d_ + hd + 1]
                act(zm_sb[:, d_, hd, :], C("zidx_m", d_ * 4, 4), AF.Exp, ["cst", "lg"], ["zm"], scale=lx)
                act(zc_sb[:, d_, hd, :], C("zidx_c", d_ * 2, 2), AF.Exp, ["cst", "lg"], ["zm"], scale=lx)
                act(dec_sb[:, d_, hd:hd + 1], lx, AF.Exp, ["lg"], ["dec"], scale=512.0)
                act(cf_sb[:, d_, :, hd], C("cfidx", d_ * 4, 4), AF.Exp, ["cst", "lg"], ["cf"], scale=lx)
        ts("dve", cf_sb[:, 0, :, :], cf_sb[:, 0, :, :], fB, None, ALU.mult, None, ["cf", "cst"], ["cf"])
        ts("dve", cf_sb[:, 1, :, :], cf_sb[:, 1, :, :], fA, None, ALU.mult, None, ["cf", "cst"], ["cf"])

    def head_tables(j, hd):
        lo = 8 * j
        lf = lg_sb[:, lo + hd:lo + hd + 1]
        lb = lg_sb[:, lo + 4 + hd:lo + 4 + hd + 1]
        acck = [("acc", i) for i in range(4)]
        dma("sp", accf[:, 0:896], tab_d[:, LAYT.o("a1idx"):LAYT.o("a1idx") + 896], [], acck, "tbl0")
        dma("sp", accf[:, 1024:1920], tab_d[:, LAYT.o("a2idx"):LAYT.o("a2idx") + 896], [], acck, "tbl1")
        act(accf[:, 0:896], accf[:, 0:896], AF.Exp, acck + ["lg", "cst"], acck, scale=lf, bias=C("lns"))
        act(accf[:, 1024:1920], accf[:, 1024:1920], AF.Exp, acck + ["lg", "cst"], acck, scale=lb, bias=C("lns"))
        tt("dve", dtab[:, :], accf[:, 0:896], accf[:, 1024:1920], ALU.add, acck, ["dtab"])
        dma("sp", accf[:, 0:512], tab_d[:, LAYT.o("xifidx"):LAYT.o("xifidx") + 512], [], acck, "tbl2")
        dma("sp", accf[:, 512:1024], tab_d[:, LAYT.o("xibidx"):LAYT.o("xibidx") + 512], [], acck, "tbl3")
        act(xi_sb[:, 0, :], accf[:, 0:512], AF.Exp, acck + ["lg", "cst"], ["xi"], scale=lf, bias=C("lns"))
        act(xi_sb[:, 1, :], accf[:, 512:1024], AF.Exp, acck + ["lg", "cst"], ["xi"], scale=lb, bias=C("lns"))

    def even_mixer(li, full_ctx):
        j = li // 2
        even_small_tables(j)
        order = [CTX_TILE, 0, 1, 2, 3]
        tiles_full = order if full_ctx else [0, 1, 2, 3]
        for ti in order:
            if ti not in tiles_full:
                norm_tile(ti, 0)
        norm_tile(tiles_full[0], 0)
        s2, (wa, wgb) = wload([evin_d[j, :, :, 2048:2560], evin_d[j, :, :, 2560:3072]])
        for i_, ti in enumerate(tiles_full):
            if i_ + 1 < len(tiles_full):
                norm_tile(tiles_full[i_ + 1], 0)
            t0, n = TILES[ti]
            ub = o16[ti % 2]
            for ch in range(4):
                b0 = (2 * ch) % 4
                b1 = b0 + 1
                proj_fm(b0, wa, s2, ch * 128, ti)
                proj_fm(b1, wgb, s2, ch * 128, ti)
                act(t32[ch % 2][:, 0:n], psA[b1][:, 0:n], AF.Sigmoid, [PS(b1)], [("t32", ch % 2)])
                tt("dve", ub[:, ch, 0:n], psA[b0][:, 0:n], t32[ch % 2][:, 0:n], ALU.mult, [PS(b0), ("t32", ch % 2)], [("o16", ti % 2)])
            if ti == CTX_TILE:
                dma("sp", uc_d[:, :, HC:HC + n], ub[:, :, 0:n], [("o16", ti % 2)], [("u_d", ti)], stkey())
            else:
                dma("sp", um_d[:, :, HC + t0:HC + t0 + n], ub[:, :, 0:n], [("o16", ti % 2)], [("u_d", ti)], stkey())
            if ti == 0:
                cp("dve", xh_sb[:, 0:4 * HC].rearrange("p (a b) -> p a b", a=4), ub[:, :, 0:HC], [("o16", ti % 2)], ["xh"])
            if ti == 3:
                cp("dve", xh_sb[:, 4 * HC:8 * HC].rearrange("p (a b) -> p a b", a=4), ub[:, :, n - HC:n], [("o16", ti % 2)], ["xh"])
        s3, (wk, wvv) = wload([evin_d[j, :, :, 512:1024], evin_d[j, :, :, 1024:1536]], slot=1 - s2)
        so = 1 - s3
        sp32 = w_sb[so][:, :].bitcast(F32).rearrange("p (a b c) -> p a b c", a=2, b=4)
        SPK = ("W", so)
        for ti in order:
            t0, n = TILES[ti]
            nb = n // 128
            load_rope(ti)
            ob = o16[0]
            for jb in range(nb):
                bank = jb % 2
                for kc in range(KC):
                    mm(psA[bank][:, :], h_sb[:, kc, t0 + jb * 128:t0 + (jb + 1) * 128], wvv[:, kc, :], kc == 0, kc == KC - 1,
                       [("W", s3), ("h", ti)], [PS(bank)])
                act(ob[:, jb, :], psA[bank][:, :], AF.Copy, [PS(bank)], [("o16", 0)])
            dma("sp", v_d[:, t0 // 128:t0 // 128 + nb, :], ob[:, 0:nb, :], [("o16", 0)], [("v_d", ti)], stkey())
            kb = o16[1]
            zt = zc_sb if ti == CTX_TILE else zm_sb
            for hd in range(4):
                bank = 2 + hd % 2
                proj_fm(bank, wk, s3, hd * 128, ti)
                rope(bank, ti, kb[:, hd, 0:n], ("o16", 1))
                for jb in range(nb):
                    pg.add("pe", lambda h, hd=hd, jb=jb: h.transpose(psT[:, jb * 128:(jb + 1) * 128], kb[:, hd, jb * 128:(jb + 1) * 128], ident),
                           [("o16", 1), "cstb"], [PS(PST)])
                for d_ in range(2):
                    kvbank = 4 if d_ == 0 else 6
                    for jb in range(nb):
                        kz = kz_sb[jb % 2]
                        ts("dve", kz[:, d_, :], psT[:, jb * 128:(jb + 1) * 128], zt[:, d_, hd, jb:jb + 1], None, ALU.mult, None,
                           [PS(PST), "zm"], [("kz", jb % 2, d_)])
                        mm(psA[kvbank][:, hd * 128:(hd + 1) * 128], kz[:, d_, :], ob[:, jb, hd * 128:(hd + 1) * 128], jb == 0, jb == nb - 1,
                           [("kz", jb % 2, d_), ("o16", 0)], [PS(kvbank)])
            dma("sp", k_d[:, :, t0:t0 + n], kb[:, :, 0:n], [("o16", 1)], [("k_d", ti)], stkey())
            if ti == CTX_TILE:
                ts("dve", S_sb[:, 0, :], psA[4][:, :], fA, None, ALU.mult, None, [PS(4), "cst"], ["S"])
                ts("dve", S_sb[:, 1, :], psA[6][:, :], fB, None, ALU.mult, None, [PS(6), "cst"], ["S"])
            else:
                cp("dve", sp32[:, 0, ti, :], S_sb[:, 0, :], ["S"], [SPK])
                for hd in range(4):
                    hs = slice(hd * 128, (hd + 1) * 128)
                    stt("dve", S_sb[:, 0, hs], S_sb[:, 0, hs], dec_sb[:, 0, hd:hd + 1], psA[4][:, hs], ALU.mult, ALU.add,
                        ["S", "dec", PS(4)], ["S"])
                cp("dve", acc32[:, ti, :], psA[6][:, :], [PS(6)], [("acc", ti)])
        for ti in (3, 2, 1, 0):
            cp("dve", sp32[:, 1, ti, :], S_sb[:, 1, :], ["S"], [SPK])
            for hd in range(4):
                hs = slice(hd * 128, (hd + 1) * 128)
                stt("dve", S_sb[:, 1, hs], S_sb[:, 1, hs], dec_sb[:, 1, hd:hd + 1], acc32[:, ti, hs], ALU.mult, ALU.add,
                    ["S", "dec", ("acc", ti)], ["S"])
        sq_, (wq, wgt) = wload([evin_d[j, :, :, 0:512], evin_d[j, :, :, 1536:2048]], slot=s3)
        cwo = j * 124
        HK = [("h", t) for t in range(5)]
        dgf = h_sb[:].rearrange("p a b -> p (a b)")
        dma("pool", xi_d.ap()[:, 0:1024], Sflat, ["S"], ["xi_d"], "xst0")
        dma("pool", xi_d.ap()[:, 1024:XW], xh_sb[:, :], ["xh"], ["xi_d2"], "xst1")
        allgather(xi_d, xo_d, ["xi_d", "xi_d2"], "xo_d")
        acck = [("acc", i) for i in range(4)]
        dma("pool", accf[:, 0:512], xo_d.ap()[0:128, 0:512], ["xo_d"], acck, "xld0")
        dma("pool", accf[:, 512:1024], xo_d.ap()[128:256, 512:1024], ["xo_d"], acck, "xld1")
        dma("pool", accf[:, 1024:1024 + 4 * HC], xo_d.ap()[0:128, 1024 + 4 * HC:XW], ["xo_d"], acck, "xld0")
        dma("pool", accf[:, 1024 + 4 * HC:XW], xo_d.ap()[128:256, 1024:1024 + 4 * HC], ["xo_d"], acck, "xld1")
        for i_, ti in enumerate(tiles_full):
            t0, n = TILES[ti]
            load_rope(ti)
            for hd in range(4):
                bank = hd % 2
                proj_fm(bank, wq, sq_, hd * 128, ti)
                rope(bank, ti, o16[0][:, hd, 0:n], ("o16", 0))
                bank = 2 + hd % 2
                proj_fm(bank, wgt, sq_, hd * 128, ti)
                act(o16[1][:, hd, 0:n], psA[bank][:, 0:n], AF.Silu, [PS(bank)], [("o16", 1)])
            dma("sp", q_d[:, :, t0:t0 + n], o16[0][:, :, 0:n], [("o16", 0)], [("q_d", ti)], stkey())
            dma("sp", g_d[:, :, t0:t0 + n], o16[1][:, :, 0:n], [("o16", 1)], [("g_d", ti)], stkey())
        for d_ in range(2):
            for ti in range(4):
                for hd in range(4):
                    hs = slice(hd * 128, (hd + 1) * 128)
                    stt("dve", o16[d_][:, ti, hs], accf[:, d_ * 512 + hd * 128:d_ * 512 + (hd + 1) * 128], cf_sb[:, d_, ti, hd:hd + 1],
                        sp32[:, d_, ti, hs], ALU.mult, ALU.add, acck + ["cf", SPK], [("o16", d_)])
        hl = tb16[2][:, 0:8 * HC].rearrange("p (a b) -> p a b", a=4)
        ts("dve", hl[:, :, 0:HC], accf[:, 1024:1024 + 4 * HC].rearrange("p (a b) -> p a b", a=4), fB, None, ALU.mult, None,
           acck + ["cst"], [("tb16", 2)])
        ts("dve", hl[:, :, HC:2 * HC], accf[:, 1024 + 4 * HC:XW].rearrange("p (a b) -> p a b", a=4), fA, None, ALU.mult, None,
           acck + ["cst"], [("tb16", 2)])
        dma("sp", um_d[:, :, 0:HC], hl[:, :, 0:HC], [("tb16", 2)], [("u_d", 0)], stkey())
        dma("sp", um_d[:, :, HC + TM:], hl[:, :, HC:2 * HC], [("tb16", 2)], [("u_d", 3)], stkey())
        s4, (wo,) = wload([evout_d[j, :, :, :]], slot=so)
        l16b = l16x
        LQ = [l16[0][:, 0, 0:512], l16b[:, 0, 0:512]]
        LK = [l16[0][:, 1, 0:512], l16b[:, 1, 0:512]]
        LG = [l16[0][:, 2, 0:512], l16b[:, 2, 0:512]]
        RB = [l16[0][:, 3, 0:512], l16b[:, 3, 0:512]]
        LV = [l16[1][:, :, 0:128], l16[1][:, :, 128:256]]
        its = [(hd, ti) for hd in range(4) for ti in tiles_full]

        def ret_loads(it):
            hd, ti = its[it]
            sset = it % 2
            t0, n = TILES[ti]
            nb = n // 128
            hs = slice(hd * 128, (hd + 1) * 128)
            dma("sp", LQ[sset][:, 0:n], q_d[:, hd, t0:t0 + n], [("q_d", ti)], [("lq", sset)], ldkey())
            dma("sp", LK[sset][:, 0:n], k_d[:, hd, t0:t0 + n], [("k_d", ti)], [("lk", sset)], ldkey())
            dma("sp", LV[sset][:, 0:nb, :], v_d[:, t0 // 128:t0 // 128 + nb, hs], [("v_d", ti)], [("lv", sset)], ldkey())

        def ret_load_g(it):
            hd, ti = its[it]
            sset = it % 2
            t0, n = TILES[ti]
            dma("sp", LG[sset][:, 0:n], g_d[:, hd, t0:t0 + n], [("g_d", ti)], [("lg_", sset)], ldkey())

        L0K = [("lq", 0), ("lk", 0), ("lg_", 0), ("rb", 0)]
        L1K = [("lv", 0), ("lv", 1)]
        def stage_a(it):
            hd, ti = its[it]
            sset = it % 2
            hs = slice(hd * 128, (hd + 1) * 128)
            if ti == tiles_full[0]:
                head_tables(j, hd)
            t0, n = TILES[ti]
            nb = n // 128
            inter = ti != CTX_TILE
            lq, lk, lv = LQ[sset], LK[sset], LV[sset]
            if inter:
                tt("dve", tb16[0][:, 0:n], lq[:, 0:n], xi_sb[:, 0, 0:n], ALU.mult, [("lq", sset), "xi"], [("tb16", 0)])
                tt("dve", tb16[1][:, 0:n], lq[:, 0:n], xi_sb[:, 1, 0:n], ALU.mult, [("lq", sset), "xi"], [("tb16", 1)])
            for jb in range(nb):
                mm(psA[jb][:, 0:n], lk[:, jb * 128:(jb + 1) * 128], lq[:, 0:n], True, True, [("lk", sset), ("lq", sset)], [PS(jb)])
                st0 = 384 - 128 * jb
                tt("dve", mix_sb[:, jb, 0:n], psA[jb][:, 0:n], dtab[:, st0:st0 + n], ALU.mult, [PS(jb), "dtab"], [("mixs", jb)])
            yb = 4 + it % 2
            nmm = nb + (2 if inter else 0)
            for jb in range(nb):
                mm(psA[yb][:, 0:n], lv[:, jb, :], mix_sb[:, jb, 0:n], jb == 0, jb == nmm - 1, [("lv", sset), ("mixs", jb)], [PS(yb)])
            if inter:
                mm(psA[yb][:, 0:n], o16[0][:, ti, hs], tb16[0][:, 0:n], False, False, [("o16", 0), ("tb16", 0)], [PS(yb)])
                mm(psA[yb][:, 0:n], o16[1][:, ti, hs], tb16[1][:, 0:n], False, True, [("o16", 1), ("tb16", 1)], [PS(yb)])

        def stage_b(it):
            hd, ti = its[it]
            sset = it % 2
            t0, n = TILES[ti]
            yb = 4 + it % 2
            lgt, rb = LG[sset], RB[sset]
            act(tb16[2][:, 0:n], psA[yb][:, 0:n], AF.Square, [PS(yb)], [("tb16", 2)])
            mm(psA[6][:, 0:n], ones, tb16[2][:, 0:n], True, True, [("tb16", 2), "cstb"], [PS(6)])
            rstd_from_ps(6, n, 1.0 / 128)
            tt("dve", t32[0][:, 0:n], psA[yb][:, 0:n], rs_sb[:, 0:n], ALU.mult, [PS(yb), "rs"], [("t32", 0)])
            tt("dve", rb[:, 0:n], t32[0][:, 0:n], lgt[:, 0:n], ALU.mult, [("t32", 0), ("lg_", sset)], [("rb", sset)])
            dma("sp", r_d[:, hd, t0:t0 + n], rb[:, 0:n], [("rb", sset)], [("r_d", ti, hd)], stkey())
            per = (124 + len(its) - 1) // len(its)
            for idx in range(it * per, min(124, (it + 1) * per)):
                ts("dve", dgf[:, idx * 128:(idx + 1) * 128], ident, C("convw", cwo + idx), None, ALU.mult, None, ["cstb", "cst"], HK)

        ret_loads(0)
        ret_load_g(0)
        if len(its) > 1:
            ret_loads(1)
            ret_load_g(1)
        stage_a(0)
        for it in range(len(its)):
            if it + 2 < len(its):
                ret_loads(it + 2)
            if it + 1 < len(its):
                stage_a(it + 1)
            stage_b(it)
            if it + 2 < len(its):
                ret_load_g(it + 2)
        for ti in tiles_full:
            t0, n = TILES[ti]
            lu, lr = l16[0], l16[1]
            lk0 = L0K
            if ti == CTX_TILE:
                dma("sp", lu[:, :, 0:n + 2 * HC], uc_d[:, :, :], [("u_d", ti), "uc_h0", "uc_h1"], lk0, ldkey())
            else:
                rk = [("u_d", ti)] + ([("u_d", ti - 1)] if ti > 0 else []) + ([("u_d", ti + 1)] if ti < 3 else [])
                dma("sp", lu[:, :, 0:n + 2 * HC], um_d[:, :, t0:t0 + n + 2 * HC], rk, lk0, ldkey())
            dma("sp", lr[:, :, 0:n], r_d[:, :, t0:t0 + n], [("r_d", ti, hd) for hd in range(4)], L1K, ldkey())
            for ch in range(4):
                for k in range(31):
                    idx = ch * 31 + k
                    mm(psA[ch][:, 0:n], dgf[:, idx * 128:(idx + 1) * 128], lu[:, ch, k:k + n], k == 0, k == 30, HK + lk0, [PS(ch)])
            for ch in range(4):
                act(tb16[ch % 2][:, 0:n], psA[ch][:, 0:n], AF.Copy, [PS(ch)], [("tb16", ch % 2)])
                mm(psA[4][:, 0:n], ones, tb16[ch % 2][:, 0:n], ch == 0, ch == 3, [("tb16", ch % 2), "cstb"], [PS(4)])
            for ch in range(4):
                act(tb16[ch % 2][:, 0:n], psA[ch][:, 0:n], AF.Square, [PS(ch)], [("tb16", ch % 2)])
                mm(psA[5][:, 0:n], ones, tb16[ch % 2][:, 0:n], ch == 0, ch == 3, [("tb16", ch % 2), "cstb"], [PS(5)])
            act(t32[0][:, 0:n], psA[4][:, 0:n], AF.Copy, [PS(4)], [("t32", 0)], scale=1.0 / 512)
            tt("dve", t32[1][:, 0:n], t32[0][:, 0:n], t32[0][:, 0:n], ALU.mult, [("t32", 0)], [("t32", 1)])
            stt("dve", t32[1][:, 0:n], psA[5][:, 0:n], 1.0 / 512, t32[1][:, 0:n], ALU.mult, ALU.subtract, [PS(5), ("t32", 1)], [("t32", 1)])
            act(rs_sb[:, 0:n], t32[1][:, 0:n], AF.Ln, [("t32", 1), "cst"], ["rs"], bias=epsc)
            act(rs_sb[:, 0:n], rs_sb[:, 0:n], AF.Exp, ["rs"], ["rs"], scale=-0.5)
            for ch in range(4):
                tt("dve", acc32[:, ch, 0:n], psA[ch][:, 0:n], t32[0][:, 0:n], ALU.subtract, [PS(ch), ("t32", 0)], [("acc", ch)])
                tt("dve", acc32[:, ch, 0:n], acc32[:, ch, 0:n], rs_sb[:, 0:n], ALU.mult, [("acc", ch), "rs"], [("acc", ch)])
                act(mix_sb[:, 4 + ch, 0:n], acc32[:, ch, 0:n], AF.Silu, [("acc", ch), "cst"], [("mixs", 4 + ch)],
                    bias=C("convlnb", j * 4 + ch), scale=C("convlnw", j * 4 + ch))
            wout_tile(ti, wo, s4, [lr[:, kc, :] for kc in range(4)] + [mix_sb[:, 4 + kc, :] for kc in range(4)],
                      [("lv", 0)] * 4 + [("mixs", 4 + kc) for kc in range(4)])

    def wout_tile(ti, wo, s, rhs_list, rkeys):
        t0, n = TILES[ti]
        s_ = tile_stream(ti)
        for oc in range(KC):
            bank = 4 + oc % 3
            for kc in range(KC):
                mm(psA[bank][:, 0:n], wo[:, kc, oc * 128:(oc + 1) * 128], rhs_list[kc][:, 0:n], kc == 0, kc == KC - 1,
                   [("W", s), rkeys[kc]], [PS(bank)])
            stt("dve", x_sb[:, oc, t0:t0 + n], psA[bank][:, 0:n], nv_sb[:, 2, oc, s_:s_ + 1], x_sb[:, oc, t0:t0 + n], ALU.mult, ALU.add,
                [PS(bank), "nv"], [("x", oc, ti)])

    def pool_window(ti, lp, lkeys):
        t0, n = TILES[ti]
        W_ = n + 2 * HP
        icn = "icc" if ti == CTX_TILE else "icm"
        for g, w in enumerate((2, 4, 8, 16)):
            left = w // 2
            right = w - 1 - left
            cur, oth = 0, 1
            tt("dve", t32[cur][:, 1:W_], lp[:, g, 1:W_], lp[:, g, 0:W_ - 1], ALU.add, lkeys, [("t32", cur)])
            k = 2
            lo = 1
            while k < w:
                tt("dve", t32[oth][:, lo + k:W_], t32[cur][:, lo + k:W_], t32[cur][:, lo:W_ - k], ALU.add, [("t32", cur)], [("t32", oth)])
                lo += k
                k *= 2
                cur, oth = oth, cur
            e0 = HP + right
            ts("dve", acc32[:, g, 0:n], t32[cur][:, e0:e0 + n], 1.0 / w, None, ALU.mult, None, [("t32", cur)], [("acc", g)])
            if ti == 0 or ti == CTX_TILE:
                tt("dve", acc32[:, g, 0:8], t32[cur][:, e0:e0 + 8], C(icn, g * 16, 8), ALU.mult, [("t32", cur), "cst"], [("acc", g)])
            if ti == 3 or ti == CTX_TILE:
                tt("dve", acc32[:, g, n - 8:n], t32[cur][:, e0 + n - 8:e0 + n], C(icn, g * 16 + 8, 8), ALU.mult, [("t32", cur), "cst"], [("acc", g)])
            tt("dve", mix_sb[:, g, 0:n], acc32[:, g, 0:n], lp[:, g, HP:HP + n], ALU.subtract, [("acc", g)] + lkeys, [("mixs", g)])

    def odd_mixer(li, tiles):
        j = li // 2
        norm_tile(tiles[0], 0)
        dma("pool", pwsg, tab_d[:, LAYT.o("pwsg") + j * 1024:LAYT.o("pwsg") + (j + 1) * 1024], [], ["S"], "tbl")
        acck = [("acc", i) for i in range(4)]
        dma("sp", accf[:, 0:1536], tab_d[:, LAYT.o("sgtab") + j * 1536:LAYT.o("sgtab") + (j + 1) * 1536], [], acck, ldkey())
        lnw = accf[:, 0:512]
        lnb = accf[:, 512:1024]
        s, (wpc,) = wload([odin_d[j, :, :, 0:512]])
        for i_, ti in enumerate(tiles):
            if i_ + 1 < len(tiles):
                norm_tile(tiles[i_ + 1], 0)
            t0, n = TILES[ti]
            pb = o16[ti % 2]
            for g in range(4):
                bank = g % 4
                proj_fm(bank, wpc, s, g * 128, ti)
                act(pb[:, g, 0:n], psA[bank][:, 0:n], AF.Copy, [PS(bank)], [("o16", ti % 2)])
            if ti == CTX_TILE:
                dma("sp", pc_d[:, :, HP:HP + n], pb[:, :, 0:n], [("o16", ti % 2)], [("p_d", ti)], stkey())
            else:
                dma("sp", pm_d[:, :, HP + t0:HP + t0 + n], pb[:, :, 0:n], [("o16", ti % 2)], [("p_d", ti)], stkey())
            if ti == 0:
                cp("dve", xh_sb[:, 0:4 * HP].rearrange("p (a b) -> p a b", a=4), pb[:, :, 0:HP], [("o16", ti % 2)], ["xh"])
            if ti == 3:
                cp("dve", xh_sb[:, 4 * HP:8 * HP].rearrange("p (a b) -> p a b", a=4), pb[:, :, n - HP:n], [("o16", ti % 2)], ["xh"])
        s, (wu_, wv_) = wload([odin_d[j, :, :, 512:1024], odin_d[j, :, :, 1024:1536]])
        dma("pool", xio_d.ap(), xh_sb[:, 0:XWO], ["xh"], ["xio_d"], "xst0")
        allgather(xio_d, xoo_d, ["xio_d"], "xoo_d")
        dma("pool", rs_sb[:, 0:4 * HP], xoo_d.ap()[0:128, 4 * HP:8 * HP], ["xoo_d"], ["rs"], "xld0")
        dma("pool", rs_sb[:, 4 * HP:8 * HP], xoo_d.ap()[128:256, 0:4 * HP], ["xoo_d"], ["rs"], "xld1")
        hl = tb16[2][:, 0:8 * HP].rearrange("p (a b) -> p a b", a=4)
        ts("dve", hl[:, :, 0:HP], rs_sb[:, 0:4 * HP].rearrange("p (a b) -> p a b", a=4), fB, None, ALU.mult, None, ["rs", "cst"], [("tb16", 2)])
        ts("dve", hl[:, :, HP:2 * HP], rs_sb[:, 4 * HP:8 * HP].rearrange("p (a b) -> p a b", a=4), fA, None, ALU.mult, None, ["rs", "cst"], [("tb16", 2)])
        dma("sp", pm_d[:, :, 0:HP], hl[:, :, 0:HP], [("tb16", 2)], [("p_d", 0)], stkey())
        dma("sp", pm_d[:, :, HP + TM:], hl[:, :, HP:2 * HP], [("tb16", 2)], [("p_d", 3)], stkey())
        for ti in tiles:
            t0, n = TILES[ti]
            nb = n // 128
            ub = o16[0]
            sb_ = o16[1]
            for g in range(4):
                bank = g % 2
                proj_fm(bank, wu_, s, g * 128, ti)
                act(ub[:, g, 0:n], psA[bank][:, 0:n], AF.Gelu, [PS(bank)], [("o16", 0)])
            def S1(jb, ti=ti, t0=t0):
                bank = 2 + jb % 2
                for kc in range(KC):
                    mm(psA[bank][:, :], h_sb[:, kc, t0 + jb * 128:t0 + (jb + 1) * 128], wv_[:, kc, :], kc == 0, kc == KC - 1,
                       [("W", s), ("h", ti)], [PS(bank)])
                z = t32[jb % 2]
                zk = ("t32", jb % 2)
                pb_ = jb % 2
                sq = l16x[:, pb_, :]
                sqk = ("sqx", pb_)
                smo = 4 * pb_
                smk = ("sm", pb_)
                act(z[:, 0:512], psA[bank][:, :], AF.Gelu, [PS(bank)], [zk])
                act(sq, z[:, 0:512], AF.Square, [zk], [sqk])
            def S2(jb, ti=ti, t0=t0):
                z = t32[jb % 2]
                zk = ("t32", jb % 2)
                pb_ = jb % 2
                sq = l16x[:, pb_, :]
                sqk = ("sqx", pb_)
                smo = 4 * pb_
                smk = ("sm", pb_)
                pg.add("dve", lambda h, z=z, smo=smo: h.reduce_sum(out=sm[:, smo:smo + 1], in_=z[:, 0:512], axis=mybir.AxisListType.X), [zk], [smk], size=1)
                pg.add("dve", lambda h, sq=sq, smo=smo: h.reduce_sum(out=sm[:, smo + 1:smo + 2], in_=sq, axis=mybir.AxisListType.X), [sqk], [smk], size=1)
                ts("dve", sm[:, smo + 2:smo + 3], sm[:, smo:smo + 1], 1.0 / 512, None, ALU.mult, None, [smk], [smk])
                tt("dve", sm[:, smo + 3:smo + 4], sm[:, smo + 2:smo + 3], sm[:, smo + 2:smo + 3], ALU.mult, [smk], [smk])
                stt("dve", sm[:, smo + 3:smo + 4], sm[:, smo + 1:smo + 2], 1.0 / 512, sm[:, smo + 3:smo + 4], ALU.mult, ALU.subtract, [smk], [smk])
                act(sm[:, smo + 3:smo + 4], sm[:, smo + 3:smo + 4], AF.Sqrt, [smk, "cst"], [smk], bias=epsc)
                recip(sm[:, smo + 3:smo + 4], smk)
                ts("dve", z[:, 0:512], z[:, 0:512], sm[:, smo + 2:smo + 3], sm[:, smo + 3:smo + 4], ALU.subtract, ALU.mult, [smk, zk], [zk])
                tt("dve", z[:, 0:512], z[:, 0:512], lnw, ALU.mult, [zk] + acck, [zk])
                vb = tb16[jb % 2]
                tt("dve", vb[:, :], z[:, 0:512], lnb, ALU.add, [zk] + acck, [("tb16", jb % 2)])
                bank2 = 4 + jb % 2
                for g in range(4):
                    mm(psA[bank2][:, g * 128:(g + 1) * 128], vb[:, g * 128:(g + 1) * 128], pwsg[:, 512 + g * 128:512 + (g + 1) * 128], True, True,
                       [("tb16", jb % 2), "S"], [PS(bank2)])
                tt("dve", t32[2][:, 0:512], psA[bank2][:, 0:512], accf[:, 1024:1536], ALU.add, [PS(bank2)] + acck, [("t32", 2)])
                tt("dve", sb_[:, :, jb * 128:(jb + 1) * 128], t32[2][:, 0:512].rearrange("p (g q) -> p g q", g=4),
                   ub[:, :, jb * 128:(jb + 1) * 128], ALU.mult, [("t32", 2), ("o16", 0)], [("o16", 1)])
            S1(0)
            for jb in range(nb):
                if jb + 1 < nb:
                    S1(jb + 1)
                S2(jb)
            dma("sp", r_d[:, :, t0:t0 + n], sb_[:, :, 0:n], [("o16", 1)], [("r_d", ti)], stkey())
        s, (wo,) = wload([odout_d[j, :, :, :]])
        for ti in tiles:
            t0, n = TILES[ti]
            lp, lr = l16[0], l16[1]
            lk0 = [("lq", 0), ("lk", 0), ("lg_", 0), ("rb", 0)]
            if ti == CTX_TILE:
                dma("sp", lp[:, :, 0:n + 2 * HP], pc_d[:, :, :], [("p_d", ti), "pc_h0", "pc_h1"], lk0, ldkey())
            else:
                rk = [("p_d", ti)] + ([("p_d", ti - 1)] if ti > 0 else []) + ([("p_d", ti + 1)] if ti < 3 else [])
                dma("sp", lp[:, :, 0:n + 2 * HP], pm_d[:, :, t0:t0 + n + 2 * HP], rk, lk0, ldkey())
            dma("sp", lr[:, :, 0:n], r_d[:, :, t0:t0 + n], [("r_d", ti)], [("lv", 0), ("lv", 1)], ldkey())
            pool_window(ti, lp, lk0)
            for g in range(4):
                bank = g % 2
                mm(psA[bank][:, 0:n], pwsg[:, g * 128:(g + 1) * 128], mix_sb[:, g, 0:n], True, True, [("mixs", g), "S"], [PS(bank)])
                ts("dve", mix_sb[:, 4 + g, 0:n], psA[bank][:, 0:n], C("poolsc", j * 4 + g), None, ALU.mult, None,
                   [PS(bank), "cst"], [("mixs", 4 + g)])
            if ti == 0:
                dbg_dump("mix", mix_sb[:, :, :], [("mixs", i) for i in range(8)])
                dbg_dump("lp", lp[:, :, :], lk0)
                dbg_dump("lr", lr[:, :, :], [("l16", 1)])
            wout_tile(ti, wo, s, [mix_sb[:, 4 + kc, :] for kc in range(4)] + [lr[:, kc, :] for kc in range(4)],
                      [("mixs", 4 + kc) for kc in range(4)] + [("lv", 0)] * 4)

    def ffn(li, tiles):
        norm_tile(tiles[0], 1)
        for f0 in range(0, NFF, 2):
            s, (wgv, wuv, wdv) = wload([wg_d[li, :, :, f0 * 128:(f0 + 2) * 128], wu_d[li, :, :, f0 * 128:(f0 + 2) * 128],
                                        wd_d[li, :, f0:f0 + 2, :]])
            for i_, ti in enumerate(tiles):
                if f0 == 0 and i_ + 1 < len(tiles):
                    norm_tile(tiles[i_ + 1], 1)
                t0, n = TILES[ti]
                s_ = tile_stream(ti)
                hsel = (f0 // 2 + ti) % 2
                for f in range(2):
                    bg, bu = 2 * f, 2 * f + 1
                    col = f * 128
                    for kc in range(KC):
                        mm(psA[bg][:, 0:n], wgv[:, kc, col:col + 128], h_sb[:, kc, t0:t0 + n], kc == 0, kc == KC - 1, [("W", s), ("h", ti)], [PS(bg)])
                    for kc in range(KC):
                        mm(psA[bu][:, 0:n], wuv[:, kc, col:col + 128], h_sb[:, kc, t0:t0 + n], kc == 0, kc == KC - 1, [("W", s), ("h", ti)], [PS(bu)])
                    act(t32[f][:, 0:n], psA[bg][:, 0:n], AF.Silu, [PS(bg)], [("t32", f)])
                    tt("dve", hid_sb[:, 2 * hsel + f, 0:n], psA[bu][:, 0:n], t32[f][:, 0:n], ALU.mult, [PS(bu), ("t32", f)], [("mixs", 4 + 2 * hsel + f)])
                for oc in range(KC):
                    bank = 4 + oc % 3
                    for fc in range(2):
                        mm(psA[bank][:, 0:n], wdv[:, fc, oc * 128:(oc + 1) * 128], hid_sb[:, 2 * hsel + fc, 0:n], fc == 0, fc == 1,
                           [("W", s), ("mixs", 4 + 2 * hsel + fc)], [PS(bank)])
                    stt("dve", x_sb[:, oc, t0:t0 + n], psA[bank][:, 0:n], nv_sb[:, 5, oc, s_:s_ + 1], x_sb[:, oc, t0:t0 + n], ALU.mult, ALU.add,
                        [PS(bank), "nv"], [("x", oc, ti)])

    mod_prologue()
    for li in range(n_layers):
        mod_layer(li)
        ctx_after = any(m % 2 == 0 for m in range(li + 1, 4))
        tl = [0, 1, 2, 3] + ([CTX_TILE] if ctx_after else [])
        if li % 2 == 0:
            even_mixer(li, ctx_after)
        else:
            odd_mixer(li, tl)
        ffn(li, tl)

    okeys = []
    for ti in range(4):
        t0, n = TILES[ti]
        if debug_x:
            for c in range(KC):
                dma("sp", out_d[:, c, t0:t0 + n], x_sb[:, c, t0:t0 + n], [("x", c, ti)], [("out", ti, c)], "out%d" % (c % 4))
        else:
            final_tile(ti)
        okeys += [("out", ti, c) for c in range(KC)]
    pg.add("sp", lambda h: h.nop(), okeys + dbg_list, [])

    pg.finalize()
    sems = {}
    for e in ENGS:
        sems[("eng", e)] = E(nc.semaphore("s_" + e))
    for i, k in enumerate(pg.dma_keys):
        sems[("dma", k)] = E(nc.semaphore("d%d" % i))
    block = E(nc.Block())

    @block.tensor
    def _(h):
        pg.emit(sems, "pe", h)

    @block.scalar
    def _(h):
        pg.emit(sems, "act", h)

    @block.vector
    def _(h):
        pg.emit(sems, "dve", h)

    @block.gpsimd
    def _(h):
        pg.emit(sems, "pool", h)

    @block.sync
    def _(h):
        pg.emit(sems, "sp", h)

    es.close()
    return nc


_NC_CACHE = {}


def prep_inputs(inp):
    x = np.asarray(inp["x"], np.float32)
    ctx = np.asarray(inp["ctx"], np.float32)
    shared = {
        "even_w_in": np.stack([wlay(inp["even_w_in"][l]) for l in range(2)]),
        "even_w_out": np.stack([wlay(inp["even_w_out"][l]) for l in range(2)]),
        "odd_w_in": np.stack([wlay(inp["odd_w_in"][l]) for l in range(2)]),
        "odd_w_out": np.stack([wlay(inp["odd_w_out"][l]) for l in range(2)]),
        "ffn_w_gate": np.stack([wlay(inp["ffn_w_gate"][l]) for l in range(4)]),
        "ffn_w_up": np.stack([wlay(inp["ffn_w_up"][l]) for l in range(4)]),
        "ffn_w_down": np.stack([wlay(inp["ffn_w_down"][l]) for l in range(4)]),
    }
    in_maps = []
    for core in range(NCORES):
        b, half = core // 2, core % 2
        xm = x[b, half * TM:(half + 1) * TM, :]
        xa = np.concatenate([xm, ctx[b]], axis=0)
        xt = np.ascontiguousarray(xa.T.reshape(KC, 128, TT).transpose(1, 0, 2))
        cst, tab, cb = host_consts(core, inp)
        m = {"xin": xt, "cst": cst, "tab": tab, "cstb": cb,
             "ada_w": np.stack([wlay(np.asarray(inp["ada_w"][l], np.float32)[:, half * 3072:(half + 1) * 3072]) for l in range(4)])}
        m.update(shared)
        in_maps.append(m)
    return in_maps


def assemble(results):
    out = np.zeros((4, SEQ, D), np.float32)
    for core in range(NCORES):
        b, half = core // 2, core % 2
        o = np.asarray(results[core]["out"], np.float32)
        out[b, half * TM:(half + 1) * TM, :] = o.transpose(1, 0, 2).reshape(D, TM).T
    return out


def kernel(**inputs):
    key = "full"
    if key not in _NC_CACHE:
        _NC_CACHE[key] = build_program()
    nc = _NC_CACHE[key]
    in_maps = prep_inputs(inputs)
    res = run_bass_kernel_spmd(nc, in_maps, core_ids=list(range(NCORES)))
    return assemble(res.results)
```

```python
import numpy as np
from contextlib import ExitStack
import concourse.bass as bass
import concourse.mybir as mybir
from concourse.bass_utils import run_bass_kernel_spmd

F32 = mybir.dt.float32
BF16 = mybir.dt.bfloat16
AF = mybir.ActivationFunctionType
ALU = mybir.AluOpType

ENGS = ("pe", "act", "dve", "pool", "sp")
NCORES = 8
DBG_DUMPS = False
D = 1024
KC = 8
TM = 2048
TCX = 256
TT = TM + TCX
SEQ = 4096
DFF = 2816
NFF = DFF // 128
EPS = 1e-6
BIG = 1.0e6
TILES = [(0, 512), (512, 512), (1024, 512), (1536, 512), (2048, 256)]
CTX_TILE = 4
HC = 15
HP = 8
XW = 1024 + 4 * 2 * HC
XWO = 4 * 2 * HP


class Op:
    __slots__ = ("eng", "fn", "deps", "sig", "done", "is_dma", "inc", "size", "idx")

    def __init__(self, eng, fn):
        self.eng = eng
        self.fn = fn
        self.deps = []
        self.sig = False
        self.done = None
        self.is_dma = None
        self.inc = 1
        self.size = 1 << 20
        self.idx = 0


class Prog:
    def __init__(self):
        self.ops = {e: [] for e in ENGS}
        self.last_w = {}
        self.readers = {}
        self.dma_keys = []

    def add(self, eng, fn, reads=(), writes=(), dma=None, inc=None, size=None):
        op = Op(eng, fn)
        op.idx = len(self.ops[eng])
        if size is not None:
            op.size = size
        if dma is not None:
            op.is_dma = dma
            op.inc = 16 if inc is None else inc
            if dma not in self.dma_keys:
                self.dma_keys.append(dma)
        xk = [k for k in reads if isinstance(k, tuple) and k[0] == "ps"]
        if xk:
            reads = [k for k in reads if k not in xk]
            writes = list(writes) + xk
        deps = []
        for k in reads:
            lw = self.last_w.get(k)
            if lw is not None:
                deps.append(lw)
        for k in writes:
            lw = self.last_w.get(k)
            if lw is not None:
                deps.append(lw)
            deps.extend(self.readers.get(k, ()))
        seen = set()
        for d in deps:
            if id(d) in seen or d is op:
                continue
            seen.add(id(d))
            if d.eng == eng and d.is_dma is None and dma is None:
                if eng == "pe" or d.size >= 512 or op.idx - d.idx > 6:
                    continue
            op.deps.append(d)
            d.sig = True
        for k in reads:
            lst = self.readers.setdefault(k, [])
            if dma is None:
                for i_, r_ in enumerate(lst):
                    if r_.is_dma is None and r_.eng == eng:
                        lst.pop(i_)
                        break
            lst.append(op)
        for k in writes:
            self.last_w[k] = op
            self.readers[k] = []
        self.ops[eng].append(op)
        return op

    def finalize(self):
        cnt = {}
        for e in ENGS:
            for op in self.ops[e]:
                if op.is_dma is not None:
                    key = ("dma", op.is_dma)
                    cnt[key] = cnt.get(key, 0) + op.inc
                    op.done = (key, cnt[key])
        for e in ENGS:
            c = 0
            for op in self.ops[e]:
                if op.is_dma is None and op.sig:
                    c += 1
                    op.done = (("eng", e), c)

    def emit(self, sems, e, h):
        waited = {}
        for op in self.ops[e]:
            need = {}
            for d in op.deps:
                k, v = d.done
                if waited.get(k, 0) >= v:
                    continue
                if need.get(k, 0) < v:
                    need[k] = v
            items = list(need.items())
            attach = None
            if items and op.is_dma is None:
                attach = items.pop()
            for k, v in items:
                h.wait_ge(sems[k], v)
                waited[k] = v
            ins = op.fn(h)
            if attach is not None:
                ins._wait_ge(sems[attach[0]], attach[1])
                waited[attach[0]] = attach[1]
            if op.is_dma is not None:
                ins.then_inc(sems[op.done[0]], op.inc)
            elif op.sig:
                ins.then_inc(sems[op.done[0]], 1)


class Lay:
    def __init__(self):
        self.off = {}
        self.n = 0

    def add(self, name, w):
        self.off[name] = self.n
        self.n += w

    def o(self, name):
        return self.off[name]


def make_layouts():
    L = Lay()
    L.add("flags", 4)
    L.add("zidx_m", 8)
    L.add("zidx_c", 4)
    L.add("cfidx", 8)
    L.add("logit", 16)
    L.add("svin", 16)
    L.add("adab", 4 * 48)
    L.add("normw", 4 * 16)
    L.add("fnw", 8)
    L.add("convw", 2 * 4 * 31)
    L.add("convlnw", 8)
    L.add("convlnb", 8)
    L.add("poolsc", 8)
    L.add("icm", 64)
    L.add("icc", 64)
    L.add("eps", 1)
    L.add("one", 1)
    L.add("lns", 1)
    LT = Lay()
    LT.add("ropeC", TT)
    LT.add("ropeS", TT)
    LT.add("a1idx", 896)
    LT.add("a2idx", 896)
    LT.add("xifidx", 512)
    LT.add("xibidx", 512)
    LT.add("sgtab", 2 * 1536)
    LT.add("pwsg", 2 * 1024)
    return L, LT


LAY, LAYT = make_layouts()
NCB = 384


def fm(v):
    v = np.asarray(v, np.float32)
    return np.ascontiguousarray(v.reshape(-1, 128).T)


def host_consts(core, inp):
    b, half = core // 2, core % 2
    L, LT = LAY, LAYT
    cst = np.zeros((128, L.n), np.float32)
    tab = np.zeros((128, LT.n), np.float32)

    def put(arr, lay, name, val):
        val = np.asarray(val, np.float32)
        o = lay.o(name)
        arr[:, o:o + val.shape[1]] = val

    pairs = (16, 24, 24)

    def angles(p_seq, p_row, p_col):
        parts = []
        for p_, n in zip((p_seq, p_row, p_col), pairs):
            freq = (np.float32(10000.0) ** (-np.arange(n, dtype=np.float32) / np.float32(n))).astype(np.float32)
            parts.append(p_[:, None].astype(np.float32) * freq[None, :])
        return np.concatenate(parts, axis=-1)

    t = np.arange(half * TM, (half + 1) * TM)
    ang_x = angles(np.full((TM,), float(TCX), np.float32), (t // 64).astype(np.float32), (t % 64).astype(np.float32))
    zc = np.zeros((TCX,), np.float32)
    ang_c = angles(np.arange(TCX, dtype=np.float32), zc, zc)
    ang = np.concatenate([ang_x, ang_c], axis=0).astype(np.float32)
    cos = np.cos(ang).T.astype(np.float32)
    sin = np.sin(ang).T.astype(np.float32)
    put(tab, LT, "ropeC", np.concatenate([cos, cos], axis=0))
    put(tab, LT, "ropeS", np.concatenate([-sin, sin], axis=0))
    p = np.arange(128)[:, None]
    i = np.arange(896)[None, :]
    dlt = i - p - 384
    put(tab, LT, "a1idx", np.where(dlt >= 0, dlt, BIG))
    put(tab, LT, "a2idx", np.where(dlt < 0, -dlt - 1, BIG))
    c = np.arange(512, dtype=np.float32)[None, :] + np.zeros((128, 1), np.float32)
    put(tab, LT, "xifidx", c + 1.0)
    put(tab, LT, "xibidx", 511.0 - c)
    sgt = np.zeros((128, 2 * 1536), np.float32)
    pws = np.zeros((128, 2 * 1024), np.float32)
    pw = np.asarray(inp["pool_w"], np.float32)
    sw = np.asarray(inp["sg_w"], np.float32)
    for l in range(2):
        sgt[:, l * 1536:l * 1536 + 512] = np.asarray(inp["sg_ln_w"][l], np.float32)[None, :]
        sgt[:, l * 1536 + 512:l * 1536 + 1024] = np.asarray(inp["sg_ln_b"][l], np.float32)[None, :]
        sgt[:, l * 1536 + 1024:l * 1536 + 1536] = np.asarray(inp["sg_b"][l], np.float32).reshape(1, 512)
        pws[:, l * 1024:l * 1024 + 512] = pw[l].transpose(1, 0, 2).reshape(128, 512)
        pws[:, l * 1024 + 512:l * 1024 + 1024] = sw[l].transpose(2, 0, 1).reshape(128, 512)
    put(tab, LT, "sgtab", sgt)
    put(tab, LT, "pwsg", pws)

    put(cst, L, "flags", np.tile(np.array([1.0 - half, float(half), 0, 0], np.float32)[None, :], (128, 1)))
    j = np.arange(4)[None, :]
    m = 128 * j + p
    put(cst, L, "zidx_m", np.concatenate([511.0 - m, m], axis=1))
    j2 = np.arange(2)[None, :]
    m2 = 128 * j2 + p
    put(cst, L, "zidx_c", np.concatenate([255.0 - m2, m2], axis=1))
    ti = np.arange(4, dtype=np.float32)
    put(cst, L, "cfidx", np.tile(np.concatenate([512.0 * ti, 512.0 * (3 - ti)])[None, :], (128, 1)))
    put(cst, L, "logit", np.tile(np.asarray(inp["ret_decay_logit"], np.float32).reshape(1, 16), (128, 1)))
    svin = np.zeros((128, 8, 2), np.float32)
    svin[:, :, 0] = fm(inp["c"][b])
    svin[:, :, 1] = fm(inp["c_ctx"])
    put(cst, L, "svin", svin.reshape(128, 16))
    put(cst, L, "adab", np.concatenate([fm(inp["ada_b"][l]) for l in range(4)], axis=1))
    put(cst, L, "normw", np.concatenate([fm(inp["norm_w"][l, s]) for l in range(4) for s in range(2)], axis=1))
    put(cst, L, "fnw", fm(inp["final_norm_w"]))
    cw = np.asarray(inp["conv_dw_w"], np.float32)
    put(cst, L, "convw", np.concatenate([cw[l][:, ch * 128:(ch + 1) * 128].T for l in range(2) for ch in range(4)], axis=1))
    put(cst, L, "convlnw", np.concatenate([fm(inp["conv_ln_w"][l]) for l in range(2)], axis=1))
    put(cst, L, "convlnb", np.concatenate([fm(inp["conv_ln_b"][l]) for l in range(2)], axis=1))
    put(cst, L, "poolsc", np.concatenate([fm(inp["pool_scale"][l]) for l in range(2)], axis=1))

    def invcnt(lo_edge, hi_edge, n):
        out = np.zeros((4, 16), np.float32)
        for gi, w in enumerate((2, 4, 8, 16)):
            left = w // 2
            right = w - 1 - left
            for k in range(8):
                cl = min(left, k) if lo_edge else left
                out[gi, k] = 1.0 / (cl + right + 1)
                tpos = n - 8 + k
                cr = min(right, n - 1 - tpos) if hi_edge else right
                out[gi, 8 + k] = 1.0 / (left + cr + 1)
        return np.tile(out.reshape(1, 64), (128, 1))
    put(cst, L, "icm", invcnt(half == 0, half == 1, TM))
    put(cst, L, "icc", invcnt(True, True, TCX))
    cst[:, L.o("eps")] = EPS
    cst[:, L.o("one")] = 1.0
    cst[:, L.o("lns")] = np.log(128.0 ** -0.5)
    cb = np.zeros((128, NCB), np.float32)
    cb[:, 0:128] = 1.0
    cb[:, 128:256] = np.eye(128, dtype=np.float32)
    for q in range(128):
        cb[(q + 64) % 128, 256 + q] = 1.0
    return cst, tab, cb


def wlay(w):
    w = np.asarray(w, np.float32)
    k, n = w.shape
    return np.ascontiguousarray(w.reshape(k // 128, 128, n).transpose(1, 0, 2))


def build_program(n_layers=4, debug_x=False):
    nc = bass.Bass("TRN2", target_bir_lowering=False)
    pg = Prog()
    es = ExitStack()
    E = es.enter_context

    def din(name, shape):
        return nc.dram_tensor(name, list(shape), F32, kind="ExternalInput").ap()

    xin = din("xin", [128, KC, TT])
    cst_d = din("cst", [128, LAY.n])
    tab_d = din("tab", [128, LAYT.n])
    cstb_d = din("cstb", [128, NCB])
    ada_d = din("ada_w", [4, 128, KC, 3072])
    evin_d = din("even_w_in", [2, 128, KC, 3072])
    evout_d = din("even_w_out", [2, 128, KC, D])
    odin_d = din("odd_w_in", [2, 128, KC, 1536])
    odout_d = din("odd_w_out", [2, 128, KC, D])
    wg_d = din("ffn_w_gate", [4, 128, KC, DFF])
    wu_d = din("ffn_w_up", [4, 128, KC, DFF])
    wd_d = din("ffn_w_down", [4, 128, NFF, D])
    out_d = nc.dram_tensor("out", [128, KC, TM], F32, kind="ExternalOutput").ap()

    def scratch(name, shape, dt=BF16):
        return nc.dram_tensor(name, list(shape), dt).ap()

    q_d = scratch("q_d", [128, 4, TT])
    k_d = scratch("k_d", [128, 4, TT])
    v_d = scratch("v_d", [128, TT // 128, 512])
    g_d = scratch("g_d", [128, 4, TT])
    um_d = scratch("um_d", [128, 4, TM + 2 * HC])
    uc_d = scratch("uc_d", [128, 4, TCX + 2 * HC])
    r_d = scratch("r_d", [128, 4, TT])
    pm_d = scratch("pm_d", [128, 4, TM + 2 * HP])
    pc_d = scratch("pc_d", [128, 4, TCX + 2 * HP])
    xi_d = nc.dram_tensor("xi_d", [128, XW], F32)
    xo_d = nc.dram_tensor("xo_d", [256, XW], F32)
    xio_d = nc.dram_tensor("xio_d", [128, XWO], F32)
    xoo_d = nc.dram_tensor("xoo_d", [256, XWO], F32)
    xm_d = nc.dram_tensor("xm_d", [128, 192], F32)
    xmo_d = nc.dram_tensor("xmo_d", [256, 192], F32)

    def sb(name, shape, dt=F32):
        return E(nc.sbuf_tensor(name, list(shape), dt))

    x_sb = sb("x_sb", [128, KC, TT])
    h_sb = sb("h_sb", [128, KC, TT], BF16)
    cst = sb("cst_sb", [128, LAY.n])
    cstb = sb("cstb_sb", [128, NCB], BF16)
    WSLOT = 8192
    w_sb = [sb("w_sb%d" % i, [128, WSLOT], BF16) for i in range(2)]
    modall = sb("modall", [128, 4, 2, 48])
    modx = sb("modx", [128, 192])
    modg = sb("modg", [128, 2, 192])
    nv_sb = sb("nv_sb", [128, 6, KC, 2])
    sv_sb = sb("sv_sb", [128, KC, 2], BF16)
    rs_sb = sb("rstd", [128, 512])
    TW = 512 + 2 * HP
    t32 = [sb("t32_%d" % i, [128, TW]) for i in range(3)]
    tb16 = [sb("tb16_%d" % i, [128, 512], BF16) for i in range(3)]
    o16 = [sb("o16_%d" % i, [128, 4, 512], BF16) for i in range(2)]
    l16 = [sb("l16_%d" % i, [128, 4, 512 + 2 * HC], BF16) for i in range(2)]
    l16x = sb("l16x", [128, 4, 512], BF16)
    acc32 = sb("acc32", [128, 4, 512])
    accf = acc32[:].rearrange("p a b -> p (a b)")
    mix_sb = sb("mix", [128, 8, 512], BF16)
    ropeT = mix_sb[:, 0:4, :].rearrange("p a b -> p (a b)").bitcast(F32)
    hid_sb = mix_sb[:, 4:8, :]
    lg_sb = sb("lg", [128, 16])
    dtab = sb("dtab", [128, 896], BF16)
    xi_sb = sb("xi", [128, 2, 512], BF16)
    zm_sb = sb("zm", [128, 2, 4, 4])
    zc_sb = sb("zc", [128, 2, 4, 2])
    dec_sb = sb("dec", [128, 2, 4])
    cf_sb = sb("cf", [128, 2, 4, 4])
    kz_sb = [sb("kz%d" % i, [128, 2, 128], BF16) for i in range(2)]
    S_sb = sb("S", [128, 2, 512])
    Sflat = S_sb[:].rearrange("p a b -> p (a b)")
    pwsg = Sflat.bitcast(BF16)[:, 0:1024]
    xh_sb = sb("xh", [128, 8 * HC])
    sm = sb("small", [128, 8])

    psA = [E(nc.psum_tensor("psA%d" % i, [128, 512], F32)) for i in range(7)]
    psT = E(nc.psum_tensor("psT", [128, 1024], BF16))
    PST = 7

    def C(name, a=0, n=1):
        o = LAY.o(name) + a
        return cst[:, o:o + n]

    ones = cstb[:, 0:128]
    ident = cstb[:, 128:256]
    perm = cstb[:, 256:384]
    epsc = C("eps")
    fA = C("flags", 0)
    fB = C("flags", 1)

    def PS(i):
        return ("ps", i)

    def fsz(ap):
        n = 1
        for v in list(ap.shape)[1:]:
            n *= int(v)
        return n

    def mm(out, lhsT, rhs, start, stop, reads, writes):
        pg.add("pe", lambda h: h.matmul(out, lhsT=lhsT, rhs=rhs, start=start, stop=stop), reads=reads, writes=writes)

    def act(out, in_, func, reads, writes, bias=None, scale=None, accum=None):
        kw = {}
        if bias is not None:
            kw["bias"] = bias
        if scale is not None:
            kw["scale"] = scale
        if accum is not None:
            kw["accum_out"] = accum
        pg.add("act", lambda h: h.activation(out=out, in_=in_, func=func, **kw), reads=reads, writes=writes, size=fsz(out))

    def tt(eng, out, in0, in1, op, reads, writes):
        pg.add(eng, lambda h: h.tensor_tensor(out=out, in0=in0, in1=in1, op=op), reads=reads, writes=writes, size=fsz(out))

    def ts(eng, out, in0, s1, s2, op0, op1, reads, writes):
        if s2 is None:
            pg.add(eng, lambda h: h.tensor_scalar(out=out, in0=in0, scalar1=s1, scalar2=None, op0=op0), reads=reads, writes=writes, size=fsz(out))
        else:
            pg.add(eng, lambda h: h.tensor_scalar(out=out, in0=in0, scalar1=s1, scalar2=s2, op0=op0, op1=op1), reads=reads, writes=writes, size=fsz(out))

    def stt(eng, out, in0, scalar, in1, op0, op1, reads, writes):
        pg.add(eng, lambda h: h.scalar_tensor_tensor(out=out, in0=in0, scalar=scalar, in1=in1, op0=op0, op1=op1), reads=reads, writes=writes, size=fsz(out))

    def cp(eng, out, in_, reads, writes):
        pg.add(eng, lambda h: h.tensor_copy(out=out, in_=in_), reads=reads, writes=writes, size=fsz(out))

    def recip(ap, key):
        pg.add("dve", lambda h: h.reciprocal(out=ap, in_=ap), [key], [key], size=fsz(ap))

    dma_prev = {}

    def dma(eng, out, in_, reads, writes, key):
        op = pg.add(eng, lambda h: h.dma_start(out=out, in_=in_), reads=reads, writes=writes, dma=key)
        prev = dma_prev.get(key)
        if prev is not None and prev not in op.deps:
            op.deps.append(prev)
            prev.sig = True
        dma_prev[key] = op
        return op

    dbg_list = []

    def dbg_dump(name, ap, keys):
        if not (debug_x and DBG_DUMPS):
            return
        shp = list(ap.shape)
        dt = ap.dtype
        dd = nc.dram_tensor("dbg_" + name, shp, dt, kind="ExternalOutput").ap()
        idx = tuple(slice(None) for _ in shp)
        dma("sp", dd[idx], ap, keys, ["dbgk_" + name], "dbg%d" % (len(dbg_list) % 4))
        dbg_list.append("dbgk_" + name)

    stc = {"n": 0}

    def stkey():
        stc["n"] += 1
        return "st%d" % (stc["n"] % 4)

    ldc = {"n": 0}

    def ldkey():
        ldc["n"] += 1
        return "ld%d" % (ldc["n"] % 4)

    dma("sp", cst[:], cst_d[:, :], [], ["cst"], "cst")
    dma("pool", cstb[:], cstb_d[:, :], [], ["cstb"], "cstb")
    for c in range(KC):
        dma("sp", x_sb[:, c, :], xin[:, c, :], [], [("x", c, t) for t in range(5)], ("xin", c % 4))
    pg.add("dve", lambda h: h.memset(o16[0][:, :, 0:16], 0.0), [], [("o16", 0)])
    dma("sp", uc_d[:, :, 0:HC], o16[0][:, :, 0:HC], [("o16", 0)], ["uc_h0"], stkey())
    dma("sp", uc_d[:, :, HC + TCX:], o16[0][:, :, 0:HC], [("o16", 0)], ["uc_h1"], stkey())
    dma("sp", pc_d[:, :, 0:HP], o16[0][:, :, 0:HP], [("o16", 0)], ["pc_h0"], stkey())
    dma("sp", pc_d[:, :, HP + TCX:], o16[0][:, :, 0:HP], [("o16", 0)], ["pc_h1"], stkey())
    act(sv_sb[:].rearrange("p a b -> p (a b)"), C("svin", 0, 16), AF.Silu, ["cst"], ["sv"])
    act(lg_sb[:], C("logit", 0, 16), AF.Exp, ["cst"], ["lg"], scale=-1.0)
    act(lg_sb[:], lg_sb[:], AF.Ln, ["lg", "cst"], ["lg"], bias=C("one"))
    ts("dve", lg_sb[:], lg_sb[:], -1.0, None, ALU.mult, None, ["lg"], ["lg"])

    wstate = {"n": 0}

    def wload(parts, slot=None):
        s = wstate["n"] % 2 if slot is None else slot
        wstate["n"] = s + 1
        views = []
        off = 0
        for pi, ap in enumerate(parts):
            shp = list(ap.shape)
            n = shp[1] * shp[2]
            v = w_sb[s][:, off:off + n].rearrange("p (a b) -> p a b", a=shp[1])
            dma("pool", v, ap, [], [("W", s)], ("W", s, pi))
            views.append(v)
            off += n
        assert off <= WSLOT
        return s, views

    def mod_prologue():
        for li in range(4):
            for pi in range(3):
                s, (wv,) = wload([ada_d[li, :, :, pi * 1024:(pi + 1) * 1024]])
                for cc in range(8):
                    bank = cc % 2
                    for kc in range(KC):
                        mm(psA[bank][:, 0:2], wv[:, kc, cc * 128:(cc + 1) * 128], sv_sb[:, kc, :],
                           kc == 0, kc == KC - 1, [("W", s), "sv"], [PS(bank)])
                    o = li * 48 + (pi * 8 + cc) * 2
                    cp("dve", modx[:, o:o + 2], psA[bank][:, 0:2], [PS(bank)], ["modx"])
        dma("pool", xm_d.ap(), modx[:, :], ["modx"], ["xm_d"], "xst0")
        allgather(xm_d, xmo_d, ["xm_d"], "xmo_d")
        dma("pool", modg[:], xmo_d.ap().rearrange("(r p) n -> p r n", p=128), ["xmo_d"], ["modg"], "xld0")
        G = modg[:].rearrange("p r (l c v) -> p r l c v", l=4, c=24)
        for li in range(4):
            ab = C("adab", li * 48, 48).rearrange("p (r c) -> p r c", r=2)
            for s_ in range(2):
                mx = modall[:, li, s_, :].rearrange("p (r c) -> p r c", r=2)
                tt("dve", mx, G[:, :, li, :, s_], ab, ALU.add, ["modg", "cst"], ["mod"])

    def mod_layer(li):
        for s_ in range(2):
            M = modall[:, li, s_, :]
            stt("dve", nv_sb[:, 0, :, s_], M[:, 8:16], 1.0, C("normw", li * 16, 8), ALU.add, ALU.mult, ["mod", "cst"], ["nv"])
            cp("dve", nv_sb[:, 1, :, s_], M[:, 0:8], ["mod"], ["nv"])
            cp("dve", nv_sb[:, 2, :, s_], M[:, 16:24], ["mod"], ["nv"])
            stt("dve", nv_sb[:, 3, :, s_], M[:, 32:40], 1.0, C("normw", li * 16 + 8, 8), ALU.add, ALU.mult, ["mod", "cst"], ["nv"])
            cp("dve", nv_sb[:, 4, :, s_], M[:, 24:32], ["mod"], ["nv"])
            cp("dve", nv_sb[:, 5, :, s_], M[:, 40:48], ["mod"], ["nv"])

    def tile_stream(ti):
        return 1 if ti == CTX_TILE else 0

    def rstd_from_ps(bank, n, scale):
        act(rs_sb[:, 0:n], psA[bank][:, 0:n], AF.Ln, [PS(bank), "cst"], ["rs"], bias=epsc, scale=scale)
        act(rs_sb[:, 0:n], rs_sb[:, 0:n], AF.Exp, ["rs"], ["rs"], scale=-0.5)

    def sumsq_x(ti):
        t0, n = TILES[ti]
        for c in range(KC):
            q = tb16[c % 2]
            act(q[:, 0:n], x_sb[:, c, t0:t0 + n], AF.Square, [("x", c, ti)], [("tb16", c % 2)])
            mm(psA[6][:, 0:n], ones, q[:, 0:n], c == 0, c == KC - 1, [("tb16", c % 2), "cstb"], [PS(6)])
        rstd_from_ps(6, n, 1.0 / D)

    def norm_tile(ti, which):
        t0, n = TILES[ti]
        s_ = tile_stream(ti)
        sumsq_x(ti)
        for c in range(KC):
            tmp = t32[c % 2]
            tt("dve", tmp[:, 0:n], x_sb[:, c, t0:t0 + n], rs_sb[:, 0:n], ALU.mult, [("x", c, ti), "rs"], [("t32", c % 2)])
            a = nv_sb[:, 3 * which, c, s_:s_ + 1]
            b = nv_sb[:, 3 * which + 1, c, s_:s_ + 1]
            act(h_sb[:, c, t0:t0 + n], tmp[:, 0:n], AF.Identity, [("t32", c % 2), "nv"], [("h", ti)], bias=b, scale=a)

    def final_tile(ti):
        t0, n = TILES[ti]
        sumsq_x(ti)
        for c in range(KC):
            tmp = t32[c % 2]
            tt("dve", tmp[:, 0:n], x_sb[:, c, t0:t0 + n], rs_sb[:, 0:n], ALU.mult, [("x", c, ti), "rs"], [("t32", c % 2)])
            act(acc32[:, c % 4, 0:n], tmp[:, 0:n], AF.Identity, [("t32", c % 2), "cst"], [("acc", c % 4)], scale=C("fnw", c))
            dma("sp", out_d[:, c, t0:t0 + n], acc32[:, c % 4, 0:n], [("acc", c % 4)], [("out", ti, c)], "out%d" % (c % 4))

    def proj_fm(bank, wv, s, col0, ti):
        t0, n = TILES[ti]
        for kc in range(KC):
            mm(psA[bank][:, 0:n], wv[:, kc, col0:col0 + 128], h_sb[:, kc, t0:t0 + n], kc == 0, kc == KC - 1,
               [("W", s), ("h", ti)], [PS(bank)])

    def load_rope(ti):
        t0, n = TILES[ti]
        dma("sp", ropeT[:, 0:n], tab_d[:, LAYT.o("ropeC") + t0:LAYT.o("ropeC") + t0 + n], [], [("mixs", 0), ("mixs", 1)], ldkey())
        dma("sp", ropeT[:, 512:512 + n], tab_d[:, LAYT.o("ropeS") + t0:LAYT.o("ropeS") + t0 + n], [], [("mixs", 2), ("mixs", 3)], ldkey())

    rope_ctr = {"n": 0}

    def rope(bank, ti, out_ap, okey):
        t0, n = TILES[ti]
        i = rope_ctr["n"] % 2
        rope_ctr["n"] += 1
        tb = tb16[i]
        act(tb[:, 0:n], psA[bank][:, 0:n], AF.Copy, [PS(bank)], [("tb16", i)])
        tt("dve", t32[2][:, 0:n], psA[bank][:, 0:n], ropeT[:, 0:n], ALU.mult, [PS(bank), ("mixs", 0), ("mixs", 1)], [("t32", 2)])
        mm(psA[5][:, 0:n], perm, tb[:, 0:n], True, True, [("tb16", i), "cstb"], [PS(5)])
        tt("dve", t32[i][:, 0:n], psA[5][:, 0:n], ropeT[:, 512:512 + n], ALU.mult, [PS(5), ("mixs", 2), ("mixs", 3)], [("t32", i)])
        tt("dve", out_ap, t32[i][:, 0:n], t32[2][:, 0:n], ALU.add, [("t32", i), ("t32", 2)], [okey])

    RG = [[0, 1], [2, 3], [4, 5], [6, 7]]

    def allgather(src, dst, skeys, dkey, groups=None):
        groups = RG if groups is None else groups
        pg.add("pool", lambda h: h.collective_compute("AllGather", ALU.bypass, replica_groups=groups,
                                                      ins=[src.ap().opt()], outs=[dst.ap().opt()]),
               list(skeys), [dkey], dma="cc", inc=1)

    def even_small_tables(j):
        lo = 8 * j
        for hd in range(4):
            for d_ in range(2):
                lx = lg_sb[:, lo + 4 * d_ + hd:lo + 4 * d_ + hd + 1]
                act(zm_sb[:, d_, hd, :], C("zidx_m", d_ * 4, 4), AF.Exp, ["cst", "lg"], ["zm"], scale=lx)
                act(zc_sb[:, d_, hd, :], C("zidx_c", d_ * 2, 2), AF.Exp, ["cst", "lg"], ["zm"], scale=lx)
                act(dec_sb[:, d_, hd:hd + 1], lx, AF.Exp, ["lg"], ["dec"], scale=512.0)
                act(cf_sb[:, d_, :, hd], C("cfidx", d_ * 4, 4), AF.Exp, ["cst", "lg"], ["cf"], scale=lx)
        ts("dve", cf_sb[:, 0, :, :], cf_sb[:, 0, :, :], fB, None, ALU.mult, None, ["cf", "cst"], ["cf"])
        ts("dve", cf_sb[:, 1, :, :], cf_sb[:, 1, :, :], fA, None, ALU.mult, None, ["cf", "cst"], ["cf"])

    def head_tables(j, hd):
        lo = 8 * j
        lf = lg_sb[:, lo + hd:lo + hd + 1]
        lb = lg_sb[:, lo + 4 + hd:lo + 4 + hd + 1]
        acck = [("acc", i) for i in range(4)]
        dma("sp", accf[:, 0:896], tab_d[:, LAYT.o("a1idx"):LAYT.o("a1idx") + 896], [], acck, ldkey())
        dma("sp", accf[:, 1024:1920], tab_d[:, LAYT.o("a2idx"):LAYT.o("a2idx") + 896], [], acck, ldkey())
        act(accf[:, 0:896], accf[:, 0:896], AF.Exp, acck + ["lg", "cst"], acck, scale=lf, bias=C("lns"))
        act(accf[:, 1024:1920], accf[:, 1024:1920], AF.Exp, acck + ["lg", "cst"], acck, scale=lb, bias=C("lns"))
        tt("dve", dtab[:, :], accf[:, 0:896], accf[:, 1024:1920], ALU.add, acck, ["dtab"])
        dma("sp", accf[:, 0:512], tab_d[:, LAYT.o("xifidx"):LAYT.o("xifidx") + 512], [], acck, ldkey())
        dma("sp", accf[:, 512:1024], tab_d[:, LAYT.o("xibidx"):LAYT.o("xibidx") + 512], [], acck, ldkey())
        act(xi_sb[:, 0, :], accf[:, 0:512], AF.Exp, acck + ["lg", "cst"], ["xi"], scale=lf, bias=C("lns"))
        act(xi_sb[:, 1, :], accf[:, 512:1024], AF.Exp, acck + ["lg", "cst"], ["xi"], scale=lb, bias=C("lns"))

    def even_mixer(li, full_ctx):
        j = li // 2
        even_small_tables(j)
        order = [CTX_TILE, 0, 1, 2, 3]
        tiles_full = order if full_ctx else [0, 1, 2, 3]
        for ti in order:
            if ti not in tiles_full:
                norm_tile(ti, 0)
        norm_tile(tiles_full[0], 0)
        s2, (wa, wgb) = wload([evin_d[j, :, :, 2048:2560], evin_d[j, :, :, 2560:3072]])
        for i_, ti in enumerate(tiles_full):
            if i_ + 1 < len(tiles_full):
                norm_tile(tiles_full[i_ + 1], 0)
            t0, n = TILES[ti]
            ub = o16[ti % 2]
            for ch in range(4):
                b0 = (2 * ch) % 4
                b1 = b0 + 1
                proj_fm(b0, wa, s2, ch * 128, ti)
                proj_fm(b1, wgb, s2, ch * 128, ti)
                act(t32[ch % 2][:, 0:n], psA[b1][:, 0:n], AF.Sigmoid, [PS(b1)], [("t32", ch % 2)])
                tt("dve", ub[:, ch, 0:n], psA[b0][:, 0:n], t32[ch % 2][:, 0:n], ALU.mult, [PS(b0), ("t32", ch % 2)], [("o16", ti % 2)])
            if ti == CTX_TILE:
                dma("sp", uc_d[:, :, HC:HC + n], ub[:, :, 0:n], [("o16", ti % 2)], [("u_d", ti)], stkey())
            else:
                dma("sp", um_d[:, :, HC + t0:HC + t0 + n], ub[:, :, 0:n], [("o16", ti % 2)], [("u_d", ti)], stkey())
            if ti == 0:
                cp("dve", xh_sb[:, 0:4 * HC].rearrange("p (a b) -> p a b", a=4), ub[:, :, 0:HC], [("o16", ti % 2)], ["xh"])
            if ti == 3:
                cp("dve", xh_sb[:, 4 * HC:8 * HC].rearrange("p (a b) -> p a b", a=4), ub[:, :, n - HC:n], [("o16", ti % 2)], ["xh"])
        s3, (wk, wvv) = wload([evin_d[j, :, :, 512:1024], evin_d[j, :, :, 1024:1536]], slot=1 - s2)
        so = 1 - s3
        sp32 = w_sb[so][:, :].bitcast(F32).rearrange("p (a b c) -> p a b c", a=2, b=4)
        SPK = ("W", so)
        for ti in order:
            t0, n = TILES[ti]
            nb = n // 128
            load_rope(ti)
            ob = o16[0]
            for jb in range(nb):
                bank = jb % 2
                for kc in range(KC):
                    mm(psA[bank][:, :], h_sb[:, kc, t0 + jb * 128:t0 + (jb + 1) * 128], wvv[:, kc, :], kc == 0, kc == KC - 1,
                       [("W", s3), ("h", ti)], [PS(bank)])
                act(ob[:, jb, :], psA[bank][:, :], AF.Copy, [PS(bank)], [("o16", 0)])
            dma("sp", v_d[:, t0 // 128:t0 // 128 + nb, :], ob[:, 0:nb, :], [("o16", 0)], [("v_d", ti)], stkey())
            kb = o16[1]
            zt = zc_sb if ti == CTX_TILE else zm_sb
            for hd in range(4):
                bank = 2 + hd % 2
                proj_fm(bank, wk, s3, hd * 128, ti)
                rope(bank, ti, kb[:, hd, 0:n], ("o16", 1))
                for jb in range(nb):
                    pg.add("pe", lambda h, hd=hd, jb=jb: h.transpose(psT[:, jb * 128:(jb + 1) * 128], kb[:, hd, jb * 128:(jb + 1) * 128], ident),
                           [("o16", 1), "cstb"], [PS(PST)])
                for d_ in range(2):
                    kvbank = 4 if d_ == 0 else 6
                    for jb in range(nb):
                        kz = kz_sb[jb % 2]
                        ts("dve", kz[:, d_, :], psT[:, jb * 128:(jb + 1) * 128], zt[:, d_, hd, jb:jb + 1], None, ALU.mult, None,
                           [PS(PST), "zm"], [("kz", jb % 2, d_)])
                        mm(psA[kvbank][:, hd * 128:(hd + 1) * 128], kz[:, d_, :], ob[:, jb, hd * 128:(hd + 1) * 128], jb == 0, jb == nb - 1,
                           [("kz", jb % 2, d_), ("o16", 0)], [PS(kvbank)])
            dma("sp", k_d[:, :, t0:t0 + n], kb[:, :, 0:n], [("o16", 1)], [("k_d", ti)], stkey())
            if ti == CTX_TILE:
                ts("dve", S_sb[:, 0, :], psA[4][:, :], fA, None, ALU.mult, None, [PS(4), "cst"], ["S"])
                ts("dve", S_sb[:, 1, :], psA[6][:, :], fB, None, ALU.mult, None, [PS(6), "cst"], ["S"])
            else:
                cp("dve", sp32[:, 0, ti, :], S_sb[:, 0, :], ["S"], [SPK])
                for hd in range(4):
                    hs = slice(hd * 128, (hd + 1) * 128)
                    stt("dve", S_sb[:, 0, hs], S_sb[:, 0, hs], dec_sb[:, 0, hd:hd + 1], psA[4][:, hs], ALU.mult, ALU.add,
                        ["S", "dec", PS(4)], ["S"])
                cp("dve", acc32[:, ti, :], psA[6][:, :], [PS(6)], [("acc", ti)])
        for ti in (3, 2, 1, 0):
            cp("dve", sp32[:, 1, ti, :], S_sb[:, 1, :], ["S"], [SPK])
            for hd in range(4):
                hs = slice(hd * 128, (hd + 1) * 128)
                stt("dve", S_sb[:, 1, hs], S_sb[:, 1, hs], dec_sb[:, 1, hd:hd + 1], acc32[:, ti, hs], ALU.mult, ALU.add,
                    ["S", "dec", ("acc", ti)], ["S"])
        sq_, (wq, wgt) = wload([evin_d[j, :, :, 0:512], evin_d[j, :, :, 1536:2048]], slot=s3)
        cwo = j * 124
        HK = [("h", t) for t in range(5)]
        dgf = h_sb[:].rearrange("p a b -> p (a b)")
        dma("pool", xi_d.ap()[:, 0:1024], Sflat, ["S"], ["xi_d"], "xst0")
        dma("pool", xi_d.ap()[:, 1024:XW], xh_sb[:, :], ["xh"], ["xi_d2"], "xst1")
        allgather(xi_d, xo_d, ["xi_d", "xi_d2"], "xo_d")
        acck = [("acc", i) for i in range(4)]
        dma("pool", accf[:, 0:512], xo_d.ap()[0:128, 0:512], ["xo_d"], acck, "xld0")
        dma("pool", accf[:, 512:1024], xo_d.ap()[128:256, 512:1024], ["xo_d"], acck, "xld1")
        dma("pool", accf[:, 1024:1024 + 4 * HC], xo_d.ap()[0:128, 1024 + 4 * HC:XW], ["xo_d"], acck, "xld0")
        dma("pool", accf[:, 1024 + 4 * HC:XW], xo_d.ap()[128:256, 1024:1024 + 4 * HC], ["xo_d"], acck, "xld1")
        for i_, ti in enumerate(tiles_full):
            t0, n = TILES[ti]
            load_rope(ti)
            for hd in range(4):
                bank = hd % 2
                proj_fm(bank, wq, sq_, hd * 128, ti)
                rope(bank, ti, o16[0][:, hd, 0:n], ("o16", 0))
                bank = 2 + hd % 2
                proj_fm(bank, wgt, sq_, hd * 128, ti)
                act(o16[1][:, hd, 0:n], psA[bank][:, 0:n], AF.Silu, [PS(bank)], [("o16", 1)])
            dma("sp", q_d[:, :, t0:t0 + n], o16[0][:, :, 0:n], [("o16", 0)], [("q_d", ti)], stkey())
            dma("sp", g_d[:, :, t0:t0 + n], o16[1][:, :, 0:n], [("o16", 1)], [("g_d", ti)], stkey())
        for d_ in range(2):
            for ti in range(4):
                for hd in range(4):
                    hs = slice(hd * 128, (hd + 1) * 128)
                    stt("dve", o16[d_][:, ti, hs], accf[:, d_ * 512 + hd * 128:d_ * 512 + (hd + 1) * 128], cf_sb[:, d_, ti, hd:hd + 1],
                        sp32[:, d_, ti, hs], ALU.mult, ALU.add, acck + ["cf", SPK], [("o16", d_)])
        hl = tb16[2][:, 0:8 * HC].rearrange("p (a b) -> p a b", a=4)
        ts("dve", hl[:, :, 0:HC], accf[:, 1024:1024 + 4 * HC].rearrange("p (a b) -> p a b", a=4), fB, None, ALU.mult, None,
           acck + ["cst"], [("tb16", 2)])
        ts("dve", hl[:, :, HC:2 * HC], accf[:, 1024 + 4 * HC:XW].rearrange("p (a b) -> p a b", a=4), fA, None, ALU.mult, None,
           acck + ["cst"], [("tb16", 2)])
        dma("sp", um_d[:, :, 0:HC], hl[:, :, 0:HC], [("tb16", 2)], [("u_d", 0)], stkey())
        dma("sp", um_d[:, :, HC + TM:], hl[:, :, HC:2 * HC], [("tb16", 2)], [("u_d", 3)], stkey())
        s4, (wo,) = wload([evout_d[j, :, :, :]], slot=so)
        l16b = l16x
        LQ = [l16[0][:, 0, 0:512], l16b[:, 0, 0:512]]
        LK = [l16[0][:, 1, 0:512], l16b[:, 1, 0:512]]
        LG = [l16[0][:, 2, 0:512], l16b[:, 2, 0:512]]
        RB = [l16[0][:, 3, 0:512], l16b[:, 3, 0:512]]
        LV = [l16[1][:, :, 0:128], l16[1][:, :, 128:256]]
        its = [(hd, ti) for hd in range(4) for ti in tiles_full]

        def ret_loads(it):
            hd, ti = its[it]
            sset = it % 2
            t0, n = TILES[ti]
            nb = n // 128
            hs = slice(hd * 128, (hd + 1) * 128)
            dma("sp", LQ[sset][:, 0:n], q_d[:, hd, t0:t0 + n], [("q_d", ti)], [("lq", sset)], ldkey())
            dma("sp", LK[sset][:, 0:n], k_d[:, hd, t0:t0 + n], [("k_d", ti)], [("lk", sset)], ldkey())
            dma("sp", LV[sset][:, 0:nb, :], v_d[:, t0 // 128:t0 // 128 + nb, hs], [("v_d", ti)], [("lv", sset)], ldkey())

        def ret_load_g(it):
            hd, ti = its[it]
            sset = it % 2
            t0, n = TILES[ti]
            dma("sp", LG[sset][:, 0:n], g_d[:, hd, t0:t0 + n], [("g_d", ti)], [("lg_", sset)], ldkey())

        L0K = [("lq", 0), ("lk", 0), ("lg_", 0), ("rb", 0)]
        L1K = [("lv", 0), ("lv", 1)]
        def stage_a(it):
            hd, ti = its[it]
            sset = it % 2
            hs = slice(hd * 128, (hd + 1) * 128)
            if ti == tiles_full[0]:
                head_tables(j, hd)
            t0, n = TILES[ti]
            nb = n // 128
            inter = ti != CTX_TILE
            lq, lk, lv = LQ[sset], LK[sset], LV[sset]
            if inter:
                tt("dve", tb16[0][:, 0:n], lq[:, 0:n], xi_sb[:, 0, 0:n], ALU.mult, [("lq", sset), "xi"], [("tb16", 0)])
                tt("dve", tb16[1][:, 0:n], lq[:, 0:n], xi_sb[:, 1, 0:n], ALU.mult, [("lq", sset), "xi"], [("tb16", 1)])
            for jb in range(nb):
                mm(psA[jb][:, 0:n], lk[:, jb * 128:(jb + 1) * 128], lq[:, 0:n], True, True, [("lk", sset), ("lq", sset)], [PS(jb)])
                st0 = 384 - 128 * jb
                tt("dve", mix_sb[:, jb, 0:n], psA[jb][:, 0:n], dtab[:, st0:st0 + n], ALU.mult, [PS(jb), "dtab"], [("mixs", jb)])
            yb = 4 + it % 2
            nmm = nb + (2 if inter else 0)
            for jb in range(nb):
                mm(psA[yb][:, 0:n], lv[:, jb, :], mix_sb[:, jb, 0:n], jb == 0, jb == nmm - 1, [("lv", sset), ("mixs", jb)], [PS(yb)])
            if inter:
                mm(psA[yb][:, 0:n], o16[0][:, ti, hs], tb16[0][:, 0:n], False, False, [("o16", 0), ("tb16", 0)], [PS(yb)])
                mm(psA[yb][:, 0:n], o16[1][:, ti, hs], tb16[1][:, 0:n], False, True, [("o16", 1), ("tb16", 1)], [PS(yb)])

        def stage_b(it):
            hd, ti = its[it]
            sset = it % 2
            t0, n = TILES[ti]
            yb = 4 + it % 2
            lgt, rb = LG[sset], RB[sset]
            act(tb16[2][:, 0:n], psA[yb][:, 0:n], AF.Square, [PS(yb)], [("tb16", 2)])
            mm(psA[6][:, 0:n], ones, tb16[2][:, 0:n], True, True, [("tb16", 2), "cstb"], [PS(6)])
            rstd_from_ps(6, n, 1.0 / 128)
            tt("dve", t32[0][:, 0:n], psA[yb][:, 0:n], rs_sb[:, 0:n], ALU.mult, [PS(yb), "rs"], [("t32", 0)])
            tt("dve", rb[:, 0:n], t32[0][:, 0:n], lgt[:, 0:n], ALU.mult, [("t32", 0), ("lg_", sset)], [("rb", sset)])
            dma("sp", r_d[:, hd, t0:t0 + n], rb[:, 0:n], [("rb", sset)], [("r_d", ti, hd)], stkey())
            if ti == tiles_full[-1]:
                for idx in range(hd * 31, (hd + 1) * 31):
                    act(dgf[:, idx * 128:(idx + 1) * 128], ident, AF.Identity, ["cstb", "cst"], HK, scale=C("convw", cwo + idx))

        ret_loads(0)
        ret_load_g(0)
        if len(its) > 1:
            ret_loads(1)
            ret_load_g(1)
        stage_a(0)
        for it in range(len(its)):
            if it + 2 < len(its):
                ret_loads(it + 2)
            if it + 1 < len(its):
                stage_a(it + 1)
            stage_b(it)
            if it + 2 < len(its):
                ret_load_g(it + 2)
        for ti in tiles_full:
            t0, n = TILES[ti]
            lu, lr = l16[0], l16[1]
            lk0 = L0K
            if ti == CTX_TILE:
                dma("sp", lu[:, :, 0:n + 2 * HC], uc_d[:, :, :], [("u_d", ti), "uc_h0", "uc_h1"], lk0, ldkey())
            else:
                rk = [("u_d", ti)] + ([("u_d", ti - 1)] if ti > 0 else []) + ([("u_d", ti + 1)] if ti < 3 else [])
                dma("sp", lu[:, :, 0:n + 2 * HC], um_d[:, :, t0:t0 + n + 2 * HC], rk, lk0, ldkey())
            dma("sp", lr[:, :, 0:n], r_d[:, :, t0:t0 + n], [("r_d", ti, hd) for hd in range(4)], L1K, ldkey())
            for ch in range(4):
                for k in range(31):
                    idx = ch * 31 + k
                    mm(psA[ch][:, 0:n], dgf[:, idx * 128:(idx + 1) * 128], lu[:, ch, k:k + n], k == 0, k == 30, HK + lk0, [PS(ch)])
            for ch in range(4):
                act(tb16[ch % 2][:, 0:n], psA[ch][:, 0:n], AF.Copy, [PS(ch)], [("tb16", ch % 2)])
                mm(psA[4][:, 0:n], ones, tb16[ch % 2][:, 0:n], ch == 0, ch == 3, [("tb16", ch % 2), "cstb"], [PS(4)])
            for ch in range(4):
                act(tb16[ch % 2][:, 0:n], psA[ch][:, 0:n], AF.Square, [PS(ch)], [("tb16", ch % 2)])
                mm(psA[5][:, 0:n], ones, tb16[ch % 2][:, 0:n], ch == 0, ch == 3, [("tb16", ch % 2), "cstb"], [PS(5)])
            act(t32[0][:, 0:n], psA[4][:, 0:n], AF.Copy, [PS(4)], [("t32", 0)], scale=1.0 / 512)
            tt("dve", t32[1][:, 0:n], t32[0][:, 0:n], t32[0][:, 0:n], ALU.mult, [("t32", 0)], [("t32", 1)])
            stt("dve", t32[1][:, 0:n], psA[5][:, 0:n], 1.0 / 512, t32[1][:, 0:n], ALU.mult, ALU.subtract, [PS(5), ("t32", 1)], [("t32", 1)])
            act(rs_sb[:, 0:n], t32[1][:, 0:n], AF.Ln, [("t32", 1), "cst"], ["rs"], bias=epsc)
            act(rs_sb[:, 0:n], rs_sb[:, 0:n], AF.Exp, ["rs"], ["rs"], scale=-0.5)
            for ch in range(4):
                tt("dve", acc32[:, ch, 0:n], psA[ch][:, 0:n], t32[0][:, 0:n], ALU.subtract, [PS(ch), ("t32", 0)], [("acc", ch)])
                tt("dve", acc32[:, ch, 0:n], acc32[:, ch, 0:n], rs_sb[:, 0:n], ALU.mult, [("acc", ch), "rs"], [("acc", ch)])
                act(mix_sb[:, 4 + ch, 0:n], acc32[:, ch, 0:n], AF.Silu, [("acc", ch), "cst"], [("mixs", 4 + ch)],
                    bias=C("convlnb", j * 4 + ch), scale=C("convlnw", j * 4 + ch))
            wout_tile(ti, wo, s4, [lr[:, kc, :] for kc in range(4)] + [mix_sb[:, 4 + kc, :] for kc in range(4)],
                      [("lv", 0)] * 4 + [("mixs", 4 + kc) for kc in range(4)])

    def wout_tile(ti, wo, s, rhs_list, rkeys):
        t0, n = TILES[ti]
        s_ = tile_stream(ti)
        for oc in range(KC):
            bank = 4 + oc % 3
            for kc in range(KC):
                mm(psA[bank][:, 0:n], wo[:, kc, oc * 128:(oc + 1) * 128], rhs_list[kc][:, 0:n], kc == 0, kc == KC - 1,
                   [("W", s), rkeys[kc]], [PS(bank)])
            stt("dve", x_sb[:, oc, t0:t0 + n], psA[bank][:, 0:n], nv_sb[:, 2, oc, s_:s_ + 1], x_sb[:, oc, t0:t0 + n], ALU.mult, ALU.add,
                [PS(bank), "nv"], [("x", oc, ti)])

    def pool_window(ti, lp, lkeys):
        t0, n = TILES[ti]
        W_ = n + 2 * HP
        icn = "icc" if ti == CTX_TILE else "icm"
        for g, w in enumerate((2, 4, 8, 16)):
            left = w // 2
            right = w - 1 - left
            cur, oth = 0, 1
            tt("dve", t32[cur][:, 1:W_], lp[:, g, 1:W_], lp[:, g, 0:W_ - 1], ALU.add, lkeys, [("t32", cur)])
            k = 2
            lo = 1
            while k < w:
                tt("dve", t32[oth][:, lo + k:W_], t32[cur][:, lo + k:W_], t32[cur][:, lo:W_ - k], ALU.add, [("t32", cur)], [("t32", oth)])
                lo += k
                k *= 2
                cur, oth = oth, cur
            e0 = HP + right
            ts("dve", acc32[:, g, 0:n], t32[cur][:, e0:e0 + n], 1.0 / w, None, ALU.mult, None, [("t32", cur)], [("acc", g)])
            if ti == 0 or ti == CTX_TILE:
                tt("dve", acc32[:, g, 0:8], t32[cur][:, e0:e0 + 8], C(icn, g * 16, 8), ALU.mult, [("t32", cur), "cst"], [("acc", g)])
            if ti == 3 or ti == CTX_TILE:
                tt("dve", acc32[:, g, n - 8:n], t32[cur][:, e0 + n - 8:e0 + n], C(icn, g * 16 + 8, 8), ALU.mult, [("t32", cur), "cst"], [("acc", g)])
            tt("dve", mix_sb[:, g, 0:n], acc32[:, g, 0:n], lp[:, g, HP:HP + n], ALU.subtract, [("acc", g)] + lkeys, [("mixs", g)])

    def odd_mixer(li, tiles):
        j = li // 2
        norm_tile(tiles[0], 0)
        dma("pool", pwsg, tab_d[:, LAYT.o("pwsg") + j * 1024:LAYT.o("pwsg") + (j + 1) * 1024], [], ["S"], "tbl")
        acck = [("acc", i) for i in range(4)]
        dma("sp", accf[:, 0:1536], tab_d[:, LAYT.o("sgtab") + j * 1536:LAYT.o("sgtab") + (j + 1) * 1536], [], acck, ldkey())
        lnw = accf[:, 0:512]
        lnb = accf[:, 512:1024]
        s, (wpc,) = wload([odin_d[j, :, :, 0:512]])
        for i_, ti in enumerate(tiles):
            if i_ + 1 < len(tiles):
                norm_tile(tiles[i_ + 1], 0)
            t0, n = TILES[ti]
            pb = o16[ti % 2]
            for g in range(4):
                bank = g % 4
                proj_fm(bank, wpc, s, g * 128, ti)
                act(pb[:, g, 0:n], psA[bank][:, 0:n], AF.Copy, [PS(bank)], [("o16", ti % 2)])
            if ti == CTX_TILE:
                dma("sp", pc_d[:, :, HP:HP + n], pb[:, :, 0:n], [("o16", ti % 2)], [("p_d", ti)], stkey())
            else:
                dma("sp", pm_d[:, :, HP + t0:HP + t0 + n], pb[:, :, 0:n], [("o16", ti % 2)], [("p_d", ti)], stkey())
            if ti == 0:
                cp("dve", xh_sb[:, 0:4 * HP].rearrange("p (a b) -> p a b", a=4), pb[:, :, 0:HP], [("o16", ti % 2)], ["xh"])
            if ti == 3:
                cp("dve", xh_sb[:, 4 * HP:8 * HP].rearrange("p (a b) -> p a b", a=4), pb[:, :, n - HP:n], [("o16", ti % 2)], ["xh"])
        s, (wu_, wv_) = wload([odin_d[j, :, :, 512:1024], odin_d[j, :, :, 1024:1536]])
        dma("pool", xio_d.ap(), xh_sb[:, 0:XWO], ["xh"], ["xio_d"], "xst0")
        allgather(xio_d, xoo_d, ["xio_d"], "xoo_d")
        dma("pool", rs_sb[:, 0:4 * HP], xoo_d.ap()[0:128, 4 * HP:8 * HP], ["xoo_d"], ["rs"], "xld0")
        dma("pool", rs_sb[:, 4 * HP:8 * HP], xoo_d.ap()[128:256, 0:4 * HP], ["xoo_d"], ["rs"], "xld1")
        hl = tb16[2][:, 0:8 * HP].rearrange("p (a b) -> p a b", a=4)
        ts("dve", hl[:, :, 0:HP], rs_sb[:, 0:4 * HP].rearrange("p (a b) -> p a b", a=4), fB, None, ALU.mult, None, ["rs", "cst"], [("tb16", 2)])
        ts("dve", hl[:, :, HP:2 * HP], rs_sb[:, 4 * HP:8 * HP].rearrange("p (a b) -> p a b", a=4), fA, None, ALU.mult, None, ["rs", "cst"], [("tb16", 2)])
        dma("sp", pm_d[:, :, 0:HP], hl[:, :, 0:HP], [("tb16", 2)], [("p_d", 0)], stkey())
        dma("sp", pm_d[:, :, HP + TM:], hl[:, :, HP:2 * HP], [("tb16", 2)], [("p_d", 3)], stkey())
        for ti in tiles:
            t0, n = TILES[ti]
            nb = n // 128
            ub = o16[0]
            sb_ = o16[1]
            for g in range(4):
                bank = g % 2
                proj_fm(bank, wu_, s, g * 128, ti)
                act(ub[:, g, 0:n], psA[bank][:, 0:n], AF.Gelu, [PS(bank)], [("o16", 0)])
            def S1(jb, ti=ti, t0=t0):
                bank = 2 + jb % 2
                for kc in range(KC):
                    mm(psA[bank][:, :], h_sb[:, kc, t0 + jb * 128:t0 + (jb + 1) * 128], wv_[:, kc, :], kc == 0, kc == KC - 1,
                       [("W", s), ("h", ti)], [PS(bank)])
                z = t32[jb % 2]
                zk = ("t32", jb % 2)
                pb_ = jb % 2
                sq = l16x[:, pb_, :]
                sqk = ("sqx", pb_)
                smo = 4 * pb_
                smk = ("sm", pb_)
                act(z[:, 0:512], psA[bank][:, :], AF.Gelu, [PS(bank)], [zk])
                act(sq, z[:, 0:512], AF.Square, [zk], [sqk])
            def S2(jb, ti=ti, t0=t0):
                z = t32[jb % 2]
                zk = ("t32", jb % 2)
                pb_ = jb % 2
                sq = l16x[:, pb_, :]
                sqk = ("sqx", pb_)
                smo = 4 * pb_
                smk = ("sm", pb_)
                pg.add("dve", lambda h, z=z, smo=smo: h.reduce_sum(out=sm[:, smo:smo + 1], in_=z[:, 0:512], axis=mybir.AxisListType.X), [zk], [smk], size=1)
                pg.add("dve", lambda h, sq=sq, smo=smo: h.reduce_sum(out=sm[:, smo + 1:smo + 2], in_=sq, axis=mybir.AxisListType.X), [sqk], [smk], size=1)
                ts("dve", sm[:, smo + 2:smo + 3], sm[:, smo:smo + 1], 1.0 / 512, None, ALU.mult, None, [smk], [smk])
                tt("dve", sm[:, smo + 3:smo + 4], sm[:, smo + 2:smo + 3], sm[:, smo + 2:smo + 3], ALU.mult, [smk], [smk])
                stt("dve", sm[:, smo + 3:smo + 4], sm[:, smo + 1:smo + 2], 1.0 / 512, sm[:, smo + 3:smo + 4], ALU.mult, ALU.subtract, [smk], [smk])
                act(sm[:, smo + 3:smo + 4], sm[:, smo + 3:smo + 4], AF.Sqrt, [smk, "cst"], [smk], bias=epsc)
                recip(sm[:, smo + 3:smo + 4], smk)
                ts("dve", z[:, 0:512], z[:, 0:512], sm[:, smo + 2:smo + 3], sm[:, smo + 3:smo + 4], ALU.subtract, ALU.mult, [smk, zk], [zk])
                tt("dve", z[:, 0:512], z[:, 0:512], lnw, ALU.mult, [zk] + acck, [zk])
                vb = tb16[jb % 2]
                tt("dve", vb[:, :], z[:, 0:512], lnb, ALU.add, [zk] + acck, [("tb16", jb % 2)])
                for g in range(4):
                    bank2 = 4 + g % 2
                    tg = t32[2][:, g * 128:(g + 1) * 128]
                    mm(psA[bank2][:, 0:128], vb[:, g * 128:(g + 1) * 128], pwsg[:, 512 + g * 128:512 + (g + 1) * 128], True, True,
                       [("tb16", jb % 2), "S"], [PS(bank2)])
                    tt("dve", tg, psA[bank2][:, 0:128], accf[:, 1024 + g * 128:1024 + (g + 1) * 128], ALU.add,
                       [PS(bank2)] + acck, [("t32", 2, g)])
                    tt("dve", sb_[:, g, jb * 128:(jb + 1) * 128], tg, ub[:, g, jb * 128:(jb + 1) * 128], ALU.mult,
                       [("t32", 2, g), ("o16", 0)], [("o16", 1)])
            S1(0)
            for jb in range(nb):
                if jb + 1 < nb:
                    S1(jb + 1)
                S2(jb)
            dma("sp", r_d[:, :, t0:t0 + n], sb_[:, :, 0:n], [("o16", 1)], [("r_d", ti)], stkey())
        s, (wo,) = wload([odout_d[j, :, :, :]])
        for ti in tiles:
            t0, n = TILES[ti]
            lp, lr = l16[0], l16[1]
            lk0 = [("lq", 0), ("lk", 0), ("lg_", 0), ("rb", 0)]
            if ti == CTX_TILE:
                dma("sp", lp[:, :, 0:n + 2 * HP], pc_d[:, :, :], [("p_d", ti), "pc_h0", "pc_h1"], lk0, ldkey())
            else:
                rk = [("p_d", ti)] + ([("p_d", ti - 1)] if ti > 0 else []) + ([("p_d", ti + 1)] if ti < 3 else [])
                dma("sp", lp[:, :, 0:n + 2 * HP], pm_d[:, :, t0:t0 + n + 2 * HP], rk, lk0, ldkey())
            dma("sp", lr[:, :, 0:n], r_d[:, :, t0:t0 + n], [("r_d", ti)], [("lv", 0), ("lv", 1)], ldkey())
            pool_window(ti, lp, lk0)
            for g in range(4):
                bank = g % 2
                mm(psA[bank][:, 0:n], pwsg[:, g * 128:(g + 1) * 128], mix_sb[:, g, 0:n], True, True, [("mixs", g), "S"], [PS(bank)])
                ts("dve", mix_sb[:, 4 + g, 0:n], psA[bank][:, 0:n], C("poolsc", j * 4 + g), None, ALU.mult, None,
                   [PS(bank), "cst"], [("mixs", 4 + g)])
            if ti == 0:
                dbg_dump("mix", mix_sb[:, :, :], [("mixs", i) for i in range(8)])
                dbg_dump("lp", lp[:, :, :], lk0)
                dbg_dump("lr", lr[:, :, :], [("l16", 1)])
            wout_tile(ti, wo, s, [mix_sb[:, 4 + kc, :] for kc in range(4)] + [lr[:, kc, :] for kc in range(4)],
                      [("mixs", 4 + kc) for kc in range(4)] + [("lv", 0)] * 4)

    def ffn(li, tiles):
        norm_tile(tiles[0], 1)
        for f0 in range(0, NFF, 2):
            s, (wgv, wuv, wdv) = wload([wg_d[li, :, :, f0 * 128:(f0 + 2) * 128], wu_d[li, :, :, f0 * 128:(f0 + 2) * 128],
                                        wd_d[li, :, f0:f0 + 2, :]])
            for i_, ti in enumerate(tiles):
                if f0 == 0 and i_ + 1 < len(tiles):
                    norm_tile(tiles[i_ + 1], 1)
                t0, n = TILES[ti]
                s_ = tile_stream(ti)
                hsel = (f0 // 2 + ti) % 2
                for f in range(2):
                    bg, bu = 2 * f, 2 * f + 1
                    col = f * 128
                    for kc in range(KC):
                        mm(psA[bg][:, 0:n], wgv[:, kc, col:col + 128], h_sb[:, kc, t0:t0 + n], kc == 0, kc == KC - 1, [("W", s), ("h", ti)], [PS(bg)])
                    for kc in range(KC):
                        mm(psA[bu][:, 0:n], wuv[:, kc, col:col + 128], h_sb[:, kc, t0:t0 + n], kc == 0, kc == KC - 1, [("W", s), ("h", ti)], [PS(bu)])
                    act(t32[f][:, 0:n], psA[bg][:, 0:n], AF.Silu, [PS(bg)], [("t32", f)])
                    tt("dve", hid_sb[:, 2 * hsel + f, 0:n], psA[bu][:, 0:n], t32[f][:, 0:n], ALU.mult, [PS(bu), ("t32", f)], [("mixs", 4 + 2 * hsel + f)])
                for oc in range(KC):
                    bank = 4 + oc % 3
                    for fc in range(2):
                        mm(psA[bank][:, 0:n], wdv[:, fc, oc * 128:(oc + 1) * 128], hid_sb[:, 2 * hsel + fc, 0:n], fc == 0, fc == 1,
                           [("W", s), ("mixs", 4 + 2 * hsel + fc)], [PS(bank)])
                    stt("dve", x_sb[:, oc, t0:t0 + n], psA[bank][:, 0:n], nv_sb[:, 5, oc, s_:s_ + 1], x_sb[:, oc, t0:t0 + n], ALU.mult, ALU.add,
                        [PS(bank), "nv"], [("x", oc, ti)])

    mod_prologue()
    for li in range(n_layers):
        mod_layer(li)
        ctx_after = any(m % 2 == 0 for m in range(li + 1, 4))
        tl = [0, 1, 2, 3] + ([CTX_TILE] if ctx_after else [])
        if li % 2 == 0:
            even_mixer(li, ctx_after)
        else:
            odd_mixer(li, tl)
        ffn(li, tl)

    okeys = []
    for ti in range(4):
        t0, n = TILES[ti]
        if debug_x:
            for c in range(KC):
                dma("sp", out_d[:, c, t0:t0 + n], x_sb[:, c, t0:t0 + n], [("x", c, ti)], [("out", ti, c)], "out%d" % (c % 4))
        else:
            final_tile(ti)
        okeys += [("out", ti, c) for c in range(KC)]
    pg.add("sp", lambda h: h.nop(), okeys + dbg_list, [])

    pg.finalize()
    sems = {}
    for e in ENGS:
        sems[("eng", e)] = E(nc.semaphore("s_" + e))
    for i, k in enumerate(pg.dma_keys):
        sems[("dma", k)] = E(nc.semaphore("d%d" % i))
    block = E(nc.Block())

    @block.tensor
    def _(h):
        pg.emit(sems, "pe", h)

    @block.scalar
    def _(h):
        pg.emit(sems, "act", h)

    @block.vector
    def _(h):
        pg.emit(sems, "dve", h)

    @block.gpsimd
    def _(h):
        pg.emit(sems, "pool", h)

    @block.sync
    def _(h):
        pg.emit(sems, "sp", h)

    es.close()
    return nc


_NC_CACHE = {}


def prep_inputs(inp):
    x = np.asarray(inp["x"], np.float32)
    ctx = np.asarray(inp["ctx"], np.float32)
    shared = {
        "even_w_in": np.stack([wlay(inp["even_w_in"][l]) for l in range(2)]),
        "even_w_out": np.stack([wlay(inp["even_w_out"][l]) for l in range(2)]),
        "odd_w_in": np.stack([wlay(inp["odd_w_in"][l]) for l in range(2)]),
        "odd_w_out": np.stack([wlay(inp["odd_w_out"][l]) for l in range(2)]),
        "ffn_w_gate": np.stack([wlay(inp["ffn_w_gate"][l]) for l in range(4)]),
        "ffn_w_up": np.stack([wlay(inp["ffn_w_up"][l]) for l in range(4)]),
        "ffn_w_down": np.stack([wlay(inp["ffn_w_down"][l]) for l in range(4)]),
    }
    in_maps = []
    for core in range(NCORES):
        b, half = core // 2, core % 2
        xm = x[b, half * TM:(half + 1) * TM, :]
        xa = np.concatenate([xm, ctx[b]], axis=0)
        xt = np.ascontiguousarray(xa.T.reshape(KC, 128, TT).transpose(1, 0, 2))
        cst, tab, cb = host_consts(core, inp)
        m = {"xin": xt, "cst": cst, "tab": tab, "cstb": cb,
             "ada_w": np.stack([wlay(np.asarray(inp["ada_w"][l], np.float32)[:, half * 3072:(half + 1) * 3072]) for l in range(4)])}
        m.update(shared)
        in_maps.append(m)
    return in_maps


def assemble(results):
    out = np.zeros((4, SEQ, D), np.float32)
    for core in range(NCORES):
        b, half = core // 2, core % 2
        o = np.asarray(results[core]["out"], np.float32)
        out[b, half * TM:(half + 1) * TM, :] = o.transpose(1, 0, 2).reshape(D, TM).T
    return out


def kernel(**inputs):
    key = "full"
    if key not in _NC_CACHE:
        _NC_CACHE[key] = build_program()
    nc = _NC_CACHE[key]
    in_maps = prep_inputs(inputs)
    res = run_bass_kernel_spmd(nc, in_maps, core_ids=list(range(NCORES)))
    return assemble(res.results)
```

```python
import numpy as np
from contextlib import ExitStack
import concourse.bass as bass
import concourse.mybir as mybir
from concourse.bass_utils import run_bass_kernel_spmd

F32 = mybir.dt.float32
BF16 = mybir.dt.bfloat16
AF = mybir.ActivationFunctionType
ALU = mybir.AluOpType

ENGS = ("pe", "act", "dve", "pool", "sp")
NCORES = 8
DBG_DUMPS = False
D = 1024
KC = 8
TM = 2048
TCX = 256
TT = TM + TCX
SEQ = 4096
DFF = 2816
NFF = DFF // 128
EPS = 1e-6
BIG = 1.0e6
TILES = [(0, 512), (512, 512), (1024, 512), (1536, 512), (2048, 256)]
CTX_TILE = 4
HC = 15
HP = 8
XW = 1024 + 4 * 2 * HC
XWO = 4 * 2 * HP


class Op:
    __slots__ = ("eng", "fn", "deps", "sig", "done", "is_dma", "inc", "size", "idx")

    def __init__(self, eng, fn):
        self.eng = eng
        self.fn = fn
        self.deps = []
        self.sig = False
        self.done = None
        self.is_dma = None
        self.inc = 1
        self.size = 1 << 20
        self.idx = 0


class Prog:
    def __init__(self):
        self.ops = {e: [] for e in ENGS}
        self.last_w = {}
        self.readers = {}
        self.dma_keys = []

    def add(self, eng, fn, reads=(), writes=(), dma=None, inc=None, size=None):
        op = Op(eng, fn)
        op.idx = len(self.ops[eng])
        if size is not None:
            op.size = size
        if dma is not None:
            op.is_dma = dma
            op.inc = 16 if inc is None else inc
            if dma not in self.dma_keys:
                self.dma_keys.append(dma)
        xk = [k for k in reads if isinstance(k, tuple) and k[0] == "ps"]
        if xk:
            reads = [k for k in reads if k not in xk]
            writes = list(writes) + xk
        deps = []
        for k in reads:
            lw = self.last_w.get(k)
            if lw is not None:
                deps.append(lw)
        for k in writes:
            lw = self.last_w.get(k)
            if lw is not None:
                deps.append(lw)
            deps.extend(self.readers.get(k, ()))
        seen = set()
        for d in deps:
            if id(d) in seen or d is op:
                continue
            seen.add(id(d))
            if d.eng == eng and d.is_dma is None and dma is None:
                if eng == "pe" or d.size >= 512 or op.idx - d.idx > 6:
                    continue
            op.deps.append(d)
            d.sig = True
        for k in reads:
            lst = self.readers.setdefault(k, [])
            if dma is None:
                for i_, r_ in enumerate(lst):
                    if r_.is_dma is None and r_.eng == eng:
                        lst.pop(i_)
                        break
            lst.append(op)
        for k in writes:
            self.last_w[k] = op
            self.readers[k] = []
        self.ops[eng].append(op)
        return op

    def finalize(self):
        cnt = {}
        for e in ENGS:
            for op in self.ops[e]:
                if op.is_dma is not None:
                    key = ("dma", op.is_dma)
                    cnt[key] = cnt.get(key, 0) + op.inc
                    op.done = (key, cnt[key])
        for e in ENGS:
            c = 0
            for op in self.ops[e]:
                if op.is_dma is None and op.sig:
                    c += 1
                    op.done = (("eng", e), c)

    def emit(self, sems, e, h):
        waited = {}
        for op in self.ops[e]:
            need = {}
            for d in op.deps:
                k, v = d.done
                if waited.get(k, 0) >= v:
                    continue
                if need.get(k, 0) < v:
                    need[k] = v
            items = list(need.items())
            attach = None
            if items and op.is_dma is None:
                attach = items.pop()
            for k, v in items:
                h.wait_ge(sems[k], v)
                waited[k] = v
            ins = op.fn(h)
            if attach is not None:
                ins._wait_ge(sems[attach[0]], attach[1])
                waited[attach[0]] = attach[1]
            if op.is_dma is not None:
                ins.then_inc(sems[op.done[0]], op.inc)
            elif op.sig:
                ins.then_inc(sems[op.done[0]], 1)


class Lay:
    def __init__(self):
        self.off = {}
        self.n = 0

    def add(self, name, w):
        self.off[name] = self.n
        self.n += w

    def o(self, name):
        return self.off[name]


def make_layouts():
    L = Lay()
    L.add("flags", 4)
    L.add("zidx_m", 8)
    L.add("zidx_c", 4)
    L.add("cfidx", 8)
    L.add("logit", 16)
    L.add("svin", 16)
    L.add("adab", 4 * 48)
    L.add("normw", 4 * 16)
    L.add("fnw", 8)
    L.add("convw", 2 * 4 * 31)
    L.add("convlnw", 8)
    L.add("convlnb", 8)
    L.add("poolsc", 8)
    L.add("icm", 64)
    L.add("icc", 64)
    L.add("eps", 1)
    L.add("one", 1)
    L.add("lns", 1)
    LT = Lay()
    LT.add("ropeC", TT)
    LT.add("ropeS", TT)
    LT.add("a1idx", 896)
    LT.add("a2idx", 896)
    LT.add("xifidx", 512)
    LT.add("xibidx", 512)
    LT.add("sgtab", 2 * 1536)
    LT.add("pwsg", 2 * 1024)
    return L, LT


LAY, LAYT = make_layouts()
NCB = 384


def fm(v):
    v = np.asarray(v, np.float32)
    return np.ascontiguousarray(v.reshape(-1, 128).T)


def host_consts(core, inp):
    b, half = core // 2, core % 2
    L, LT = LAY, LAYT
    cst = np.zeros((128, L.n), np.float32)
    tab = np.zeros((128, LT.n), np.float32)

    def put(arr, lay, name, val):
        val = np.asarray(val, np.float32)
        o = lay.o(name)
        arr[:, o:o + val.shape[1]] = val

    pairs = (16, 24, 24)

    def angles(p_seq, p_row, p_col):
        parts = []
        for p_, n in zip((p_seq, p_row, p_col), pairs):
            freq = (np.float32(10000.0) ** (-np.arange(n, dtype=np.float32) / np.float32(n))).astype(np.float32)
            parts.append(p_[:, None].astype(np.float32) * freq[None, :])
        return np.concatenate(parts, axis=-1)

    t = np.arange(half * TM, (half + 1) * TM)
    ang_x = angles(np.full((TM,), float(TCX), np.float32), (t // 64).astype(np.float32), (t % 64).astype(np.float32))
    zc = np.zeros((TCX,), np.float32)
    ang_c = angles(np.arange(TCX, dtype=np.float32), zc, zc)
    ang = np.concatenate([ang_x, ang_c], axis=0).astype(np.float32)
    cos = np.cos(ang).T.astype(np.float32)
    sin = np.sin(ang).T.astype(np.float32)
    put(tab, LT, "ropeC", np.concatenate([cos, cos], axis=0))
    put(tab, LT, "ropeS", np.concatenate([-sin, sin], axis=0))
    p = np.arange(128)[:, None]
    i = np.arange(896)[None, :]
    dlt = i - p - 384
    put(tab, LT, "a1idx", np.where(dlt >= 0, dlt, BIG))
    put(tab, LT, "a2idx", np.where(dlt < 0, -dlt - 1, BIG))
    c = np.arange(512, dtype=np.float32)[None, :] + np.zeros((128, 1), np.float32)
    put(tab, LT, "xifidx", c + 1.0)
    put(tab, LT, "xibidx", 511.0 - c)
    sgt = np.zeros((128, 2 * 1536), np.float32)
    pws = np.zeros((128, 2 * 1024), np.float32)
    pw = np.asarray(inp["pool_w"], np.float32)
    sw = np.asarray(inp["sg_w"], np.float32)
    for l in range(2):
        sgt[:, l * 1536:l * 1536 + 512] = np.asarray(inp["sg_ln_w"][l], np.float32)[None, :]
        sgt[:, l * 1536 + 512:l * 1536 + 1024] = np.asarray(inp["sg_ln_b"][l], np.float32)[None, :]
        sgt[:, l * 1536 + 1024:l * 1536 + 1536] = np.asarray(inp["sg_b"][l], np.float32).reshape(1, 512)
        pws[:, l * 1024:l * 1024 + 512] = pw[l].transpose(1, 0, 2).reshape(128, 512)
        pws[:, l * 1024 + 512:l * 1024 + 1024] = sw[l].transpose(2, 0, 1).reshape(128, 512)
    put(tab, LT, "sgtab", sgt)
    put(tab, LT, "pwsg", pws)

    put(cst, L, "flags", np.tile(np.array([1.0 - half, float(half), 0, 0], np.float32)[None, :], (128, 1)))
    j = np.arange(4)[None, :]
    m = 128 * j + p
    put(cst, L, "zidx_m", np.concatenate([511.0 - m, m], axis=1))
    j2 = np.arange(2)[None, :]
    m2 = 128 * j2 + p
    put(cst, L, "zidx_c", np.concatenate([255.0 - m2, m2], axis=1))
    ti = np.arange(4, dtype=np.float32)
    put(cst, L, "cfidx", np.tile(np.concatenate([512.0 * ti, 512.0 * (3 - ti)])[None, :], (128, 1)))
    put(cst, L, "logit", np.tile(np.asarray(inp["ret_decay_logit"], np.float32).reshape(1, 16), (128, 1)))
    svin = np.zeros((128, 8, 2), np.float32)
    svin[:, :, 0] = fm(inp["c"][b])
    svin[:, :, 1] = fm(inp["c_ctx"])
    put(cst, L, "svin", svin.reshape(128, 16))
    put(cst, L, "adab", np.concatenate([fm(inp["ada_b"][l]) for l in range(4)], axis=1))
    put(cst, L, "normw", np.concatenate([fm(inp["norm_w"][l, s]) for l in range(4) for s in range(2)], axis=1))
    put(cst, L, "fnw", fm(inp["final_norm_w"]))
    cw = np.asarray(inp["conv_dw_w"], np.float32)
    put(cst, L, "convw", np.concatenate([cw[l][:, ch * 128:(ch + 1) * 128].T for l in range(2) for ch in range(4)], axis=1))
    put(cst, L, "convlnw", np.concatenate([fm(inp["conv_ln_w"][l]) for l in range(2)], axis=1))
    put(cst, L, "convlnb", np.concatenate([fm(inp["conv_ln_b"][l]) for l in range(2)], axis=1))
    put(cst, L, "poolsc", np.concatenate([fm(inp["pool_scale"][l]) for l in range(2)], axis=1))

    def invcnt(lo_edge, hi_edge, n):
        out = np.zeros((4, 16), np.float32)
        for gi, w in enumerate((2, 4, 8, 16)):
            left = w // 2
            right = w - 1 - left
            for k in range(8):
                cl = min(left, k) if lo_edge else left
                out[gi, k] = 1.0 / (cl + right + 1)
                tpos = n - 8 + k
                cr = min(right, n - 1 - tpos) if hi_edge else right
                out[gi, 8 + k] = 1.0 / (left + cr + 1)
        return np.tile(out.reshape(1, 64), (128, 1))
    put(cst, L, "icm", invcnt(half == 0, half == 1, TM))
    put(cst, L, "icc", invcnt(True, True, TCX))
    cst[:, L.o("eps")] = EPS
    cst[:, L.o("one")] = 1.0
    cst[:, L.o("lns")] = np.log(128.0 ** -0.5)
    cb = np.zeros((128, NCB), np.float32)
    cb[:, 0:128] = 1.0
    cb[:, 128:256] = np.eye(128, dtype=np.float32)
    for q in range(128):
        cb[(q + 64) % 128, 256 + q] = 1.0
    return cst, tab, cb


def wlay(w):
    w = np.asarray(w, np.float32)
    k, n = w.shape
    return np.ascontiguousarray(w.reshape(k // 128, 128, n).transpose(1, 0, 2))


def build_program(n_layers=4, debug_x=False):
    nc = bass.Bass("TRN2", target_bir_lowering=False)
    pg = Prog()
    es = ExitStack()
    E = es.enter_context

    def din(name, shape):
        return nc.dram_tensor(name, list(shape), F32, kind="ExternalInput").ap()

    xin = din("xin", [128, KC, TT])
    cst_d = din("cst", [128, LAY.n])
    tab_d = din("tab", [128, LAYT.n])
    cstb_d = din("cstb", [128, NCB])
    ada_d = din("ada_w", [4, 128, KC, 3072])
    evin_d = din("even_w_in", [2, 128, KC, 3072])
    evout_d = din("even_w_out", [2, 128, KC, D])
    odin_d = din("odd_w_in", [2, 128, KC, 1536])
    odout_d = din("odd_w_out", [2, 128, KC, D])
    wg_d = din("ffn_w_gate", [4, 128, KC, DFF])
    wu_d = din("ffn_w_up", [4, 128, KC, DFF])
    wd_d = din("ffn_w_down", [4, 128, NFF, D])
    out_d = nc.dram_tensor("out", [128, KC, TM], F32, kind="ExternalOutput").ap()

    def scratch(name, shape, dt=BF16):
        return nc.dram_tensor(name, list(shape), dt).ap()

    q_d = scratch("q_d", [128, 4, TT])
    k_d = scratch("k_d", [128, 4, TT])
    v_d = scratch("v_d", [128, TT // 128, 512])
    g_d = scratch("g_d", [128, 4, TT])
    um_d = scratch("um_d", [128, 4, TM + 2 * HC])
    uc_d = scratch("uc_d", [128, 4, TCX + 2 * HC])
    r_d = scratch("r_d", [128, 4, TT])
    pm_d = scratch("pm_d", [128, 4, TM + 2 * HP])
    pc_d = scratch("pc_d", [128, 4, TCX + 2 * HP])
    xi_d = nc.dram_tensor("xi_d", [128, XW], F32)
    xo_d = nc.dram_tensor("xo_d", [256, XW], F32)
    xio_d = nc.dram_tensor("xio_d", [128, XWO], F32)
    xoo_d = nc.dram_tensor("xoo_d", [256, XWO], F32)
    xm_d = nc.dram_tensor("xm_d", [128, 192], F32)
    xmo_d = nc.dram_tensor("xmo_d", [256, 192], F32)

    def sb(name, shape, dt=F32):
        return E(nc.sbuf_tensor(name, list(shape), dt))

    x_sb = sb("x_sb", [128, KC, TT])
    h_sb = sb("h_sb", [128, KC, TT], BF16)
    cst = sb("cst_sb", [128, LAY.n])
    cstb = sb("cstb_sb", [128, NCB], BF16)
    WSLOT = 8192
    w_sb = [sb("w_sb%d" % i, [128, WSLOT], BF16) for i in range(2)]
    modall = sb("modall", [128, 4, 2, 48])
    modx = sb("modx", [128, 192])
    modg = sb("modg", [128, 2, 192])
    nv_sb = sb("nv_sb", [128, 6, KC, 2])
    sv_sb = sb("sv_sb", [128, KC, 2], BF16)
    rs_sb = sb("rstd", [128, 512])
    TW = 512 + 2 * HP
    t32 = [sb("t32_%d" % i, [128, TW]) for i in range(3)]
    tb16 = [sb("tb16_%d" % i, [128, 512], BF16) for i in range(3)]
    o16 = [sb("o16_%d" % i, [128, 4, 512], BF16) for i in range(2)]
    l16 = [sb("l16_%d" % i, [128, 4, 512 + 2 * HC], BF16) for i in range(2)]
    l16x = sb("l16x", [128, 4, 512], BF16)
    acc32 = sb("acc32", [128, 4, 512])
    accf = acc32[:].rearrange("p a b -> p (a b)")
    mix_sb = sb("mix", [128, 8, 512], BF16)
    ropeT = mix_sb[:, 0:4, :].rearrange("p a b -> p (a b)").bitcast(F32)
    hid_sb = mix_sb[:, 4:8, :]
    lg_sb = sb("lg", [128, 16])
    dtab = sb("dtab", [128, 896], BF16)
    xi_sb = sb("xi", [128, 2, 512], BF16)
    zm_sb = sb("zm", [128, 2, 4, 4])
    zc_sb = sb("zc", [128, 2, 4, 2])
    dec_sb = sb("dec", [128, 2, 4])
    cf_sb = sb("cf", [128, 2, 4, 4])
    kz_sb = [sb("kz%d" % i, [128, 2, 128], BF16) for i in range(2)]
    S_sb = sb("S", [128, 2, 512])
    Sflat = S_sb[:].rearrange("p a b -> p (a b)")
    pwsg = Sflat.bitcast(BF16)[:, 0:1024]
    xh_sb = sb("xh", [128, 8 * HC])
    sm = sb("small", [128, 8])

    psA = [E(nc.psum_tensor("psA%d" % i, [128, 512], F32)) for i in range(7)]
    psT = E(nc.psum_tensor("psT", [128, 1024], BF16))
    PST = 7

    def C(name, a=0, n=1):
        o = LAY.o(name) + a
        return cst[:, o:o + n]

    ones = cstb[:, 0:128]
    ident = cstb[:, 128:256]
    perm = cstb[:, 256:384]
    epsc = C("eps")
    fA = C("flags", 0)
    fB = C("flags", 1)

    def PS(i):
        return ("ps", i)

    def fsz(ap):
        n = 1
        for v in list(ap.shape)[1:]:
            n *= int(v)
        return n

    def mm(out, lhsT, rhs, start, stop, reads, writes):
        pg.add("pe", lambda h: h.matmul(out, lhsT=lhsT, rhs=rhs, start=start, stop=stop), reads=reads, writes=writes)

    def act(out, in_, func, reads, writes, bias=None, scale=None, accum=None):
        kw = {}
        if bias is not None:
            kw["bias"] = bias
        if scale is not None:
            kw["scale"] = scale
        if accum is not None:
            kw["accum_out"] = accum
        pg.add("act", lambda h: h.activation(out=out, in_=in_, func=func, **kw), reads=reads, writes=writes, size=fsz(out))

    def tt(eng, out, in0, in1, op, reads, writes):
        pg.add(eng, lambda h: h.tensor_tensor(out=out, in0=in0, in1=in1, op=op), reads=reads, writes=writes, size=fsz(out))

    def ts(eng, out, in0, s1, s2, op0, op1, reads, writes):
        if s2 is None:
            pg.add(eng, lambda h: h.tensor_scalar(out=out, in0=in0, scalar1=s1, scalar2=None, op0=op0), reads=reads, writes=writes, size=fsz(out))
        else:
            pg.add(eng, lambda h: h.tensor_scalar(out=out, in0=in0, scalar1=s1, scalar2=s2, op0=op0, op1=op1), reads=reads, writes=writes, size=fsz(out))

    def stt(eng, out, in0, scalar, in1, op0, op1, reads, writes):
        pg.add(eng, lambda h: h.scalar_tensor_tensor(out=out, in0=in0, scalar=scalar, in1=in1, op0=op0, op1=op1), reads=reads, writes=writes, size=fsz(out))

    def cp(eng, out, in_, reads, writes):
        pg.add(eng, lambda h: h.tensor_copy(out=out, in_=in_), reads=reads, writes=writes, size=fsz(out))

    def recip(ap, key):
        pg.add("dve", lambda h: h.reciprocal(out=ap, in_=ap), [key], [key], size=fsz(ap))

    dma_prev = {}

    def dma(eng, out, in_, reads, writes, key):
        op = pg.add(eng, lambda h: h.dma_start(out=out, in_=in_), reads=reads, writes=writes, dma=key)
        prev = dma_prev.get(key)
        if prev is not None and prev not in op.deps:
            op.deps.append(prev)
            prev.sig = True
        dma_prev[key] = op
        return op

    dbg_list = []

    def dbg_dump(name, ap, keys):
        if not (debug_x and DBG_DUMPS):
            return
        shp = list(ap.shape)
        dt = ap.dtype
        dd = nc.dram_tensor("dbg_" + name, shp, dt, kind="ExternalOutput").ap()
        idx = tuple(slice(None) for _ in shp)
        dma("sp", dd[idx], ap, keys, ["dbgk_" + name], "dbg%d" % (len(dbg_list) % 4))
        dbg_list.append("dbgk_" + name)

    stc = {"n": 0}

    def stkey():
        stc["n"] += 1
        return "st%d" % (stc["n"] % 4)

    ldc = {"n": 0}

    def ldkey():
        ldc["n"] += 1
        return "ld%d" % (ldc["n"] % 4)

    dma("sp", cst[:], cst_d[:, :], [], ["cst"], "cst")
    dma("pool", cstb[:], cstb_d[:, :], [], ["cstb"], "cstb")
    for c in range(KC):
        dma("sp", x_sb[:, c, :], xin[:, c, :], [], [("x", c, t) for t in range(5)], ("xin", c % 4))
    pg.add("dve", lambda h: h.memset(o16[0][:, :, 0:16], 0.0), [], [("o16", 0)])
    dma("sp", uc_d[:, :, 0:HC], o16[0][:, :, 0:HC], [("o16", 0)], ["uc_h0"], stkey())
    dma("sp", uc_d[:, :, HC + TCX:], o16[0][:, :, 0:HC], [("o16", 0)], ["uc_h1"], stkey())
    dma("sp", pc_d[:, :, 0:HP], o16[0][:, :, 0:HP], [("o16", 0)], ["pc_h0"], stkey())
    dma("sp", pc_d[:, :, HP + TCX:], o16[0][:, :, 0:HP], [("o16", 0)], ["pc_h1"], stkey())
    act(sv_sb[:].rearrange("p a b -> p (a b)"), C("svin", 0, 16), AF.Silu, ["cst"], ["sv"])
    act(lg_sb[:], C("logit", 0, 16), AF.Exp, ["cst"], ["lg"], scale=-1.0)
    act(lg_sb[:], lg_sb[:], AF.Ln, ["lg", "cst"], ["lg"], bias=C("one"))
    ts("dve", lg_sb[:], lg_sb[:], -1.0, None, ALU.mult, None, ["lg"], ["lg"])

    wstate = {"n": 0}

    def wload(parts, slot=None):
        s = wstate["n"] % 2 if slot is None else slot
        wstate["n"] = s + 1
        views = []
        off = 0
        for pi, ap in enumerate(parts):
            shp = list(ap.shape)
            n = shp[1] * shp[2]
            v = w_sb[s][:, off:off + n].rearrange("p (a b) -> p a b", a=shp[1])
            dma("pool", v, ap, [], [("W", s)], ("W", s, pi))
            views.append(v)
            off += n
        assert off <= WSLOT
        return s, views

    def mod_prologue():
        for li in range(4):
            for pi in range(3):
                s, (wv,) = wload([ada_d[li, :, :, pi * 1024:(pi + 1) * 1024]])
                for cc in range(8):
                    bank = cc % 2
                    for kc in range(KC):
                        mm(psA[bank][:, 0:2], wv[:, kc, cc * 128:(cc + 1) * 128], sv_sb[:, kc, :],
                           kc == 0, kc == KC - 1, [("W", s), "sv"], [PS(bank)])
                    o = li * 48 + (pi * 8 + cc) * 2
                    cp("dve", modx[:, o:o + 2], psA[bank][:, 0:2], [PS(bank)], ["modx"])
        dma("pool", xm_d.ap(), modx[:, :], ["modx"], ["xm_d"], "xst0")
        allgather(xm_d, xmo_d, ["xm_d"], "xmo_d")
        dma("pool", modg[:], xmo_d.ap().rearrange("(r p) n -> p r n", p=128), ["xmo_d"], ["modg"], "xld0")
        G = modg[:].rearrange("p r (l c v) -> p r l c v", l=4, c=24)
        for li in range(4):
            ab = C("adab", li * 48, 48).rearrange("p (r c) -> p r c", r=2)
            for s_ in range(2):
                mx = modall[:, li, s_, :].rearrange("p (r c) -> p r c", r=2)
                tt("dve", mx, G[:, :, li, :, s_], ab, ALU.add, ["modg", "cst"], ["mod"])

    def mod_layer(li):
        for s_ in range(2):
            M = modall[:, li, s_, :]
            stt("dve", nv_sb[:, 0, :, s_], M[:, 8:16], 1.0, C("normw", li * 16, 8), ALU.add, ALU.mult, ["mod", "cst"], ["nv"])
            cp("dve", nv_sb[:, 1, :, s_], M[:, 0:8], ["mod"], ["nv"])
            cp("dve", nv_sb[:, 2, :, s_], M[:, 16:24], ["mod"], ["nv"])
            stt("dve", nv_sb[:, 3, :, s_], M[:, 32:40], 1.0, C("normw", li * 16 + 8, 8), ALU.add, ALU.mult, ["mod", "cst"], ["nv"])
            cp("dve", nv_sb[:, 4, :, s_], M[:, 24:32], ["mod"], ["nv"])
            cp("dve", nv_sb[:, 5, :, s_], M[:, 40:48], ["mod"], ["nv"])

    def tile_stream(ti):
        return 1 if ti == CTX_TILE else 0

    def rstd_from_ps(bank, n, scale):
        act(rs_sb[:, 0:n], psA[bank][:, 0:n], AF.Ln, [PS(bank), "cst"], ["rs"], bias=epsc, scale=scale)
        act(rs_sb[:, 0:n], rs_sb[:, 0:n], AF.Exp, ["rs"], ["rs"], scale=-0.5)

    def sumsq_x(ti):
        t0, n = TILES[ti]
        for c in range(KC):
            q = tb16[c % 2]
            act(q[:, 0:n], x_sb[:, c, t0:t0 + n], AF.Square, [("x", c, ti)], [("tb16", c % 2)])
            mm(psA[6][:, 0:n], ones, q[:, 0:n], c == 0, c == KC - 1, [("tb16", c % 2), "cstb"], [PS(6)])
        rstd_from_ps(6, n, 1.0 / D)

    def norm_tile(ti, which):
        t0, n = TILES[ti]
        s_ = tile_stream(ti)
        sumsq_x(ti)
        for c in range(KC):
            tmp = t32[c % 2]
            tt("dve", tmp[:, 0:n], x_sb[:, c, t0:t0 + n], rs_sb[:, 0:n], ALU.mult, [("x", c, ti), "rs"], [("t32", c % 2)])
            a = nv_sb[:, 3 * which, c, s_:s_ + 1]
            b = nv_sb[:, 3 * which + 1, c, s_:s_ + 1]
            act(h_sb[:, c, t0:t0 + n], tmp[:, 0:n], AF.Identity, [("t32", c % 2), "nv"], [("h", ti)], bias=b, scale=a)

    def final_tile(ti):
        t0, n = TILES[ti]
        sumsq_x(ti)
        for c in range(KC):
            tmp = t32[c % 2]
            tt("dve", tmp[:, 0:n], x_sb[:, c, t0:t0 + n], rs_sb[:, 0:n], ALU.mult, [("x", c, ti), "rs"], [("t32", c % 2)])
            act(acc32[:, c % 4, 0:n], tmp[:, 0:n], AF.Identity, [("t32", c % 2), "cst"], [("acc", c % 4)], scale=C("fnw", c))
            dma("sp", out_d[:, c, t0:t0 + n], acc32[:, c % 4, 0:n], [("acc", c % 4)], [("out", ti, c)], "out%d" % (c % 4))

    def proj_fm(bank, wv, s, col0, ti):
        t0, n = TILES[ti]
        for kc in range(KC):
            mm(psA[bank][:, 0:n], wv[:, kc, col0:col0 + 128], h_sb[:, kc, t0:t0 + n], kc == 0, kc == KC - 1,
               [("W", s), ("h", ti)], [PS(bank)])

    def load_rope(ti):
        t0, n = TILES[ti]
        dma("sp", ropeT[:, 0:n], tab_d[:, LAYT.o("ropeC") + t0:LAYT.o("ropeC") + t0 + n], [], [("mixs", 0), ("mixs", 1)], ldkey())
        dma("sp", ropeT[:, 512:512 + n], tab_d[:, LAYT.o("ropeS") + t0:LAYT.o("ropeS") + t0 + n], [], [("mixs", 2), ("mixs", 3)], ldkey())

    rope_ctr = {"n": 0}

    def rope(bank, ti, out_ap, okey):
        t0, n = TILES[ti]
        i = rope_ctr["n"] % 2
        rope_ctr["n"] += 1
        tb = tb16[i]
        act(tb[:, 0:n], psA[bank][:, 0:n], AF.Copy, [PS(bank)], [("tb16", i)])
        tt("dve", t32[2][:, 0:n], psA[bank][:, 0:n], ropeT[:, 0:n], ALU.mult, [PS(bank), ("mixs", 0), ("mixs", 1)], [("t32", 2)])
        mm(psA[5][:, 0:n], perm, tb[:, 0:n], True, True, [("tb16", i), "cstb"], [PS(5)])
        tt("dve", t32[i][:, 0:n], psA[5][:, 0:n], ropeT[:, 512:512 + n], ALU.mult, [PS(5), ("mixs", 2), ("mixs", 3)], [("t32", i)])
        tt("dve", out_ap, t32[i][:, 0:n], t32[2][:, 0:n], ALU.add, [("t32", i), ("t32", 2)], [okey])

    RG = [[0, 1], [2, 3], [4, 5], [6, 7]]

    def allgather(src, dst, skeys, dkey, groups=None):
        groups = RG if groups is None else groups
        pg.add("pool", lambda h: h.collective_compute("AllGather", ALU.bypass, replica_groups=groups,
                                                      ins=[src.ap().opt()], outs=[dst.ap().opt()]),
               list(skeys), [dkey], dma="cc", inc=1)

    def even_small_tables(j):
        lo = 8 * j
        for hd in range(4):
            for d_ in range(2):
                lx = lg_sb[:, lo + 4 * d_ + hd:lo + 4 * d_ + hd + 1]
                act(zm_sb[:, d_, hd, :], C("zidx_m", d_ * 4, 4), AF.Exp, ["cst", "lg"], ["zm"], scale=lx)
                act(zc_sb[:, d_, hd, :], C("zidx_c", d_ * 2, 2), AF.Exp, ["cst", "lg"], ["zm"], scale=lx)
                act(dec_sb[:, d_, hd:hd + 1], lx, AF.Exp, ["lg"], ["dec"], scale=512.0)
                act(cf_sb[:, d_, :, hd], C("cfidx", d_ * 4, 4), AF.Exp, ["cst", "lg"], ["cf"], scale=lx)
        ts("dve", cf_sb[:, 0, :, :], cf_sb[:, 0, :, :], fB, None, ALU.mult, None, ["cf", "cst"], ["cf"])
        ts("dve", cf_sb[:, 1, :, :], cf_sb[:, 1, :, :], fA, None, ALU.mult, None, ["cf", "cst"], ["cf"])

    def head_tables(j, hd):
        lo = 8 * j
        lf = lg_sb[:, lo + hd:lo + hd + 1]
        lb = lg_sb[:, lo + 4 + hd:lo + 4 + hd + 1]
        acck = [("acc", i) for i in range(4)]
        dma("sp", accf[:, 0:896], tab_d[:, LAYT.o("a1idx"):LAYT.o("a1idx") + 896], [], acck, "tbl0")
        dma("sp", accf[:, 1024:1920], tab_d[:, LAYT.o("a2idx"):LAYT.o("a2idx") + 896], [], acck, "tbl1")
        act(accf[:, 0:896], accf[:, 0:896], AF.Exp, acck + ["lg", "cst"], acck, scale=lf, bias=C("lns"))
        act(accf[:, 1024:1920], accf[:, 1024:1920], AF.Exp, acck + ["lg", "cst"], acck, scale=lb, bias=C("lns"))
        tt("dve", dtab[:, :], accf[:, 0:896], accf[:, 1024:1920], ALU.add, acck, ["dtab"])
        dma("sp", accf[:, 0:512], tab_d[:, LAYT.o("xifidx"):LAYT.o("xifidx") + 512], [], acck, "tbl2")
        dma("sp", accf[:, 512:1024], tab_d[:, LAYT.o("xibidx"):LAYT.o("xibidx") + 512], [], acck, "tbl3")
        act(xi_sb[:, 0, :], accf[:, 0:512], AF.Exp, acck + ["lg", "cst"], ["xi"], scale=lf, bias=C("lns"))
        act(xi_sb[:, 1, :], accf[:, 512:1024], AF.Exp, acck + ["lg", "cst"], ["xi"], scale=lb, bias=C("lns"))

    def even_mixer(li, full_ctx):
        j = li // 2
        even_small_tables(j)
        order = [CTX_TILE, 0, 1, 2, 3]
        tiles_full = order if full_ctx else [0, 1, 2, 3]
        for ti in order:
            if ti not in tiles_full:
                norm_tile(ti, 0)
        norm_tile(tiles_full[0], 0)
        s2, (wa, wgb) = wload([evin_d[j, :, :, 2048:2560], evin_d[j, :, :, 2560:3072]])
        for i_, ti in enumerate(tiles_full):
            if i_ + 1 < len(tiles_full):
                norm_tile(tiles_full[i_ + 1], 0)
            t0, n = TILES[ti]
            ub = o16[ti % 2]
            for ch in range(4):
                b0 = (2 * ch) % 4
                b1 = b0 + 1
                proj_fm(b0, wa, s2, ch * 128, ti)
                proj_fm(b1, wgb, s2, ch * 128, ti)
                act(t32[ch % 2][:, 0:n], psA[b1][:, 0:n], AF.Sigmoid, [PS(b1)], [("t32", ch % 2)])
                tt("dve", ub[:, ch, 0:n], psA[b0][:, 0:n], t32[ch % 2][:, 0:n], ALU.mult, [PS(b0), ("t32", ch % 2)], [("o16", ti % 2)])
            if ti == CTX_TILE:
                dma("sp", uc_d[:, :, HC:HC + n], ub[:, :, 0:n], [("o16", ti % 2)], [("u_d", ti)], stkey())
            else:
                dma("sp", um_d[:, :, HC + t0:HC + t0 + n], ub[:, :, 0:n], [("o16", ti % 2)], [("u_d", ti)], stkey())
            if ti == 0:
                cp("dve", xh_sb[:, 0:4 * HC].rearrange("p (a b) -> p a b", a=4), ub[:, :, 0:HC], [("o16", ti % 2)], ["xh"])
            if ti == 3:
                cp("dve", xh_sb[:, 4 * HC:8 * HC].rearrange("p (a b) -> p a b", a=4), ub[:, :, n - HC:n], [("o16", ti % 2)], ["xh"])
        s3, (wk, wvv) = wload([evin_d[j, :, :, 512:1024], evin_d[j, :, :, 1024:1536]], slot=1 - s2)
        so = 1 - s3
        sp32 = w_sb[so][:, :].bitcast(F32).rearrange("p (a b c) -> p a b c", a=2, b=4)
        SPK = ("W", so)
        for ti in order:
            t0, n = TILES[ti]
            nb = n // 128
            load_rope(ti)
            ob = o16[0]
            for jb in range(nb):
                bank = jb % 2
                for kc in range(KC):
                    mm(psA[bank][:, :], h_sb[:, kc, t0 + jb * 128:t0 + (jb + 1) * 128], wvv[:, kc, :], kc == 0, kc == KC - 1,
                       [("W", s3), ("h", ti)], [PS(bank)])
                act(ob[:, jb, :], psA[bank][:, :], AF.Copy, [PS(bank)], [("o16", 0)])
            dma("sp", v_d[:, t0 // 128:t0 // 128 + nb, :], ob[:, 0:nb, :], [("o16", 0)], [("v_d", ti)], stkey())
            kb = o16[1]
            zt = zc_sb if ti == CTX_TILE else zm_sb
            for hd in range(4):
                bank = 2 + hd % 2
                proj_fm(bank, wk, s3, hd * 128, ti)
                rope(bank, ti, kb[:, hd, 0:n], ("o16", 1))
                for jb in range(nb):
                    pg.add("pe", lambda h, hd=hd, jb=jb: h.transpose(psT[:, jb * 128:(jb + 1) * 128], kb[:, hd, jb * 128:(jb + 1) * 128], ident),
                           [("o16", 1), "cstb"], [PS(PST)])
                for d_ in range(2):
                    kvbank = 4 if d_ == 0 else 6
                    for jb in range(nb):
                        kz = kz_sb[jb % 2]
                        ts("dve", kz[:, d_, :], psT[:, jb * 128:(jb + 1) * 128], zt[:, d_, hd, jb:jb + 1], None, ALU.mult, None,
                           [PS(PST), "zm"], [("kz", jb % 2, d_)])
                        mm(psA[kvbank][:, hd * 128:(hd + 1) * 128], kz[:, d_, :], ob[:, jb, hd * 128:(hd + 1) * 128], jb == 0, jb == nb - 1,
                           [("kz", jb % 2, d_), ("o16", 0)], [PS(kvbank)])
            dma("sp", k_d[:, :, t0:t0 + n], kb[:, :, 0:n], [("o16", 1)], [("k_d", ti)], stkey())
            if ti == CTX_TILE:
                ts("dve", S_sb[:, 0, :], psA[4][:, :], fA, None, ALU.mult, None, [PS(4), "cst"], ["S"])
                ts("dve", S_sb[:, 1, :], psA[6][:, :], fB, None, ALU.mult, None, [PS(6), "cst"], ["S"])
            else:
                cp("dve", sp32[:, 0, ti, :], S_sb[:, 0, :], ["S"], [SPK])
                for hd in range(4):
                    hs = slice(hd * 128, (hd + 1) * 128)
                    stt("dve", S_sb[:, 0, hs], S_sb[:, 0, hs], dec_sb[:, 0, hd:hd + 1], psA[4][:, hs], ALU.mult, ALU.add,
                        ["S", "dec", PS(4)], ["S"])
                cp("dve", acc32[:, ti, :], psA[6][:, :], [PS(6)], [("acc", ti)])
        for ti in (3, 2, 1, 0):
            cp("dve", sp32[:, 1, ti, :], S_sb[:, 1, :], ["S"], [SPK])
            for hd in range(4):
                hs = slice(hd * 128, (hd + 1) * 128)
                stt("dve", S_sb[:, 1, hs], S_sb[:, 1, hs], dec_sb[:, 1, hd:hd + 1], acc32[:, ti, hs], ALU.mult, ALU.add,
                    ["S", "dec", ("acc", ti)], ["S"])
        sq_, (wq, wgt) = wload([evin_d[j, :, :, 0:512], evin_d[j, :, :, 1536:2048]], slot=s3)
        cwo = j * 124
        HK = [("h", t) for t in range(5)]
        dgf = h_sb[:].rearrange("p a b -> p (a b)")
        dma("pool", xi_d.ap()[:, 0:1024], Sflat, ["S"], ["xi_d"], "xst0")
        dma("pool", xi_d.ap()[:, 1024:XW], xh_sb[:, :], ["xh"], ["xi_d2"], "xst1")
        allgather(xi_d, xo_d, ["xi_d", "xi_d2"], "xo_d")
        acck = [("acc", i) for i in range(4)]
        dma("pool", accf[:, 0:512], xo_d.ap()[0:128, 0:512], ["xo_d"], acck, "xld0")
        dma("pool", accf[:, 512:1024], xo_d.ap()[128:256, 512:1024], ["xo_d"], acck, "xld1")
        dma("pool", accf[:, 1024:1024 + 4 * HC], xo_d.ap()[0:128, 1024 + 4 * HC:XW], ["xo_d"], acck, "xld0")
        dma("pool", accf[:, 1024 + 4 * HC:XW], xo_d.ap()[128:256, 1024:1024 + 4 * HC], ["xo_d"], acck, "xld1")
        for i_, ti in enumerate(tiles_full):
            t0, n = TILES[ti]
            load_rope(ti)
            for hd in range(4):
                bank = hd % 2
                proj_fm(bank, wq, sq_, hd * 128, ti)
                rope(bank, ti, o16[0][:, hd, 0:n], ("o16", 0))
                bank = 2 + hd % 2
                proj_fm(bank, wgt, sq_, hd * 128, ti)
                act(o16[1][:, hd, 0:n], psA[bank][:, 0:n], AF.Silu, [PS(bank)], [("o16", 1)])
            dma("sp", q_d[:, :, t0:t0 + n], o16[0][:, :, 0:n], [("o16", 0)], [("q_d", ti)], stkey())
            dma("sp", g_d[:, :, t0:t0 + n], o16[1][:, :, 0:n], [("o16", 1)], [("g_d", ti)], stkey())
        for d_ in range(2):
            for ti in range(4):
                for hd in range(4):
                    hs = slice(hd * 128, (hd + 1) * 128)
                    stt("dve", o16[d_][:, ti, hs], accf[:, d_ * 512 + hd * 128:d_ * 512 + (hd + 1) * 128], cf_sb[:, d_, ti, hd:hd + 1],
                        sp32[:, d_, ti, hs], ALU.mult, ALU.add, acck + ["cf", SPK], [("o16", d_)])
        hl = tb16[2][:, 0:8 * HC].rearrange("p (a b) -> p a b", a=4)
        ts("dve", hl[:, :, 0:HC], accf[:, 1024:1024 + 4 * HC].rearrange("p (a b) -> p a b", a=4), fB, None, ALU.mult, None,
           acck + ["cst"], [("tb16", 2)])
        ts("dve", hl[:, :, HC:2 * HC], accf[:, 1024 + 4 * HC:XW].rearrange("p (a b) -> p a b", a=4), fA, None, ALU.mult, None,
           acck + ["cst"], [("tb16", 2)])
        dma("sp", um_d[:, :, 0:HC], hl[:, :, 0:HC], [("tb16", 2)], [("u_d", 0)], stkey())
        dma("sp", um_d[:, :, HC + TM:], hl[:, :, HC:2 * HC], [("tb16", 2)], [("u_d", 3)], stkey())
        s4, (wo,) = wload([evout_d[j, :, :, :]], slot=so)
        l16b = l16x
        LQ = [l16[0][:, 0, 0:512], l16b[:, 0, 0:512]]
        LK = [l16[0][:, 1, 0:512], l16b[:, 1, 0:512]]
        LG = [l16[0][:, 2, 0:512], l16b[:, 2, 0:512]]
        RB = [l16[0][:, 3, 0:512], l16b[:, 3, 0:512]]
        LV = [l16[1][:, :, 0:128], l16[1][:, :, 128:256]]
        its = [(hd, ti) for hd in range(4) for ti in tiles_full]

        def ret_loads(it):
            hd, ti = its[it]
            sset = it % 2
            t0, n = TILES[ti]
            nb = n // 128
            hs = slice(hd * 128, (hd + 1) * 128)
            dma("sp", LQ[sset][:, 0:n], q_d[:, hd, t0:t0 + n], [("q_d", ti)], [("lq", sset)], ldkey())
            dma("sp", LK[sset][:, 0:n], k_d[:, hd, t0:t0 + n], [("k_d", ti)], [("lk", sset)], ldkey())
            dma("sp", LV[sset][:, 0:nb, :], v_d[:, t0 // 128:t0 // 128 + nb, hs], [("v_d", ti)], [("lv", sset)], ldkey())

        def ret_load_g(it):
            hd, ti = its[it]
            sset = it % 2
            t0, n = TILES[ti]
            dma("sp", LG[sset][:, 0:n], g_d[:, hd, t0:t0 + n], [("g_d", ti)], [("lg_", sset)], ldkey())

        L0K = [("lq", 0), ("lk", 0), ("lg_", 0), ("rb", 0)]
        L1K = [("lv", 0), ("lv", 1)]
        def stage_a(it):
            hd, ti = its[it]
            sset = it % 2
            hs = slice(hd * 128, (hd + 1) * 128)
            if ti == tiles_full[0]:
                head_tables(j, hd)
            t0, n = TILES[ti]
            nb = n // 128
            inter = ti != CTX_TILE
            lq, lk, lv = LQ[sset], LK[sset], LV[sset]
            if inter:
                tt("dve", tb16[0][:, 0:n], lq[:, 0:n], xi_sb[:, 0, 0:n], ALU.mult, [("lq", sset), "xi"], [("tb16", 0)])
                tt("dve", tb16[1][:, 0:n], lq[:, 0:n], xi_sb[:, 1, 0:n], ALU.mult, [("lq", sset), "xi"], [("tb16", 1)])
            for jb in range(nb):
                mm(psA[jb][:, 0:n], lk[:, jb * 128:(jb + 1) * 128], lq[:, 0:n], True, True, [("lk", sset), ("lq", sset)], [PS(jb)])
                st0 = 384 - 128 * jb
                tt("dve", mix_sb[:, jb, 0:n], psA[jb][:, 0:n], dtab[:, st0:st0 + n], ALU.mult, [PS(jb), "dtab"], [("mixs", jb)])
            yb = 4 + it % 2
            nmm = nb + (2 if inter else 0)
            for jb in range(nb):
                mm(psA[yb][:, 0:n], lv[:, jb, :], mix_sb[:, jb, 0:n], jb == 0, jb == nmm - 1, [("lv", sset), ("mixs", jb)], [PS(yb)])
            if inter:
                mm(psA[yb][:, 0:n], o16[0][:, ti, hs], tb16[0][:, 0:n], False, False, [("o16", 0), ("tb16", 0)], [PS(yb)])
                mm(psA[yb][:, 0:n], o16[1][:, ti, hs], tb16[1][:, 0:n], False, True, [("o16", 1), ("tb16", 1)], [PS(yb)])

        def stage_b(it):
            hd, ti = its[it]
            sset = it % 2
            t0, n = TILES[ti]
            yb = 4 + it % 2
            lgt, rb = LG[sset], RB[sset]
            act(tb16[2][:, 0:n], psA[yb][:, 0:n], AF.Square, [PS(yb)], [("tb16", 2)])
            mm(psA[6][:, 0:n], ones, tb16[2][:, 0:n], True, True, [("tb16", 2), "cstb"], [PS(6)])
            rstd_from_ps(6, n, 1.0 / 128)
            tt("dve", t32[0][:, 0:n], psA[yb][:, 0:n], rs_sb[:, 0:n], ALU.mult, [PS(yb), "rs"], [("t32", 0)])
            tt("dve", rb[:, 0:n], t32[0][:, 0:n], lgt[:, 0:n], ALU.mult, [("t32", 0), ("lg_", sset)], [("rb", sset)])
            dma("sp", r_d[:, hd, t0:t0 + n], rb[:, 0:n], [("rb", sset)], [("r_d", ti, hd)], stkey())
            per = (124 + len(its) - 1) // len(its)
            for idx in range(it * per, min(124, (it + 1) * per)):
                ts("dve", dgf[:, idx * 128:(idx + 1) * 128], ident, C("convw", cwo + idx), None, ALU.mult, None, ["cstb", "cst"], HK)

        ret_loads(0)
        ret_load_g(0)
        if len(its) > 1:
            ret_loads(1)
            ret_load_g(1)
        stage_a(0)
        for it in range(len(its)):
            if it + 2 < len(its):
                ret_loads(it + 2)
            if it + 1 < len(its):
                stage_a(it + 1)
            stage_b(it)
            if it + 2 < len(its):
                ret_load_g(it + 2)
        for ti in tiles_full:
            t0, n = TILES[ti]
            lu, lr = l16[0], l16[1]
            lk0 = L0K
            if ti == CTX_TILE:
                dma("sp", lu[:, :, 0:n + 2 * HC], uc_d[:, :, :], [("u_d", ti), "uc_h0", "uc_h1"], lk0, ldkey())
            else:
                rk = [("u_d", ti)] + ([("u_d", ti - 1)] if ti > 0 else []) + ([("u_d", ti + 1)] if ti < 3 else [])
                dma("sp", lu[:, :, 0:n + 2 * HC], um_d[:, :, t0:t0 + n + 2 * HC], rk, lk0, ldkey())
            dma("sp", lr[:, :, 0:n], r_d[:, :, t0:t0 + n], [("r_d", ti, hd) for hd in range(4)], L1K, ldkey())
            for ch in range(4):
                for k in range(31):
                    idx = ch * 31 + k
                    mm(psA[ch][:, 0:n], dgf[:, idx * 128:(idx + 1) * 128], lu[:, ch, k:k + n], k == 0, k == 30, HK + lk0, [PS(ch)])
            for ch in range(4):
                act(tb16[ch % 2][:, 0:n], psA[ch][:, 0:n], AF.Copy, [PS(ch)], [("tb16", ch % 2)])
                mm(psA[4][:, 0:n], ones, tb16[ch % 2][:, 0:n], ch == 0, ch == 3, [("tb16", ch % 2), "cstb"], [PS(4)])
            for ch in range(4):
                act(tb16[ch % 2][:, 0:n], psA[ch][:, 0:n], AF.Square, [PS(ch)], [("tb16", ch % 2)])
                mm(psA[5][:, 0:n], ones, tb16[ch % 2][:, 0:n], ch == 0, ch == 3, [("tb16", ch % 2), "cstb"], [PS(5)])
            act(t32[0][:, 0:n], psA[4][:, 0:n], AF.Copy, [PS(4)], [("t32", 0)], scale=1.0 / 512)
            tt("dve", t32[1][:, 0:n], t32[0][:, 0:n], t32[0][:, 0:n], ALU.mult, [("t32", 0)], [("t32", 1)])
            stt("dve", t32[1][:, 0:n], psA[5][:, 0:n], 1.0 / 512, t32[1][:, 0:n], ALU.mult, ALU.subtract, [PS(5), ("t32", 1)], [("t32", 1)])
            act(rs_sb[:, 0:n], t32[1][:, 0:n], AF.Ln, [("t32", 1), "cst"], ["rs"], bias=epsc)
            act(rs_sb[:, 0:n], rs_sb[:, 0:n], AF.Exp, ["rs"], ["rs"], scale=-0.5)
            for ch in range(4):
                tt("dve", acc32[:, ch, 0:n], psA[ch][:, 0:n], t32[0][:, 0:n], ALU.subtract, [PS(ch), ("t32", 0)], [("acc", ch)])
                tt("dve", acc32[:, ch, 0:n], acc32[:, ch, 0:n], rs_sb[:, 0:n], ALU.mult, [("acc", ch), "rs"], [("acc", ch)])
                act(mix_sb[:, 4 + ch, 0:n], acc32[:, ch, 0:n], AF.Silu, [("acc", ch), "cst"], [("mixs", 4 + ch)],
                    bias=C("convlnb", j * 4 + ch), scale=C("convlnw", j * 4 + ch))
            wout_tile(ti, wo, s4, [lr[:, kc, :] for kc in range(4)] + [mix_sb[:, 4 + kc, :] for kc in range(4)],
                      [("lv", 0)] * 4 + [("mixs", 4 + kc) for kc in range(4)])

    def wout_tile(ti, wo, s, rhs_list, rkeys):
        t0, n = TILES[ti]
        s_ = tile_stream(ti)
        for oc in range(KC):
            bank = 4 + oc % 3
            for kc in range(KC):
                mm(psA[bank][:, 0:n], wo[:, kc, oc * 128:(oc + 1) * 128], rhs_list[kc][:, 0:n], kc == 0, kc == KC - 1,
                   [("W", s), rkeys[kc]], [PS(bank)])
            stt("dve", x_sb[:, oc, t0:t0 + n], psA[bank][:, 0:n], nv_sb[:, 2, oc, s_:s_ + 1], x_sb[:, oc, t0:t0 + n], ALU.mult, ALU.add,
                [PS(bank), "nv"], [("x", oc, ti)])

    def pool_window(ti, lp, lkeys):
        t0, n = TILES[ti]
        W_ = n + 2 * HP
        icn = "icc" if ti == CTX_TILE else "icm"
        for g, w in enumerate((2, 4, 8, 16)):
            left = w // 2
            right = w - 1 - left
            cur, oth = 0, 1
            tt("dve", t32[cur][:, 1:W_], lp[:, g, 1:W_], lp[:, g, 0:W_ - 1], ALU.add, lkeys, [("t32", cur)])
            k = 2
            lo = 1
            while k < w:
                tt("dve", t32[oth][:, lo + k:W_], t32[cur][:, lo + k:W_], t32[cur][:, lo:W_ - k], ALU.add, [("t32", cur)], [("t32", oth)])
                lo += k
                k *= 2
                cur, oth = oth, cur
            e0 = HP + right
            ts("dve", acc32[:, g, 0:n], t32[cur][:, e0:e0 + n], 1.0 / w, None, ALU.mult, None, [("t32", cur)], [("acc", g)])
            if ti == 0 or ti == CTX_TILE:
                tt("dve", acc32[:, g, 0:8], t32[cur][:, e0:e0 + 8], C(icn, g * 16, 8), ALU.mult, [("t32", cur), "cst"], [("acc", g)])
            if ti == 3 or ti == CTX_TILE:
                tt("dve", acc32[:, g, n - 8:n], t32[cur][:, e0 + n - 8:e0 + n], C(icn, g * 16 + 8, 8), ALU.mult, [("t32", cur), "cst"], [("acc", g)])
            tt("dve", mix_sb[:, g, 0:n], acc32[:, g, 0:n], lp[:, g, HP:HP + n], ALU.subtract, [("acc", g)] + lkeys, [("mixs", g)])

    def odd_mixer(li, tiles):
        j = li // 2
        norm_tile(tiles[0], 0)
        dma("pool", pwsg, tab_d[:, LAYT.o("pwsg") + j * 1024:LAYT.o("pwsg") + (j + 1) * 1024], [], ["S"], "tbl")
        acck = [("acc", i) for i in range(4)]
        dma("sp", accf[:, 0:1536], tab_d[:, LAYT.o("sgtab") + j * 1536:LAYT.o("sgtab") + (j + 1) * 1536], [], acck, ldkey())
        lnw = accf[:, 0:512]
        lnb = accf[:, 512:1024]
        s, (wpc,) = wload([odin_d[j, :, :, 0:512]])
        for i_, ti in enumerate(tiles):
            if i_ + 1 < len(tiles):
                norm_tile(tiles[i_ + 1], 0)
            t0, n = TILES[ti]
            pb = o16[ti % 2]
            for g in range(4):
                bank = g % 4
                proj_fm(bank, wpc, s, g * 128, ti)
                act(pb[:, g, 0:n], psA[bank][:, 0:n], AF.Copy, [PS(bank)], [("o16", ti % 2)])
            if ti == CTX_TILE:
                dma("sp", pc_d[:, :, HP:HP + n], pb[:, :, 0:n], [("o16", ti % 2)], [("p_d", ti)], stkey())
            else:
                dma("sp", pm_d[:, :, HP + t0:HP + t0 + n], pb[:, :, 0:n], [("o16", ti % 2)], [("p_d", ti)], stkey())
            if ti == 0:
                cp("dve", xh_sb[:, 0:4 * HP].rearrange("p (a b) -> p a b", a=4), pb[:, :, 0:HP], [("o16", ti % 2)], ["xh"])
            if ti == 3:
                cp("dve", xh_sb[:, 4 * HP:8 * HP].rearrange("p (a b) -> p a b", a=4), pb[:, :, n - HP:n], [("o16", ti % 2)], ["xh"])
        s, (wu_, wv_) = wload([odin_d[j, :, :, 512:1024], odin_d[j, :, :, 1024:1536]])
        dma("pool", xio_d.ap(), xh_sb[:, 0:XWO], ["xh"], ["xio_d"], "xst0")
        allgather(xio_d, xoo_d, ["xio_d"], "xoo_d")
        dma("pool", rs_sb[:, 0:4 * HP], xoo_d.ap()[0:128, 4 * HP:8 * HP], ["xoo_d"], ["rs"], "xld0")
        dma("pool", rs_sb[:, 4 * HP:8 * HP], xoo_d.ap()[128:256, 0:4 * HP], ["xoo_d"], ["rs"], "xld1")
        hl = tb16[2][:, 0:8 * HP].rearrange("p (a b) -> p a b", a=4)
        ts("dve", hl[:, :, 0:HP], rs_sb[:, 0:4 * HP].rearrange("p (a b) -> p a b", a=4), fB, None, ALU.mult, None, ["rs", "cst"], [("tb16", 2)])
        ts("dve", hl[:, :, HP:2 * HP], rs_sb[:, 4 * HP:8 * HP].rearrange("p (a b) -> p a b", a=4), fA, None, ALU.mult, None, ["rs", "cst"], [("tb16", 2)])
        dma("sp", pm_d[:, :, 0:HP], hl[:, :, 0:HP], [("tb16", 2)], [("p_d", 0)], stkey())
        dma("sp", pm_d[:, :, HP + TM:], hl[:, :, HP:2 * HP], [("tb16", 2)], [("p_d", 3)], stkey())
        for ti in tiles:
            t0, n = TILES[ti]
            nb = n // 128
            ub = o16[0]
            sb_ = o16[1]
            for g in range(4):
                bank = g % 2
                proj_fm(bank, wu_, s, g * 128, ti)
                act(ub[:, g, 0:n], psA[bank][:, 0:n], AF.Gelu, [PS(bank)], [("o16", 0)])
            def S1(jb, ti=ti, t0=t0):
                bank = 2 + jb % 2
                for kc in range(KC):
                    mm(psA[bank][:, :], h_sb[:, kc, t0 + jb * 128:t0 + (jb + 1) * 128], wv_[:, kc, :], kc == 0, kc == KC - 1,
                       [("W", s), ("h", ti)], [PS(bank)])
                z = t32[jb % 2]
                zk = ("t32", jb % 2)
                pb_ = jb % 2
                sq = l16x[:, pb_, :]
                sqk = ("sqx", pb_)
                smo = 4 * pb_
                smk = ("sm", pb_)
                act(z[:, 0:512], psA[bank][:, :], AF.Gelu, [PS(bank)], [zk])
                act(sq, z[:, 0:512], AF.Square, [zk], [sqk])
            def S2(jb, ti=ti, t0=t0):
                z = t32[jb % 2]
                zk = ("t32", jb % 2)
                pb_ = jb % 2
                sq = l16x[:, pb_, :]
                sqk = ("sqx", pb_)
                smo = 4 * pb_
                smk = ("sm", pb_)
                pg.add("dve", lambda h, z=z, smo=smo: h.reduce_sum(out=sm[:, smo:smo + 1], in_=z[:, 0:512], axis=mybir.AxisListType.X), [zk], [smk], size=1)
                pg.add("dve", lambda h, sq=sq, smo=smo: h.reduce_sum(out=sm[:, smo + 1:smo + 2], in_=sq, axis=mybir.AxisListType.X), [sqk], [smk], size=1)
                ts("dve", sm[:, smo + 2:smo + 3], sm[:, smo:smo + 1], 1.0 / 512, None, ALU.mult, None, [smk], [smk])
                tt("dve", sm[:, smo + 3:smo + 4], sm[:, smo + 2:smo + 3], sm[:, smo + 2:smo + 3], ALU.mult, [smk], [smk])
                stt("dve", sm[:, smo + 3:smo + 4], sm[:, smo + 1:smo + 2], 1.0 / 512, sm[:, smo + 3:smo + 4], ALU.mult, ALU.subtract, [smk], [smk])
                act(sm[:, smo + 3:smo + 4], sm[:, smo + 3:smo + 4], AF.Sqrt, [smk, "cst"], [smk], bias=epsc)
                recip(sm[:, smo + 3:smo + 4], smk)
                ts("dve", z[:, 0:512], z[:, 0:512], sm[:, smo + 2:smo + 3], sm[:, smo + 3:smo + 4], ALU.subtract, ALU.mult, [smk, zk], [zk])
                tt("dve", z[:, 0:512], z[:, 0:512], lnw, ALU.mult, [zk] + acck, [zk])
                vb = tb16[jb % 2]
                tt("dve", vb[:, :], z[:, 0:512], lnb, ALU.add, [zk] + acck, [("tb16", jb % 2)])
                bank2 = 4 + jb % 2
                for g in range(4):
                    mm(psA[bank2][:, g * 128:(g + 1) * 128], vb[:, g * 128:(g + 1) * 128], pwsg[:, 512 + g * 128:512 + (g + 1) * 128], True, True,
                       [("tb16", jb % 2), "S"], [PS(bank2)])
                tt("dve", t32[2][:, 0:512], psA[bank2][:, 0:512], accf[:, 1024:1536], ALU.add, [PS(bank2)] + acck, [("t32", 2)])
                tt("dve", sb_[:, :, jb * 128:(jb + 1) * 128], t32[2][:, 0:512].rearrange("p (g q) -> p g q", g=4),
                   ub[:, :, jb * 128:(jb + 1) * 128], ALU.mult, [("t32", 2), ("o16", 0)], [("o16", 1)])
            S1(0)
            for jb in range(nb):
                if jb + 1 < nb:
                    S1(jb + 1)
                S2(jb)
            dma("sp", r_d[:, :, t0:t0 + n], sb_[:, :, 0:n], [("o16", 1)], [("r_d", ti)], stkey())
        s, (wo,) = wload([odout_d[j, :, :, :]])
        for ti in tiles:
            t0, n = TILES[ti]
            lp, lr = l16[0], l16[1]
            lk0 = [("lq", 0), ("lk", 0), ("lg_", 0), ("rb", 0)]
            if ti == CTX_TILE:
                dma("sp", lp[:, :, 0:n + 2 * HP], pc_d[:, :, :], [("p_d", ti), "pc_h0", "pc_h1"], lk0, ldkey())
            else:
                rk = [("p_d", ti)] + ([("p_d", ti - 1)] if ti > 0 else []) + ([("p_d", ti + 1)] if ti < 3 else [])
                dma("sp", lp[:, :, 0:n + 2 * HP], pm_d[:, :, t0:t0 + n + 2 * HP], rk, lk0, ldkey())
            dma("sp", lr[:, :, 0:n], r_d[:, :, t0:t0 + n], [("r_d", ti)], [("lv", 0), ("lv", 1)], ldkey())
            pool_window(ti, lp, lk0)
            for g in range(4):
                bank = g % 2
                mm(psA[bank][:, 0:n], pwsg[:, g * 128:(g + 1) * 128], mix_sb[:, g, 0:n], True, True, [("mixs", g), "S"], [PS(bank)])
                ts("dve", mix_sb[:, 4 + g, 0:n], psA[bank][:, 0:n], C("poolsc", j * 4 + g), None, ALU.mult, None,
                   [PS(bank), "cst"], [("mixs", 4 + g)])
            if ti == 0:
                dbg_dump("mix", mix_sb[:, :, :], [("mixs", i) for i in range(8)])
                dbg_dump("lp", lp[:, :, :], lk0)
                dbg_dump("lr", lr[:, :, :], [("l16", 1)])
            wout_tile(ti, wo, s, [mix_sb[:, 4 + kc, :] for kc in range(4)] + [lr[:, kc, :] for kc in range(4)],
                      [("mixs", 4 + kc) for kc in range(4)] + [("lv", 0)] * 4)

    def ffn(li, tiles):
        norm_tile(tiles[0], 1)
        for f0 in range(0, NFF, 2):
            s, (wgv, wuv, wdv) = wload([wg_d[li, :, :, f0 * 128:(f0 + 2) * 128], wu_d[li, :, :, f0 * 128:(f0 + 2) * 128],
                                        wd_d[li, :, f0:f0 + 2, :]])
            for i_, ti in enumerate(tiles):
                if f0 == 0 and i_ + 1 < len(tiles):
                    norm_tile(tiles[i_ + 1], 1)
                t0, n = TILES[ti]
                s_ = tile_stream(ti)
                hsel = (f0 // 2 + ti) % 2
                for f in range(2):
                    bg, bu = 2 * f, 2 * f + 1
                    col = f * 128
                    for kc in range(KC):
                        mm(psA[bg][:, 0:n], wgv[:, kc, col:col + 128], h_sb[:, kc, t0:t0 + n], kc == 0, kc == KC - 1, [("W", s), ("h", ti)], [PS(bg)])
                    for kc in range(KC):
                        mm(psA[bu][:, 0:n], wuv[:, kc, col:col + 128], h_sb[:, kc, t0:t0 + n], kc == 0, kc == KC - 1, [("W", s), ("h", ti)], [PS(bu)])
                    act(t32[f][:, 0:n], psA[bg][:, 0:n], AF.Silu, [PS(bg)], [("t32", f)])
                    tt("dve", hid_sb[:, 2 * hsel + f, 0:n], psA[bu][:, 0:n], t32[f][:, 0:n], ALU.mult, [PS(bu), ("t32", f)], [("mixs", 4 + 2 * hsel + f)])
                for oc in range(KC):
                    bank = 4 + oc % 3
                    for fc in range(2):
                        mm(psA[bank][:, 0:n], wdv[:, fc, oc * 128:(oc + 1) * 128], hid_sb[:, 2 * hsel + fc, 0:n], fc == 0, fc == 1,
                           [("W", s), ("mixs", 4 + 2 * hsel + fc)], [PS(bank)])
                    stt("dve", x_sb[:, oc, t0:t0 + n], psA[bank][:, 0:n], nv_sb[:, 5, oc, s_:s_ + 1], x_sb[:, oc, t0:t0 + n], ALU.mult, ALU.add,
                        [PS(bank), "nv"], [("x", oc, ti)])

    mod_prologue()
    for li in range(n_layers):
        mod_layer(li)
        ctx_after = any(m % 2 == 0 for m in range(li + 1, 4))
        tl = [0, 1, 2, 3] + ([CTX_TILE] if ctx_after else [])
        if li % 2 == 0:
            even_mixer(li, ctx_after)
        else:
            odd_mixer(li, tl)
        ffn(li, tl)

    okeys = []
    for ti in range(4):
        t0, n = TILES[ti]
        if debug_x:
            for c in range(KC):
                dma("sp", out_d[:, c, t0:t0 + n], x_sb[:, c, t0:t0 + n], [("x", c, ti)], [("out", ti, c)], "out%d" % (c % 4))
        else:
            final_tile(ti)
        okeys += [("out", ti, c) for c in range(KC)]
    pg.add("sp", lambda h: h.nop(), okeys + dbg_list, [])

    pg.finalize()
    sems = {}
    for e in ENGS:
        sems[("eng", e)] = E(nc.semaphore("s_" + e))
    for i, k in enumerate(pg.dma_keys):
        sems[("dma", k)] = E(nc.semaphore("d%d" % i))
    block = E(nc.Block())

    @block.tensor
    def _(h):
        pg.emit(sems, "pe", h)

    @block.scalar
    def _(h):
        pg.emit(sems, "act", h)

    @block.vector
    def _(h):
        pg.emit(sems, "dve", h)

    @block.gpsimd
    def _(h):
        pg.emit(sems, "pool", h)

    @block.sync
    def _(h):
        pg.emit(sems, "sp", h)

    es.close()
    return nc


_NC_CACHE = {}


def prep_inputs(inp):
    x = np.asarray(inp["x"], np.float32)
    ctx = np.asarray(inp["ctx"], np.float32)
    shared = {
        "even_w_in": np.stack([wlay(inp["even_w_in"][l]) for l in range(2)]),
        "even_w_out": np.stack([wlay(inp["even_w_out"][l]) for l in range(2)]),
        "odd_w_in": np.stack([wlay(inp["odd_w_in"][l]) for l in range(2)]),
        "odd_w_out": np.stack([wlay(inp["odd_w_out"][l]) for l in range(2)]),
        "ffn_w_gate": np.stack([wlay(inp["ffn_w_gate"][l]) for l in range(4)]),
        "ffn_w_up": np.stack([wlay(inp["ffn_w_up"][l]) for l in range(4)]),
        "ffn_w_down": np.stack([wlay(inp["ffn_w_down"][l]) for l in range(4)]),
    }
    in_maps = []
    for core in range(NCORES):
        b, half = core // 2, core % 2
        xm = x[b, half * TM:(half + 1) * TM, :]
        xa = np.concatenate([xm, ctx[b]], axis=0)
        xt = np.ascontiguousarray(xa.T.reshape(KC, 128, TT).transpose(1, 0, 2))
        cst, tab, cb = host_consts(core, inp)
        m = {"xin": xt, "cst": cst, "tab": tab, "cstb": cb,
             "ada_w": np.stack([wlay(np.asarray(inp["ada_w"][l], np.float32)[:, half * 3072:(half + 1) * 3072]) for l in range(4)])}
        m.update(shared)
        in_maps.append(m)
    return in_maps


def assemble(results):
    out = np.zeros((4, SEQ, D), np.float32)
    for core in range(NCORES):
        b, half = core // 2, core % 2
        o = np.asarray(results[core]["out"], np.float32)
        out[b, half * TM:(half + 1) * TM, :] = o.transpose(1, 0, 2).reshape(D, TM).T
    return out


def kernel(**inputs):
    key = "full"
    if key not in _NC_CACHE:
        _NC_CACHE[key] = build_program()
    nc = _NC_CACHE[key]
    in_maps = prep_inputs(inputs)
    res = run_bass_kernel_spmd(nc, in_maps, core_ids=list(range(NCORES)))
    return assemble(res.results)
```
